# Optimizing a Trainium2 kernel written in Bass

```python
import math
import jax, jax.numpy as jnp
from jax import lax
import numpy as np

D_MODEL = 2048
BATCH = 8
SEQ = 2048
DEPTH = 1

GLA_HEADS = 4
GLA_DK = D_MODEL // 8
GLA_DV = D_MODEL // 4
GLA_KEY = GLA_HEADS * GLA_DK
GLA_VAL = GLA_HEADS * GLA_DV
GLA_GATE_RANK = 16
GLA_TAU = 16.0
GLA_CHUNK = 64
HY_WIDTH = D_MODEL
HY_ORDER = 2
HY_SHORT = 3
HY_EMB = 33
HY_FILTER_HIDDEN = 64
HY_FAST_DECAY = 0.3
HY_SLOW_DECAY = 1.5
HY_DECAY_TARGET = 1e-2
HY_FILTER_OUT = HY_ORDER * 2 * HY_WIDTH
N_BRANCH = 2
D_FF = 4 * D_MODEL
LN_EPS = 1e-5
DEEPNORM_ALPHA = (2 * DEPTH) ** 0.25
DEEPNORM_BETA = (8 * DEPTH) ** -0.25
IN_SPLIT_SIZES = (GLA_KEY, GLA_KEY, GLA_VAL, GLA_VAL, GLA_GATE_RANK, GLA_GATE_RANK, 3 * HY_WIDTH, N_BRANCH * D_MODEL)
IN_WIDTH = sum(IN_SPLIT_SIZES)

kernel_name = 'hybrid_gla_hyena_deepnorm_block'


def _layernorm(x, g, b):
    xf = x.astype(jnp.float32)
    mu = jnp.mean(xf, axis=-1, keepdims=True)
    var = jnp.mean(jnp.square(xf - mu), axis=-1, keepdims=True)
    return ((xf - mu) * lax.rsqrt(var + LN_EPS) * g + b).astype(x.dtype)


def _gla_one_direction(q, k, v, log_a):
    B, H, S, DK = q.shape
    DV = v.shape[-1]
    C = GLA_CHUNK
    n = S // C
    q = q.reshape(B, H, n, C, DK)
    k = k.reshape(B, H, n, C, DK)
    v = v.reshape(B, H, n, C, DV)
    b = jnp.cumsum(log_a.reshape(B, H, n, C, DK), axis=3)
    b_last = b[:, :, :, -1, :]
    q_in = q * jnp.exp(b)
    k_in = k * jnp.exp(-b)
    k_end = k * jnp.exp(b_last[:, :, :, None, :] - b)
    lower_tri = jnp.tril(jnp.ones((C, C), dtype=bool))
    scores = jnp.where(lower_tri, jnp.einsum('bhnid,bhnjd->bhnij', q_in, k_in), 0.0)
    o_intra = jnp.einsum('bhnij,bhnje->bhnie', scores, v)

    def step(state, inp):
        q_c, k_c, v_c, bl_c = inp
        o_c = jnp.einsum('bhid,bhde->bhie', q_c, state)
        state = jnp.exp(bl_c)[..., None] * state + jnp.einsum('bhjd,bhje->bhde', k_c, v_c)
        return state, o_c

    xs = (jnp.moveaxis(q_in, 2, 0), jnp.moveaxis(k_end, 2, 0), jnp.moveaxis(v, 2, 0), jnp.moveaxis(b_last, 2, 0))
    _, o_inter = lax.scan(step, jnp.zeros((B, H, DK, DV), jnp.float32), xs)
    o = o_intra + jnp.moveaxis(o_inter, 0, 2)
    return o.reshape(B, H, S, DV)


def _gla_branch(q, k, v, r, af, ab, wa2_f, ba_f, wa2_b, ba_b, norm_g):
    B, L, _ = q.shape
    f32 = jnp.float32

    def heads(t, d):
        return t.astype(f32).reshape(B, L, GLA_HEADS, d).transpose(0, 2, 1, 3)

    qh = heads(q, GLA_DK) * (GLA_DK ** -0.5)
    kh = heads(k, GLA_DK)
    vh = heads(v, GLA_DV)
    la_f = heads(jax.nn.log_sigmoid((af @ wa2_f + ba_f).astype(f32)) / GLA_TAU, GLA_DK)
    la_b = heads(jax.nn.log_sigmoid((ab @ wa2_b + ba_b).astype(f32)) / GLA_TAU, GLA_DK)
    flip = lambda t: jnp.flip(t, axis=2)
    o_f = _gla_one_direction(qh, kh, vh, la_f)
    o_b = flip(_gla_one_direction(flip(qh), flip(kh), flip(vh), flip(la_b)))
    o_b = o_b - jnp.sum(qh * kh, axis=-1, keepdims=True) * vh
    o = (o_f + o_b).transpose(0, 2, 1, 3)
    o = o * lax.rsqrt(jnp.mean(jnp.square(o), axis=-1, keepdims=True) + LN_EPS)
    o = o * norm_g.astype(f32).reshape(GLA_HEADS, GLA_DV)
    o = o * jax.nn.silu(r.astype(f32)).reshape(B, L, GLA_HEADS, GLA_DV)
    return o.reshape(B, L, GLA_VAL).astype(q.dtype)


def _short_conv(u, w, b):
    C = u.shape[-1]
    pad = HY_SHORT // 2
    y = lax.conv_general_dilated(u, w[:, None, :].astype(u.dtype), (1,), [(pad, pad)],
                                 dimension_numbers=('NWC', 'WIO', 'NWC'), feature_group_count=C)
    return y + b


def _hyena_filters(L, w1, b1, w2, b2, w3, b3, w4, b4, freq):
    f32 = jnp.float32
    t = jnp.linspace(0.0, 1.0, L, dtype=f32)[:, None]
    bands = (HY_EMB - 1) // 2
    f = jnp.linspace(1e-4, bands - 1, bands, dtype=f32)
    wpos = 2.0 * math.pi * jnp.arange(L, dtype=f32) / L
    ang = wpos[:, None] * f[None, :]
    emb = jnp.concatenate([t, jnp.cos(ang), -jnp.sin(ang)], axis=-1)
    fr = freq.astype(f32)
    h = jnp.sin(fr * (emb @ w1.astype(f32) + b1.astype(f32)))
    h = jnp.sin(fr * (h @ w2.astype(f32) + b2.astype(f32)))
    h = jnp.sin(fr * (h @ w3.astype(f32) + b3.astype(f32)))
    h = h @ w4.astype(f32) + b4.astype(f32)
    min_decay = math.log(HY_DECAY_TARGET) / HY_SLOW_DECAY
    max_decay = math.log(HY_DECAY_TARGET) / HY_FAST_DECAY
    deltas = jnp.abs(jnp.linspace(min_decay, max_decay, HY_WIDTH, dtype=f32))
    decay = jnp.exp(-t * deltas[None, :])
    return h.reshape(L, HY_ORDER, 2, HY_WIDTH) * decay[:, None, None, :]


def _hyena_branch(u, conv_w, conv_b, w1, b1, w2, b2, w3, b3, w4, b4, freq, skip):
    B, L, _ = u.shape
    out_dtype = u.dtype
    f32 = jnp.float32
    uc = _short_conv(u, conv_w, conv_b).astype(f32)
    v, x1, x2 = jnp.split(uc, 3, axis=-1)
    h = _hyena_filters(L, w1, b1, w2, b2, w3, b3, w4, b4, freq)
    filt = jnp.concatenate([h[:, :, 0], jnp.zeros((1, HY_ORDER, HY_WIDTH), f32), jnp.flip(h[1:, :, 1], axis=0)], axis=0)
    filt_f = jnp.fft.rfft(filt, axis=0)
    z = v
    for o, gate in enumerate((x1, x2)):
        zf = jnp.fft.rfft(z, n=2 * L, axis=1)
        zc = jnp.fft.irfft(zf * filt_f[None, :, o], n=2 * L, axis=1)[:, :L]
        z = gate * (zc + skip[o].astype(f32) * z)
    return z.astype(out_dtype)


def setup_inputs(seed: int = 0) -> dict:
    key = jax.random.key(seed)
    ks = iter(jax.random.split(key, 40))
    D = D_MODEL

    def nrm(shape, scale):
        return jax.random.normal(next(ks), shape, jnp.float32) * scale

    x = nrm((BATCH, SEQ, D), 1.0)
    s_in = D ** -0.5
    w_in = jnp.concatenate([
        nrm((DEPTH, D, GLA_KEY), s_in),
        nrm((DEPTH, D, GLA_KEY), s_in),
        nrm((DEPTH, D, GLA_VAL), s_in * DEEPNORM_BETA),
        nrm((DEPTH, D, GLA_VAL), s_in),
        nrm((DEPTH, D, GLA_GATE_RANK), s_in),
        nrm((DEPTH, D, GLA_GATE_RANK), s_in),
        nrm((DEPTH, D, HY_WIDTH), s_in * DEEPNORM_BETA),
        nrm((DEPTH, D, 2 * HY_WIDTH), s_in),
        nrm((DEPTH, D, N_BRANCH * D), s_in),
    ], axis=-1)
    r_s = GLA_GATE_RANK ** -0.5
    return {
        'x': x,
        'w_in': w_in,
        'gla_wa2_f': nrm((DEPTH, GLA_GATE_RANK, GLA_KEY), r_s),
        'gla_ba_f': nrm((DEPTH, GLA_KEY), 0.1),
        'gla_wa2_b': nrm((DEPTH, GLA_GATE_RANK, GLA_KEY), r_s),
        'gla_ba_b': nrm((DEPTH, GLA_KEY), 0.1),
        'gla_norm_g': 1.0 + nrm((DEPTH, GLA_VAL), 0.02),
        'w_gla_o': nrm((DEPTH, GLA_VAL, D), GLA_VAL ** -0.5 * DEEPNORM_BETA),
        'hy_conv_w': nrm((DEPTH, HY_SHORT, 3 * HY_WIDTH), HY_SHORT ** -0.5),
        'hy_conv_b': nrm((DEPTH, 3 * HY_WIDTH), 0.02),
        'hy_w1': nrm((DEPTH, HY_EMB, HY_FILTER_HIDDEN), HY_EMB ** -0.5),
        'hy_b1': nrm((DEPTH, HY_FILTER_HIDDEN), 0.02),
        'hy_w2': nrm((DEPTH, HY_FILTER_HIDDEN, HY_FILTER_HIDDEN), HY_FILTER_HIDDEN ** -0.5),
        'hy_b2': nrm((DEPTH, HY_FILTER_HIDDEN), 0.02),
        'hy_w3': nrm((DEPTH, HY_FILTER_HIDDEN, HY_FILTER_HIDDEN), HY_FILTER_HIDDEN ** -0.5),
        'hy_b3': nrm((DEPTH, HY_FILTER_HIDDEN), 0.02),
        'hy_w4': nrm((DEPTH, HY_FILTER_HIDDEN, HY_FILTER_OUT), 0.02),
        'hy_b4': nrm((DEPTH, HY_FILTER_OUT), 0.002),
        'hy_freq': 1.0 + nrm((DEPTH, HY_FILTER_HIDDEN), 0.01),
        'hy_skip': nrm((DEPTH, HY_ORDER, HY_WIDTH), 1.0),
        'w_hy_o': nrm((DEPTH, HY_WIDTH, D), HY_WIDTH ** -0.5 * DEEPNORM_BETA),
        'w_out': nrm((DEPTH, D, D), D ** -0.5 * DEEPNORM_BETA),
        'ln1_g': 1.0 + nrm((DEPTH, D), 0.02),
        'ln1_b': nrm((DEPTH, D), 0.02),
        'w_ff1': nrm((DEPTH, D, D_FF), D ** -0.5 * DEEPNORM_BETA),
        'w_ff2': nrm((DEPTH, D_FF, D), D_FF ** -0.5 * DEEPNORM_BETA),
        'ln2_g': 1.0 + nrm((DEPTH, D), 0.02),
        'ln2_b': nrm((DEPTH, D), 0.02),
    }


def reference(x, w_in, gla_wa2_f, gla_ba_f, gla_wa2_b, gla_ba_b, gla_norm_g, w_gla_o,
              hy_conv_w, hy_conv_b, hy_w1, hy_b1, hy_w2, hy_b2, hy_w3, hy_b3, hy_w4, hy_b4,
              hy_freq, hy_skip, w_hy_o, w_out, ln1_g, ln1_b, w_ff1, w_ff2, ln2_g, ln2_b):
    B, L, D = x.shape
    split_idx = [int(i) for i in np.cumsum(IN_SPLIT_SIZES)[:-1]]
    h = x
    for l in range(DEPTH):
        proj = h @ w_in[l]
        q, k, v, r, af, ab, hy_u, gate_logits = jnp.split(proj, split_idx, axis=-1)
        y_gla = _gla_branch(q, k, v, r, af, ab, gla_wa2_f[l], gla_ba_f[l], gla_wa2_b[l], gla_ba_b[l], gla_norm_g[l])
        y_hy = _hyena_branch(hy_u, hy_conv_w[l], hy_conv_b[l], hy_w1[l], hy_b1[l], hy_w2[l], hy_b2[l],
                             hy_w3[l], hy_b3[l], hy_w4[l], hy_b4[l], hy_freq[l], hy_skip[l])
        g = jax.nn.sigmoid(gate_logits.astype(jnp.float32)).astype(h.dtype).reshape(B, L, N_BRANCH, D)
        merged = g[:, :, 0, :] * (y_gla @ w_gla_o[l]) + g[:, :, 1, :] * (y_hy @ w_hy_o[l])
        mix = merged @ w_out[l]
        h = _layernorm(DEEPNORM_ALPHA * h + mix, ln1_g[l], ln1_b[l])
        ff = jnp.square(jax.nn.relu(h @ w_ff1[l])) @ w_ff2[l]
        h = _layernorm(DEEPNORM_ALPHA * h + ff, ln2_g[l], ln2_b[l])
    return h
```

```python
import math
from contextlib import ExitStack

import numpy as np
import ml_dtypes

import concourse.bass as bass
import concourse.mybir as mybir
from concourse.bass_utils import run_bass_kernel_spmd

F32 = mybir.dt.float32
BF16 = mybir.dt.bfloat16
AF = mybir.ActivationFunctionType
ALU = mybir.AluOpType
AX = mybir.AxisListType

D = 2048
NCORES = 8
GLA_H = 4
DK = 256
DV = 512
KEYW = 1024
RANK = 16
TAU = 16.0
HYW = 2048
HID = 64
EMB = 33
DFF = 8192
LN_EPS = 1e-5
ALPHA = 2.0 ** 0.25
CH = 128


class Tok:
    __slots__ = ("w", "r", "name", "dsem", "dcnt", "did")

    def __init__(self, name=""):
        self.w = {}
        self.r = {}
        self.name = name
        self.dsem = None
        self.dcnt = 0
        self.did = None


class Sched:
    def __init__(self, nc, es, n_dma_sems=96):
        self.nc = nc
        self.eng = dict(pe=nc.tensor, act=nc.scalar, dve=nc.vector, pool=nc.gpsimd, sp=nc.sync)
        self.semobj = {}
        self.cnt = {}
        for k in ("pe", "act", "dve", "pool"):
            self.semobj[k] = es.enter_context(nc.semaphore("sem_" + k))
            self.cnt[k] = 0
        self.seen = {k: {} for k in self.eng}
        self.free_dsems = []
        for i in range(n_dma_sems):
            sem = es.enter_context(nc.semaphore(f"dsem{i}"))
            self.semobj[f"d{i}"] = sem
            self.free_dsems.append((sem, f"d{i}", 0))
        self.active = []
        self.pe_pending = False
        self.n_inst = {k: 0 for k in self.eng}
        self.n_wait = 0

    def _waits(self, e, reads, writes):
        need = {}
        for t in reads:
            for k, c in t.w.items():
                if need.get(k, 0) < c:
                    need[k] = c
        for t in writes:
            for k, c in t.w.items():
                if k == e:
                    continue
                if need.get(k, 0) < c:
                    need[k] = c
            for k, c in t.r.items():
                if k == e:
                    continue
                if need.get(k, 0) < c:
                    need[k] = c
        seen = self.seen[e]
        for k, c in need.items():
            if e == "pe" and k == "pe":
                continue
            if seen.get(k, 0) >= c:
                continue
            self.eng[e].wait_ge(self.semobj[k], c)
            self.n_wait += 1
            seen[k] = c

    def op(self, e, fn, reads=(), writes=(), sig=True):
        self._waits(e, reads, writes)
        ins = fn(self.eng[e])
        self.n_inst[e] += 1
        if sig:
            self.cnt[e] += 1
            ins.then_inc(self.semobj[e], 1)
            c = self.cnt[e]
            if e == "pe":
                self.pe_pending = False
        else:
            assert e == "pe"
            c = self.cnt[e] + 1
            self.pe_pending = True
        for t in reads:
            if t.r.get(e, 0) < c:
                t.r[e] = c
        for t in writes:
            t.w = {e: c}
            t.r = {}
        return ins

    def dma(self, q, out, in_, sb, reads=(), writes=()):
        self._waits(q, reads, writes)
        if sb.dsem is None:
            sb.dsem, sb.did, sb.dcnt = self.free_dsems.pop()
            self.active.append(sb)
        ins = self.eng[q].dma_start(out=out, in_=in_)
        self.n_inst[q] += 1
        sb.dcnt += 16
        ins.then_inc(sb.dsem, 16)
        k, c = sb.did, sb.dcnt
        for t in reads:
            if t.r.get(k, 0) < c:
                t.r[k] = c
        for t in writes:
            t.w = {k: c}
            t.r = {}
        return ins

    def barrier(self):
        assert not self.pe_pending, "PE has unsignaled instructions at barrier"
        targets = {k: self.cnt[k] for k in ("pe", "act", "dve", "pool")}
        for t in self.active:
            targets[t.did] = t.dcnt
        for e in self.eng:
            seen = self.seen[e]
            for k, c in targets.items():
                if c > seen.get(k, 0):
                    self.eng[e].wait_ge(self.semobj[k], c)
                    self.n_wait += 1
                    seen[k] = c
        for t in self.active:
            self.free_dsems.append((t.dsem, t.did, t.dcnt))
            t.dsem = None
        self.active = []


def T(name=""):
    return Tok(name)


def pk(dram, r0, kt, c0, w):
    return dram[r0:r0 + kt * 128, c0:c0 + w].rearrange("(k p) w -> p k w", p=128)


class Prog:
    def __init__(self, L=2048, debug=False, upto=99):
        self.L = L
        self.NT = L // 128
        self.NB = L // 512
        self.debug = debug
        self.upto = upto
        self.nc = bass.Bass("TRN2", target_bir_lowering=False)
        self.ins = {}
        self.scr = {}
        self.toks = {}

    def inp(self, name, shape, dt=F32):
        t = self.nc.dram_tensor(name, list(shape), dt, kind="ExternalInput").ap()
        self.ins[name] = t
        return t

    def scratch(self, name, shape, dt):
        kind = "ExternalOutput" if self.debug else "Internal"
        t = self.nc.dram_tensor(name, list(shape), dt, kind=kind).ap()
        self.scr[name] = t
        return t

    def tok(self, key):
        if key not in self.toks:
            self.toks[key] = Tok(str(key))
        return self.toks[key]

    def build(self):
        nc, L, NT, NB = self.nc, self.L, self.NT, self.NB
        x = self.inp("x", [L, D])
        w_fm = self.inp("w_fm", [D, 97 * 128])
        w_tm = self.inp("w_tm", [D, 4096])
        ident = self.inp("ident", [128, 128])
        colp = self.inp("colp", [128, NCOL])
        out = self.nc.dram_tensor("out", [L, D], F32, kind="ExternalOutput").ap()
        self.out = out
        qkT = self.scratch("qkT", [2048, L], BF16)
        ucT = self.scratch("ucT", [6144, L], BF16)
        gT = self.scratch("gT", [4096, L], BF16)
        afbT = self.scratch("afbT", [128, L], F32)
        vtm = self.scratch("vtm", [L, 2048], BF16)
        srtm = self.scratch("srtm", [L, 2048], BF16)
        wa2p = self.inp("wa2p", [2, 64, 1024])
        gng = self.inp("gng", [1, 2048])
        masks = self.inp("masks", [128, 2, 512])
        ofD = self.scratch("ofD", [L, 2048], F32)
        ytm = self.scratch("ytm", [L, 2048], BF16)
        yT = self.scratch("yT", [2048, L], BF16)
        embT = self.inp("embT", [64, L])
        mlpw = self.inp("mlpw", [64, 3, 64])
        mlpc = self.inp("mlpc", [64, 4])
        w4aug = self.inp("w4aug", [65, 8192])
        negt = self.inp("negt", [128, NT])
        deltas = self.inp("deltas", [1, 2048])
        skipb = self.inp("skipb", [2, 2048])
        NK_, NKT_, HA_, KA_ = self.dft_dims()
        Fw = self.inp("Fw", [4, HA_, NK_], BF16)
        Iv = self.inp("Iv", [4, NK_, HA_], BF16)
        hsD = self.scratch("hsD", [2, L, 2048], BF16)
        hdD = self.scratch("hdD", [2, L, 2048], BF16)
        F2D = self.scratch("F2D", [2, NK_, 4, 2048], BF16)
        Y2D = self.scratch("Y2D", [4, NK_, 2048], BF16)
        vhtm = self.scratch("vhtm", [L, 2048], BF16)
        z1T = self.scratch("z1T", [2048, L], BF16)
        z1tm = self.scratch("z1tm", [L, 2048], BF16)
        yhT = self.scratch("yhT", [2048, L], BF16)
        w_go = self.inp("w_go", [2048, 2048])
        w_ho = self.inp("w_ho", [2048, 2048])
        w_out = self.inp("w_out", [2048, 2048])
        w_ff1 = self.inp("w_ff1", [2048, DFF])
        w_ff2 = self.inp("w_ff2", [DFF, 2048])
        lnp1 = self.inp("lnp1", [2, 2048])
        lnp2 = self.inp("lnp2", [2, 2048])
        mT = self.scratch("mT", [2048, L], BF16)
        h1D = self.scratch("h1D", [L, 2048], F32)
        h1b = self.scratch("h1b", [L, 2048], BF16)
        h1T = self.scratch("h1T", [2048, L], BF16)
        uT = self.scratch("uT", [DFF, L], BF16)
        ffT = self.scratch("ffT", [2048, L], F32)
        ffD = self.scratch("ffD", [L, 2048], F32)

        with ExitStack() as es:
            S = Sched(nc, es)
            self.S = S
            E = es.enter_context
            cp = E(nc.sbuf_tensor("colp_sb", [128, NCOL], F32))
            cp_t = T("colp")
            S.dma("sp", cp[:], colp[:, :], cp_t, writes=[cp_t])
            idb = E(nc.sbuf_tensor("ident_sb", [128, 128], BF16))
            idb_t = T("ident")
            S.dma("pool", idb[:], ident[:, :], idb_t, writes=[idb_t])
            self.cp, self.cp_t, self.idb, self.idb_t = cp, cp_t, idb, idb_t
            idf = E(nc.sbuf_tensor("identf_sb", [128, 128], F32))
            idf_t = T("identf")
            S.dma("sp", idf[:], ident[:, :], idf_t, writes=[idf_t])
            self.idf, self.idf_t = idf, idf_t

            self.phase_a(x, w_fm, w_tm, qkT, ucT, gT, afbT, vtm, srtm)
            if self.upto >= 2:
                self.phase_b(qkT, vtm, afbT, srtm, ofD, ytm, wa2p, gng, masks)
                self.transpose_dram(ytm, yT, L, 2048, lambda rt: [self.tok(("ytm", rt))],
                                    lambda rb: [self.tok(("yT", rb))], "trY")
            if self.upto >= 3:
                self.phase_c_filters(embT, mlpw, mlpc, w4aug, negt, deltas, hsD, hdD)
                self.phase_filter_spectra(Fw, hsD, hdD, F2D, skipb)
                self.transpose_dram(ucT[0:2048, :], vhtm, 2048, L, lambda rt: [self.tok(("ucT", rt))],
                                    lambda rb: [self.tok(("vhtm", rb))], "trV")
                self.phase_conv_fwd(0, vhtm, lambda j: [self.tok(("vhtm", rb)) for rb in range(4)], Fw, F2D, Y2D)
                self.phase_conv_inv(0, Y2D, Iv, ucT, 2048, lambda ct: [self.tok(("ucT", 16 + ct))], z1T,
                                    lambda ct, ab: [self.tok(("z1T", ct, ab))])
                self.transpose_dram(z1T, z1tm, 2048, L, lambda rt: [self.tok(("z1T", rt, ab)) for ab in range(max(1, L // 1024))],
                                    lambda rb: [self.tok(("z1tm", rb))], "trZ")
                self.phase_conv_fwd(1, z1tm, lambda j: [self.tok(("z1tm", rb)) for rb in range(4)], Fw, F2D, Y2D)
                self.phase_conv_inv(1, Y2D, Iv, ucT, 4096, lambda ct: [self.tok(("ucT", 32 + ct))], yhT,
                                    lambda ct, ab: [self.tok(("yhT", ct, ab))])
            if self.upto >= 4:
                self.phase_d(yT, yhT, gT, w_go, w_ho, mT)
                self.phase_e(mT, w_out, x, lnp1, h1D, h1b)
                self.transpose_dram(h1b, h1T, L, 2048, lambda rt: [self.tok(("h1b", rt))], lambda rb: [self.tok(("h1T", rb))], "trH")
                self.phase_f(h1T, w_ff1, uT)
                self.phase_g(uT, w_ff2, ffT)
                self.phase_h(ffT, h1D, lnp2, out, idf, idf_t)

            S.barrier()
        return nc

    def phase_a(self, x, w_fm, w_tm, qkT, ucT, gT, afbT, vtm, srtm):
        nc, S, L, NT, NB = self.nc, self.S, self.L, self.NT, self.NB
        cp, cp_t = self.cp, self.cp_t
        with ExitStack() as es:
            E = es.enter_context
            xT = E(nc.sbuf_tensor("xT", [128, 16, L], BF16))
            xT_t = [T(f"xT{i}") for i in range(NT)]
            with ExitStack() as es0:
                E0 = es0.enter_context
                xb = [E0(nc.sbuf_tensor(f"xb{i}", [128, D], BF16)) for i in range(2)]
                xb_t = [T(f"xb{i}") for i in range(2)]
                pt = [E0(nc.psum_tensor(f"pt{i}", [128, D], BF16)) for i in range(2)]
                pt_t = [T(f"pt{i}") for i in range(2)]
                for tt in range(NT):
                    b = tt % 2
                    S.dma("pool", xb[b][:], x[tt * 128:(tt + 1) * 128, :], xb_t[b], writes=[xb_t[b]])
                    for dt in range(16):
                        S.op("pe", lambda e, dt=dt, b=b: e.transpose(pt[b][:, dt * 128:(dt + 1) * 128],
                                                                    xb[b][:, dt * 128:(dt + 1) * 128], self.idb[:]),
                             reads=[xb_t[b], self.idb_t], writes=[pt_t[b]], sig=(dt == 15))
                    eng = "dve" if tt % 2 == 0 else "act"
                    src = pt[b][:].rearrange("p (k t) -> p k t", t=128)
                    dst = xT[:, :, tt * 128:(tt + 1) * 128]
                    if eng == "dve":
                        S.op("dve", lambda e, s=src, d=dst: e.tensor_copy(out=d, in_=s), reads=[pt_t[b]], writes=[xT_t[tt]])
                    else:
                        S.op("act", lambda e, s=src, d=dst: e.activation(out=d, in_=s, func=AF.Copy), reads=[pt_t[b]], writes=[xT_t[tt]])
                S.barrier()
            with ExitStack() as es1:
                E1 = es1.enter_context
                NWB = 3
                wp = [E1(nc.sbuf_tensor(f"wp{i}", [128, 16, 256], BF16)) for i in range(NWB)]
                wp_t = [T(f"wp{i}") for i in range(NWB)]
                ps = [E1(nc.psum_tensor(f"psA{i}", [128, L], F32)) for i in range(2)]
                ps_t = [T(f"psA{i}") for i in range(2)]
                ob = [E1(nc.sbuf_tensor(f"obA{i}", [128, L], BF16)) for i in range(2)]
                ob_t = [T(f"obA{i}") for i in range(2)]
                of = [E1(nc.sbuf_tensor(f"ofA{i}", [128, L], F32)) for i in range(2)]
                of_t = [T(f"ofA{i}") for i in range(2)]
                npan = 49
                def load(pi):
                    w = 256 if pi < 48 else 128
                    b = pi % NWB
                    S.dma("pool", wp[b][:, :, 0:w], pk(w_fm, 0, 16, pi * 256, w), wp_t[b], writes=[wp_t[b]])
                load(0)
                load(1)
                mt = 0
                for pi in range(npan):
                    if pi + 2 < npan:
                        load(pi + 2)
                    b = pi % NWB
                    for mi in range(2 if pi < 48 else 1):
                        pb = mt % 2
                        for nb in range(NB):
                            for kt in range(16):
                                S.op("pe", lambda e, kt=kt, nb=nb, mi=mi, b=b, pb=pb: e.matmul(
                                    ps[pb][:, nb * 512:(nb + 1) * 512], lhsT=wp[b][:, kt, mi * 128:(mi + 1) * 128],
                                    rhs=xT[:, kt, nb * 512:(nb + 1) * 512], start=(kt == 0), stop=(kt == 15)),
                                    reads=[wp_t[b]] + xT_t, writes=[ps_t[pb]], sig=(kt == 15 and nb == NB - 1))
                        if mt < 16:
                            dst_tok = self.tok(("qkT", mt))
                            if mt % 2 == 0:
                                S.op("dve", lambda e, pb=pb: e.tensor_copy(out=ob[pb][:], in_=ps[pb][:]), reads=[ps_t[pb]], writes=[ob_t[pb]])
                            else:
                                S.op("act", lambda e, pb=pb: e.activation(out=ob[pb][:], in_=ps[pb][:], func=AF.Copy), reads=[ps_t[pb]], writes=[ob_t[pb]])
                            S.dma("sp", qkT[mt * 128:(mt + 1) * 128, :], ob[pb][:], ob_t[pb], reads=[ob_t[pb]], writes=[dst_tok])
                        elif mt < 64:
                            ct = mt - 16
                            c0 = COL_CONV + ct * 4
                            S.op("act", lambda e, pb=pb, c0=c0: e.activation(out=of[pb][:], in_=ps[pb][:], func=AF.Identity,
                                                                              scale=cp[:, c0 + 1:c0 + 2], bias=cp[:, c0 + 3:c0 + 4]),
                                 reads=[ps_t[pb], cp_t], writes=[of_t[pb]])
                            S.op("dve", lambda e, pb=pb, c0=c0: e.scalar_tensor_tensor(out=of[pb][:, 1:L], in0=ps[pb][:, 0:L - 1], scalar=cp[:, c0:c0 + 1],
                                                                                       in1=of[pb][:, 1:L], op0=ALU.mult, op1=ALU.add),
                                 reads=[ps_t[pb], of_t[pb], cp_t], writes=[of_t[pb]])
                            S.op("dve", lambda e, pb=pb, c0=c0: e.scalar_tensor_tensor(out=ob[pb][:, 0:L - 1], in0=ps[pb][:, 1:L], scalar=cp[:, c0 + 2:c0 + 3],
                                                                                       in1=of[pb][:, 0:L - 1], op0=ALU.mult, op1=ALU.add),
                                 reads=[ps_t[pb], of_t[pb], cp_t], writes=[ob_t[pb]])
                            S.op("dve", lambda e, pb=pb: e.tensor_copy(out=ob[pb][:, L - 1:L], in_=of[pb][:, L - 1:L]),
                                 reads=[of_t[pb], ob_t[pb]], writes=[ob_t[pb]])
                            dst_tok = self.tok(("ucT", ct))
                            S.dma("sp", ucT[ct * 128:(ct + 1) * 128, :], ob[pb][:], ob_t[pb], reads=[ob_t[pb]], writes=[dst_tok])
                        elif mt < 96:
                            gt = mt - 64
                            S.op("act", lambda e, pb=pb: e.activation(out=ob[pb][:], in_=ps[pb][:], func=AF.Sigmoid), reads=[ps_t[pb]], writes=[ob_t[pb]])
                            dst_tok = self.tok(("gT", gt))
                            S.dma("sp", gT[gt * 128:(gt + 1) * 128, :], ob[pb][:], ob_t[pb], reads=[ob_t[pb]], writes=[dst_tok])
                        else:
                            S.op("dve", lambda e, pb=pb: e.tensor_copy(out=of[pb][:], in_=ps[pb][:]), reads=[ps_t[pb]], writes=[of_t[pb]])
                            dst_tok = self.tok(("afbT", 0))
                            S.dma("sp", afbT[:, :], of[pb][:], of_t[pb], reads=[of_t[pb]], writes=[dst_tok])
                        mt += 1
                S.barrier()
            with ExitStack() as es2:
                E2 = es2.enter_context
                wr = [E2(nc.sbuf_tensor(f"wr{i}", [128, 16, 512], BF16)) for i in range(2)]
                wr_t = [T(f"wr{i}") for i in range(2)]
                ps = [E2(nc.psum_tensor(f"psB{i}", [128, 512], F32)) for i in range(4)]
                ps_t = [T(f"psB{i}") for i in range(4)]
                ob = [E2(nc.sbuf_tensor(f"obB{i}", [128, 512], BF16)) for i in range(4)]
                ob_t = [T(f"obB{i}") for i in range(4)]
                def loadr(cb):
                    S.dma("pool", wr[cb % 2][:], pk(w_tm, 0, 16, cb * 512, 512), wr_t[cb % 2], writes=[wr_t[cb % 2]])
                loadr(0)
                it = 0
                for cb in range(8):
                    if cb + 1 < 8:
                        loadr(cb + 1)
                    b = cb % 2
                    for tt in range(NT):
                        pb = it % 4
                        for kt in range(16):
                            S.op("pe", lambda e, kt=kt, tt=tt, b=b, pb=pb: e.matmul(
                                ps[pb][:], lhsT=xT[:, kt, tt * 128:(tt + 1) * 128], rhs=wr[b][:, kt, :], start=(kt == 0), stop=(kt == 15)),
                                reads=[wr_t[b], xT_t[tt]], writes=[ps_t[pb]], sig=(kt == 15))
                        if cb < 4:
                            S.op("dve", lambda e, pb=pb: e.tensor_copy(out=ob[pb][:], in_=ps[pb][:]), reads=[ps_t[pb]], writes=[ob_t[pb]])
                            dst, dtok = vtm, self.tok(("vtm", tt, cb))
                        else:
                            S.op("act", lambda e, pb=pb: e.activation(out=ob[pb][:], in_=ps[pb][:], func=AF.Silu), reads=[ps_t[pb]], writes=[ob_t[pb]])
                            dst, dtok = srtm, self.tok(("srtm", tt, cb - 4))
                        c0 = (cb % 4) * 512
                        S.dma("sp", dst[tt * 128:(tt + 1) * 128, c0:c0 + 512], ob[pb][:], ob_t[pb], reads=[ob_t[pb]], writes=[dtok])
                        it += 1
                S.barrier()


    def transpose_dram(self, src, dst, R, C, src_tok, dst_tok, name, dt=BF16, ident=None, ident_t=None):
        nc, S = self.nc, self.S
        CT = C // 128
        idm = self.idb if ident is None else ident
        idm_t = self.idb_t if ident_t is None else ident_t
        with ExitStack() as es:
            E = es.enter_context
            ib = [E(nc.sbuf_tensor(f"{name}_ib{i}", [128, C], dt)) for i in range(2)]
            ib_t = [T() for _ in range(2)]
            pt = [E(nc.psum_tensor(f"{name}_pt{i}", [128, C], dt)) for i in range(2)]
            pt_t = [T() for _ in range(2)]
            st = [E(nc.sbuf_tensor(f"{name}_st{i}", [128, CT, 512], dt)) for i in range(2)]
            st_t = [T() for _ in range(2)]
            S.dma("sp", ib[0][:], src[0:128, :], ib_t[0], reads=src_tok(0), writes=[ib_t[0]])
            for rt in range(R // 128):
                b = rt % 2
                rb = rt // 4
                sb = rb % 2
                if rt + 1 < R // 128:
                    S.dma("sp", ib[1 - b][:], src[(rt + 1) * 128:(rt + 2) * 128, :], ib_t[1 - b], reads=src_tok(rt + 1), writes=[ib_t[1 - b]])
                for ct in range(CT):
                    S.op("pe", lambda e, ct=ct, b=b: e.transpose(pt[b][:, ct * 128:(ct + 1) * 128], ib[b][:, ct * 128:(ct + 1) * 128], idm[:]),
                         reads=[ib_t[b], idm_t], writes=[pt_t[b]], sig=(ct == CT - 1))
                srcv = pt[b][:].rearrange("p (k t) -> p k t", t=128)
                dstv = st[sb][:, :, (rt % 4) * 128:(rt % 4 + 1) * 128]
                if rt % 2 == 0:
                    S.op("dve", lambda e, s_=srcv, d_=dstv: e.tensor_copy(out=d_, in_=s_), reads=[pt_t[b]], writes=[st_t[sb]])
                else:
                    S.op("act", lambda e, s_=srcv, d_=dstv: e.activation(out=d_, in_=s_, func=AF.Copy), reads=[pt_t[b]], writes=[st_t[sb]])
                if rt % 4 == 3:
                    S.dma("sp", dst[:, rb * 512:(rb + 1) * 512].rearrange("(k p) r -> p k r", p=128), st[sb][:], st_t[sb],
                          reads=[st_t[sb]], writes=dst_tok(rb))
            S.barrier()

    def phase_b(self, qkT, vtm, afbT, srtm, ofD, ytm, wa2p, gng, masks):
        nc, S, L, NT, NB = self.nc, self.S, self.L, self.NT, self.NB
        cp, cp_t = self.cp, self.cp_t
        qk_toks = [self.tok(("qkT", i)) for i in range(16)]
        with ExitStack() as es:
            E = es.enter_context
            qx = E(nc.sbuf_tensor("qx", [128, 8, L], BF16))
            kx = E(nc.sbuf_tensor("kx", [128, 8, L], BF16))
            qx_t = [T() for _ in range(8)]
            kx_t = [T() for _ in range(8)]
            dec = E(nc.sbuf_tensor("dec", [128, 8, NT], F32))
            dec_t = [T() for _ in range(8)]
            mk = E(nc.sbuf_tensor("mk", [128, 2, 512], F32))
            mk_t = T()
            S.dma("sp", mk[:], masks[:, :, :], mk_t, writes=[mk_t])
            smask = E(nc.sbuf_tensor("smask", [128, L], F32))
            smask_t = T()
            S.op("dve", lambda e: e.memset(smask[:], 1.0), writes=[smask_t])
            S.op("dve", lambda e: e.memset(smask[:].rearrange("p (n c) -> p n c", c=CH)[:, :, 0:1], 0.0), writes=[smask_t])
            negba = E(nc.sbuf_tensor("negba", [128, 16], F32))
            negba_t = T()
            S.op("dve", lambda e: e.tensor_scalar(out=negba[:], in0=cp[:, COL_BA:COL_BA + 16], scalar1=-1.0, scalar2=None, op0=ALU.mult),
                 reads=[cp_t], writes=[negba_t])
            S32 = E(nc.sbuf_tensor("S32", [128, 8, 512], F32))
            Sbf = [E(nc.sbuf_tensor(f"Sbf{i}", [128, 8, 512], BF16)) for i in range(2)]
            S32_t = [T() for _ in range(8)]
            Sbf_t = [[T() for _ in range(8)] for _ in range(2)]

            for dr in (0, 1):
                with ExitStack() as esp:
                    Ep = esp.enter_context
                    zps = [Ep(nc.psum_tensor(f"zps{dr}{i}", [128, L], F32)) for i in range(2)]
                    zps_t = [T() for _ in range(2)]
                    w2 = Ep(nc.sbuf_tensor(f"w2_{dr}", [64, 1024], F32))
                    w2_t = T()
                    S.dma("sp", w2[:], wa2p[dr, :, :], w2_t, writes=[w2_t])
                    af = Ep(nc.sbuf_tensor(f"af_{dr}", [64, L], F32))
                    af_t = T()
                    S.dma("sp", af[:], afbT[0:64, :], af_t, reads=[self.tok(("afbT", 0))], writes=[af_t])
                    nl = [Ep(nc.sbuf_tensor(f"nl{dr}{i}", [128, L], F32)) for i in range(2)]
                    cs = [Ep(nc.sbuf_tensor(f"cs{dr}{i}", [128, L], F32)) for i in range(2)]
                    e1 = [Ep(nc.sbuf_tensor(f"e1{dr}{i}", [128, L], F32)) for i in range(2)]
                    e2 = [Ep(nc.sbuf_tensor(f"e2{dr}{i}", [128, L], F32)) for i in range(2)]
                    qr = [Ep(nc.sbuf_tensor(f"qr{dr}{i}", [128, L], BF16)) for i in range(2)]
                    kr = [Ep(nc.sbuf_tensor(f"kr{dr}{i}", [128, L], BF16)) for i in range(2)]
                    nl_t = [T() for _ in range(2)]
                    cs_t = [T() for _ in range(2)]
                    e1_t = [T() for _ in range(2)]
                    e2_t = [T() for _ in range(2)]
                    qr_t = [T() for _ in range(2)]
                    kr_t = [T() for _ in range(2)]
                    def prep1(dt):
                        b = dt % 2
                        S.dma("sp", qr[b][:], qkT[dt * 128:(dt + 1) * 128, :], qr_t[b], reads=[qk_toks[dt]], writes=[qr_t[b]])
                        S.dma("sp", kr[b][:], qkT[1024 + dt * 128:1024 + (dt + 1) * 128, :], kr_t[b], reads=[qk_toks[8 + dt]], writes=[kr_t[b]])
                        for nb in range(NB):
                            S.op("pe", lambda e, nb=nb, dt=dt, b=b: e.matmul(zps[b][:, nb * 512:(nb + 1) * 512], lhsT=w2[:, dt * 128:(dt + 1) * 128],
                                                                           rhs=af[:, nb * 512:(nb + 1) * 512], start=True, stop=True),
                                 reads=[w2_t, af_t], writes=[zps_t[b]], sig=(nb == NB - 1))
                        bc = dr * 8 + dt
                        S.op("act", lambda e, b=b, bc=bc: e.activation(out=e1[b][:], in_=zps[b][:], func=AF.Exp, scale=-1.0, bias=negba[:, bc:bc + 1]),
                             reads=[zps_t[b], negba_t], writes=[e1_t[b]])
                        S.op("act", lambda e, b=b: e.activation(out=nl[b][:], in_=e1[b][:], func=AF.Ln, bias=1.0, scale=1.0),
                             reads=[e1_t[b]], writes=[nl_t[b]])
                        S.op("dve", lambda e, b=b: e.tensor_tensor_scan(out=cs[b][:], data0=smask[:], data1=nl[b][:], initial=0.0, op0=ALU.mult, op1=ALU.add),
                             reads=[smask_t, nl_t[b]], writes=[cs_t[b]])

                    def prep2(dt):
                        b = dt % 2
                        bc = dr * 8 + dt
                        csl = cs[b][:].rearrange("p (n c) -> p n c", c=CH)[:, :, CH - 1:CH]
                        if dr == 0:
                            S.op("act", lambda e, b=b: e.activation(out=e1[b][:], in_=cs[b][:], func=AF.Exp, scale=-1.0 / TAU),
                                 reads=[cs_t[b]], writes=[e1_t[b]])
                            S.op("act", lambda e, b=b: e.activation(out=e2[b][:], in_=cs[b][:], func=AF.Exp, scale=1.0 / TAU),
                                 reads=[cs_t[b]], writes=[e2_t[b]])
                        else:
                            S.op("dve", lambda e, b=b: e.tensor_tensor(out=nl[b][:], in0=cs[b][:], in1=nl[b][:], op=ALU.subtract),
                                 reads=[cs_t[b], nl_t[b]], writes=[nl_t[b]])
                            S.op("act", lambda e, b=b: e.activation(out=e1[b][:], in_=nl[b][:], func=AF.Exp, scale=1.0 / TAU),
                                 reads=[nl_t[b]], writes=[e1_t[b]])
                            S.op("act", lambda e, b=b: e.activation(out=e2[b][:], in_=nl[b][:], func=AF.Exp, scale=-1.0 / TAU),
                                 reads=[nl_t[b]], writes=[e2_t[b]])
                        S.op("act", lambda e, dt=dt, csl=csl: e.activation(out=dec[:, dt, :].rearrange("p (n o) -> p n o", o=1), in_=csl, func=AF.Exp, scale=-1.0 / TAU),
                             reads=[cs_t[b]], writes=[dec_t[dt]])
                        S.op("dve", lambda e, b=b, dt=dt: e.scalar_tensor_tensor(out=qx[:, dt, :], in0=qr[b][:], scalar=DK ** -0.5, in1=e1[b][:],
                                                                                  op0=ALU.mult, op1=ALU.mult),
                             reads=[qr_t[b], e1_t[b]], writes=[qx_t[dt]])
                        S.op("pool", lambda e, b=b, dt=dt: e.tensor_tensor(out=kx[:, dt, :], in0=kr[b][:], in1=e2[b][:], op=ALU.mult),
                             reads=[kr_t[b], e2_t[b]], writes=[kx_t[dt]])
                    prep1(0)
                    for dt in range(8):
                        if dt + 1 < 8:
                            prep1(dt + 1)
                        prep2(dt)
                    S.barrier()
                with ExitStack() as esc:
                    Ec = esc.enter_context
                    vb = [Ec(nc.sbuf_tensor(f"vb{dr}{i}", [128, 2048], BF16)) for i in range(2)]
                    vb_t = [T() for _ in range(2)]
                    pkT = Ec(nc.psum_tensor(f"pkT{dr}", [128, 1024], BF16))
                    pkT_t = T()
                    kxT = [Ec(nc.sbuf_tensor(f"kxT{dr}{i}", [128, 1024], BF16)) for i in range(2)]
                    kxT_t = [T() for _ in range(2)]
                    psS = Ec(nc.psum_tensor(f"psS{dr}", [128, 512], F32))
                    psS_t = T()
                    sT = [Ec(nc.sbuf_tensor(f"sT{dr}{i}", [128, 512], BF16)) for i in range(2)]
                    sT_t = [T() for _ in range(2)]
                    psO = Ec(nc.psum_tensor(f"psO{dr}", [128, 2048], F32))
                    psO_t = [T() for _ in range(4)]
                    psKV = [Ec(nc.psum_tensor(f"psKV{dr}{i}", [128, 512], F32)) for i in range(2)]
                    psKV_t = [T() for _ in range(2)]
                    o32 = [Ec(nc.sbuf_tensor(f"o32{dr}{i}", [128, 2048], F32)) for i in range(2)]
                    o32_t = [T() for _ in range(2)]
                    if dr == 1:
                        ofin = [Ec(nc.sbuf_tensor(f"ofin{i}", [128, 2048], F32)) for i in range(2)]
                        ofin_t = [T() for _ in range(2)]
                        srin = [Ec(nc.sbuf_tensor(f"srin{i}", [128, 2048], BF16)) for i in range(2)]
                        srin_t = [T() for _ in range(2)]
                        ybf = [Ec(nc.sbuf_tensor(f"ybf{i}", [128, 2048], BF16)) for i in range(2)]
                        ybf_t = [T() for _ in range(2)]
                        junk = Ec(nc.sbuf_tensor("junkB", [128, 512], F32))
                        junk_t = T()
                        ssq = [Ec(nc.sbuf_tensor(f"ssq{i}", [128, 4], F32)) for i in range(2)]
                        ssq_t = [T() for _ in range(2)]
                    order = list(range(NT)) if dr == 0 else list(range(NT - 1, -1, -1))

                    def stage_a(step, n):
                        b = step % 2
                        c0 = n * CH
                        S.dma("sp", vb[b][:], vtm[c0:c0 + 128, :], vb_t[b], reads=[self.tok(("vtm", n, j)) for j in range(4)], writes=[vb_t[b]])
                        if dr == 1:
                            S.dma("sp", ofin[b][:], ofD[c0:c0 + 128, :], ofin_t[b], reads=[self.tok(("ofD", n))], writes=[ofin_t[b]])
                            S.dma("sp", srin[b][:], srtm[c0:c0 + 128, :], srin_t[b], reads=[self.tok(("srtm", n, j)) for j in range(4)], writes=[srin_t[b]])
                        for dt in range(8):
                            S.op("pe", lambda e, dt=dt, c0=c0: e.transpose(pkT[:, dt * 128:(dt + 1) * 128], kx[:, dt, c0:c0 + 128], self.idb[:]),
                                 reads=[kx_t[dt], self.idb_t], writes=[pkT_t], sig=(dt == 7))
                        S.op("dve", lambda e, b=b: e.tensor_copy(out=kxT[b][:], in_=pkT[:]), reads=[pkT_t], writes=[kxT_t[b]])
                        for h in range(4):
                            for dd in range(2):
                                dt = 2 * h + dd
                                S.op("pe", lambda e, h=h, dt=dt, dd=dd, c0=c0: e.matmul(psS[:, h * 128:(h + 1) * 128], lhsT=kx[:, dt, c0:c0 + 128],
                                                                                     rhs=qx[:, dt, c0:c0 + 128], start=(dd == 0), stop=(dd == 1)),
                                     reads=[kx_t[dt], qx_t[dt]], writes=[psS_t], sig=(h == 3 and dd == 1))
                        S.op("dve", lambda e, b=b: e.tensor_tensor(out=sT[b][:], in0=psS[:], in1=mk[:, dr, :], op=ALU.mult),
                             reads=[psS_t, mk_t], writes=[sT_t[b]])

                    def stage_state(step, n):
                        b = step % 2
                        c0 = n * CH
                        first = (step == 0)
                        last = (step == NT - 1)
                        sw = step % 2
                        if not last:
                            m = n if dr == 0 else n - 1
                            mprev = mprev_box[0]
                            for h in range(4):
                                for dd in range(2):
                                    dt = 2 * h + dd
                                    kb = dt % 2
                                    S.op("pe", lambda e, dt=dt, h=h, b=b, kb=kb: e.matmul(psKV[kb][:], lhsT=kxT[b][:, dt * 128:(dt + 1) * 128],
                                                                                       rhs=vb[b][:, h * 512:(h + 1) * 512], start=True, stop=True),
                                         reads=[kxT_t[b], vb_t[b]], writes=[psKV_t[kb]])
                                    if first:
                                        S.op("dve", lambda e, dt=dt, kb=kb: e.tensor_copy(out=S32[:, dt, :], in_=psKV[kb][:]),
                                             reads=[psKV_t[kb]], writes=[S32_t[dt]])
                                    else:
                                        S.op("dve", lambda e, dt=dt, kb=kb, mprev=mprev: e.scalar_tensor_tensor(out=S32[:, dt, :], in0=S32[:, dt, :], scalar=dec[:, dt, mprev:mprev + 1],
                                                                                                       in1=psKV[kb][:], op0=ALU.mult, op1=ALU.add),
                                             reads=[S32_t[dt], psKV_t[kb], dec_t[dt]], writes=[S32_t[dt]])
                                    S.op("act", lambda e, dt=dt, m=m, sw=sw: e.activation(out=Sbf[sw][:, dt, :], in_=S32[:, dt, :], func=AF.Identity, scale=dec[:, dt, m:m + 1]),
                                         reads=[S32_t[dt], dec_t[dt]], writes=[Sbf_t[sw][dt]])
                            mprev_box[0] = m


                    def stage_out(step, n):
                        b = step % 2
                        c0 = n * CH
                        first = (step == 0)
                        last = (step == NT - 1)
                        sr_ = (step - 1) % 2
                        bw = (dr == 1)
                        for h in range(4):
                            S.op("pe", lambda e, h=h, b=b: e.matmul(psO[:, h * 512:(h + 1) * 512], lhsT=sT[b][:, h * 128:(h + 1) * 128],
                                                                    rhs=vb[b][:, h * 512:(h + 1) * 512], start=True, stop=(first and not bw)),
                                 reads=[sT_t[b], vb_t[b]], writes=[psO_t[h]], sig=(first and not bw))
                            if not first:
                                for dd in range(2):
                                    dt = 2 * h + dd
                                    S.op("pe", lambda e, h=h, dt=dt, dd=dd, c0=c0, sr_=sr_: e.matmul(psO[:, h * 512:(h + 1) * 512], lhsT=qx[:, dt, c0:c0 + 128],
                                                                                         rhs=Sbf[sr_][:, dt, :], start=False, stop=(dd == 1 and not bw)),
                                         reads=[qx_t[dt], Sbf_t[sr_][dt]], writes=[psO_t[h]], sig=(dd == 1 and not bw))
                            if bw:
                                S.op("pe", lambda e, h=h, b=b: e.matmul(psO[:, h * 512:(h + 1) * 512], lhsT=self.idf[:], rhs=ofin[b][:, h * 512:(h + 1) * 512],
                                                                        start=False, stop=True),
                                     reads=[self.idf_t, ofin_t[b]], writes=[psO_t[h]], sig=True)
                        if dr == 0:
                            S.op("act", lambda e, b=b: e.activation(out=o32[b][:], in_=psO[:], func=AF.Copy), reads=psO_t, writes=[o32_t[b]])
                            S.dma("pool", ofD[c0:c0 + 128, :], o32[b][:], o32_t[b], reads=[o32_t[b]], writes=[self.tok(("ofD", n))])
                        else:
                            for h in range(4):
                                S.op("act", lambda e, b=b, h=h: e.activation(out=junk[:], in_=psO[:, h * 512:(h + 1) * 512], func=AF.Square,
                                                                            accum_out=ssq[b][:, h:h + 1]),
                                     reads=[psO_t[h]], writes=[junk_t, ssq_t[b]])
                            S.op("dve", lambda e, b=b: e.tensor_scalar(out=ssq[b][:], in0=ssq[b][:], scalar1=1.0 / DV, scalar2=LN_EPS, op0=ALU.mult, op1=ALU.add),
                                 reads=[ssq_t[b]], writes=[ssq_t[b]])
                            S.op("act", lambda e, b=b: e.activation(out=ssq[b][:], in_=ssq[b][:], func=AF.Ln), reads=[ssq_t[b]], writes=[ssq_t[b]])
                            S.op("act", lambda e, b=b: e.activation(out=ssq[b][:], in_=ssq[b][:], func=AF.Exp, scale=-0.5), reads=[ssq_t[b]], writes=[ssq_t[b]])
                            for h in range(4):
                                S.op("dve", lambda e, b=b, h=h: e.scalar_tensor_tensor(out=ybf[b][:, h * 512:(h + 1) * 512], in0=psO[:, h * 512:(h + 1) * 512],
                                                                                      scalar=ssq[b][:, h:h + 1], in1=srin[b][:, h * 512:(h + 1) * 512],
                                                                                      op0=ALU.mult, op1=ALU.mult),
                                     reads=[psO_t[h], ssq_t[b], srin_t[b]], writes=[ybf_t[b]])
                            S.dma("pool", ytm[c0:c0 + 128, :], ybf[b][:], ybf_t[b], reads=[ybf_t[b]], writes=[self.tok(("ytm", n))])
                    mprev_box = [None]
                    stage_a(0, order[0])
                    for step, n in enumerate(order):
                        if step + 1 < NT:
                            stage_a(step + 1, order[step + 1])
                        stage_state(step, n)
                        stage_out(step, n)
                    S.barrier()


    def phase_c_filters(self, embT, mlpw, mlpc, w4aug, negt, deltas, hsD, hdD):
        nc, S, L, NT, NB = self.nc, self.S, self.L, self.NT, self.NB
        PI = math.pi
        with ExitStack() as es:
            E = es.enter_context
            hid = E(nc.sbuf_tensor("hid3", [65, L], BF16))
            hid_t = T()
            w4 = E(nc.sbuf_tensor("w4sb", [65, 8192], BF16))
            w4_t = T()
            S.dma("pool", w4[:], w4aug[:, :], w4_t, writes=[w4_t])
            wsd = E(nc.sbuf_tensor("w4sd", [65, 2, 2, 2048], BF16))
            wsd_t = T()
            for o in range(2):
                S.op("dve", lambda e, o=o: e.tensor_tensor(out=wsd[:, o, 0, :], in0=w4[:, o * 4096:o * 4096 + 2048], in1=w4[:, o * 4096 + 2048:o * 4096 + 4096], op=ALU.add),
                     reads=[w4_t], writes=[wsd_t])
                S.op("dve", lambda e, o=o: e.tensor_tensor(out=wsd[:, o, 1, :], in0=w4[:, o * 4096:o * 4096 + 2048], in1=w4[:, o * 4096 + 2048:o * 4096 + 4096], op=ALU.subtract),
                     reads=[w4_t], writes=[wsd_t])
            with ExitStack() as es0:
                E0 = es0.enter_context
                em = E0(nc.sbuf_tensor("embsb", [64, L], F32))
                em_t = T()
                S.dma("sp", em[:], embT[:, :], em_t, writes=[em_t])
                mw = E0(nc.sbuf_tensor("mlpw_sb", [64, 3, 64], F32))
                mw_t = T()
                S.dma("sp", mw[:], mlpw[:, :, :], mw_t, writes=[mw_t])
                mc = E0(nc.sbuf_tensor("mlpc_sb", [64, 8], F32))
                mc_t = T()
                S.dma("sp", mc[:, 0:4], mlpc[:, :], mc_t, writes=[mc_t])
                for l in range(3):
                    S.op("dve", lambda e, l=l: e.tensor_tensor(out=mc[:, 4 + l:5 + l], in0=mc[:, l:l + 1], in1=mc[:, 3:4], op=ALU.mult),
                         reads=[mc_t], writes=[mc_t])
                hp = E0(nc.psum_tensor("hps", [64, L], F32))
                hp_t = T()
                ha = [E0(nc.sbuf_tensor(f"ha{i}", [64, L], F32)) for i in range(2)]
                ha_t = [T() for _ in range(2)]
                t1 = E0(nc.sbuf_tensor("hwrap1", [64, L], F32))
                t1_t = T()
                t2 = E0(nc.sbuf_tensor("hwrap2", [64, L], F32))
                t2_t = T()
                cur, cur_t = em, em_t
                for l in range(3):
                    for nb in range(NB):
                        S.op("pe", lambda e, l=l, nb=nb, cur=cur: e.matmul(hp[:, nb * 512:(nb + 1) * 512], lhsT=mw[:, l, :], rhs=cur[:, nb * 512:(nb + 1) * 512],
                                                                       start=True, stop=True),
                             reads=[mw_t, cur_t], writes=[hp_t], sig=(nb == NB - 1))
                    a, a_t = ha[l % 2], ha_t[l % 2]
                    S.op("act", lambda e, l=l, a=a: e.activation(out=a[:], in_=hp[:], func=AF.Identity, scale=mc[:, 3:4], bias=mc[:, 4 + l:5 + l]),
                         reads=[hp_t, mc_t], writes=[a_t])
                    S.op("dve", lambda e, a=a: e.tensor_scalar(out=t1[:], in0=a[:], scalar1=PI, scalar2=-2.0 * PI, op0=ALU.is_gt, op1=ALU.mult),
                         reads=[a_t], writes=[t1_t])
                    S.op("dve", lambda e, a=a: e.tensor_scalar(out=t2[:], in0=a[:], scalar1=-PI, scalar2=2.0 * PI, op0=ALU.is_lt, op1=ALU.mult),
                         reads=[a_t], writes=[t2_t])
                    S.op("dve", lambda e, a=a: e.tensor_tensor(out=a[:], in0=a[:], in1=t1[:], op=ALU.add), reads=[a_t, t1_t], writes=[a_t])
                    S.op("dve", lambda e, a=a: e.tensor_tensor(out=a[:], in0=a[:], in1=t2[:], op=ALU.add), reads=[a_t, t2_t], writes=[a_t])
                    if l < 2:
                        S.op("act", lambda e, a=a: e.activation(out=a[:], in_=a[:], func=AF.Sin), reads=[a_t], writes=[a_t])
                        cur, cur_t = a, a_t
                    else:
                        S.op("act", lambda e, a=a: e.activation(out=hid[0:64, :], in_=a[:], func=AF.Sin), reads=[a_t], writes=[hid_t])
                        S.op("dve", lambda e: e.memset(hid[64:65, :], 1.0), writes=[hid_t])
                S.barrier()
            with ExitStack() as es1:
                E1 = es1.enter_context
                dl = E1(nc.sbuf_tensor("dlb", [128, 2048], F32))
                dl_t = T()
                S.dma("sp", dl[:], deltas[0, :].partition_broadcast(128), dl_t, writes=[dl_t])
                ng = E1(nc.sbuf_tensor("negt_sb", [128, NT], F32))
                ng_t = T()
                S.dma("sp", ng[:], negt[:, :], ng_t, writes=[ng_t])
                dtile = [E1(nc.sbuf_tensor(f"dtile{i}", [128, 512], F32)) for i in range(2)]
                dtile_t = [T() for _ in range(2)]
                p0 = [E1(nc.psum_tensor(f"fp0{i}", [128, 512], F32)) for i in range(2)]
                p1 = [E1(nc.psum_tensor(f"fp1{i}", [128, 512], F32)) for i in range(2)]
                p0_t = [T() for _ in range(2)]
                p1_t = [T() for _ in range(2)]
                h0 = [E1(nc.sbuf_tensor(f"fh0{i}", [128, 512], F32)) for i in range(2)]
                h1 = [E1(nc.sbuf_tensor(f"fh1{i}", [128, 512], F32)) for i in range(2)]
                h0_t = [T() for _ in range(2)]
                h1_t = [T() for _ in range(2)]
                hs = [E1(nc.sbuf_tensor(f"fhs{i}", [128, 512], BF16)) for i in range(2)]
                hd = [E1(nc.sbuf_tensor(f"fhd{i}", [128, 512], BF16)) for i in range(2)]
                hs_t = [T() for _ in range(2)]
                hd_t = [T() for _ in range(2)]
                it = 0
                for tt in range(NT):
                    for cb in range(4):
                        db = (tt * 4 + cb) % 2
                        S.op("act", lambda e, db=db, cb=cb, tt=tt: e.activation(out=dtile[db][:], in_=dl[:, cb * 512:(cb + 1) * 512], func=AF.Exp, scale=ng[:, tt:tt + 1]),
                             reads=[dl_t, ng_t], writes=[dtile_t[db]])
                        for o in range(2):
                            b = it % 2
                            c0 = o * 4096 + cb * 512
                            if tt == 0:
                                S.op("pe", lambda e, b=b, c0=c0, tt=tt: e.matmul(p0[b][:], lhsT=hid[:, tt * 128:(tt + 1) * 128], rhs=w4[:, c0:c0 + 512], start=True, stop=True),
                                     reads=[hid_t, w4_t], writes=[p0_t[b]])
                                S.op("pe", lambda e, b=b, c0=c0, tt=tt: e.matmul(p1[b][:], lhsT=hid[:, tt * 128:(tt + 1) * 128], rhs=w4[:, c0 + 2048:c0 + 2560], start=True, stop=True),
                                     reads=[hid_t, w4_t], writes=[p1_t[b]])
                                S.op("dve", lambda e, b=b, db=db: e.tensor_tensor(out=h0[b][:], in0=p0[b][:], in1=dtile[db][:], op=ALU.mult),
                                     reads=[p0_t[b], dtile_t[db]], writes=[h0_t[b]])
                                S.op("dve", lambda e, b=b, db=db: e.tensor_tensor(out=h1[b][:], in0=p1[b][:], in1=dtile[db][:], op=ALU.mult),
                                     reads=[p1_t[b], dtile_t[db]], writes=[h1_t[b]])
                                S.op("dve", lambda e, b=b: e.memset(h1[b][0:1, :], 0.0), reads=[h1_t[b]], writes=[h1_t[b]])
                                S.op("dve", lambda e, b=b: e.tensor_tensor(out=hs[b][:], in0=h0[b][:], in1=h1[b][:], op=ALU.add),
                                     reads=[h0_t[b], h1_t[b]], writes=[hs_t[b]])
                                S.op("dve", lambda e, b=b: e.tensor_tensor(out=hd[b][:], in0=h0[b][:], in1=h1[b][:], op=ALU.subtract),
                                     reads=[h0_t[b], h1_t[b]], writes=[hd_t[b]])
                            else:
                                S.op("pe", lambda e, b=b, o=o, cb=cb, tt=tt: e.matmul(p0[b][:], lhsT=hid[:, tt * 128:(tt + 1) * 128], rhs=wsd[:, o, 0, cb * 512:(cb + 1) * 512], start=True, stop=True),
                                     reads=[hid_t, wsd_t], writes=[p0_t[b]])
                                S.op("pe", lambda e, b=b, o=o, cb=cb, tt=tt: e.matmul(p1[b][:], lhsT=hid[:, tt * 128:(tt + 1) * 128], rhs=wsd[:, o, 1, cb * 512:(cb + 1) * 512], start=True, stop=True),
                                     reads=[hid_t, wsd_t], writes=[p1_t[b]])
                                S.op("dve", lambda e, b=b, db=db: e.tensor_tensor(out=hs[b][:], in0=p0[b][:], in1=dtile[db][:], op=ALU.mult),
                                     reads=[p0_t[b], dtile_t[db]], writes=[hs_t[b]])
                                S.op("dve", lambda e, b=b, db=db: e.tensor_tensor(out=hd[b][:], in0=p1[b][:], in1=dtile[db][:], op=ALU.mult),
                                     reads=[p1_t[b], dtile_t[db]], writes=[hd_t[b]])
                            S.dma("sp", hsD[o, tt * 128:(tt + 1) * 128, cb * 512:(cb + 1) * 512], hs[b][:], hs_t[b], reads=[hs_t[b]], writes=[self.tok(("hsD", o, tt, cb))])
                            S.dma("sp", hdD[o, tt * 128:(tt + 1) * 128, cb * 512:(cb + 1) * 512], hd[b][:], hd_t[b], reads=[hd_t[b]], writes=[self.tok(("hdD", o, tt, cb))])
                            it += 1
                S.barrier()

    def dft_dims(self):
        L = self.L
        NK = ((((2 * L - 1) + 3) // 4 + 1) + 127) // 128 * 128
        return NK, NK // 128, L // 2, L // 256

    def eo_rows(self, src2d, j, par):
        return src2d[j * 256:(j + 1) * 256, :].rearrange("(p two) c -> two p c", two=2)[par]

    def phase_filter_spectra(self, Fw, hsD, hdD, F2D, skipb):
        nc, S, L, NT = self.nc, self.S, self.L, self.NT
        NK, NKT, HA, KA = self.dft_dims()
        PW = 3 if NKT % 3 == 0 else (5 if NKT % 5 == 0 else 1)
        with ExitStack() as es:
            E = es.enter_context
            R = [[E(nc.sbuf_tensor(f"fsR{v}{i}", [128, KA, 512], BF16)) for i in range(4)] for v in range(2)]
            R_t = [[[T() for _ in range(KA)] for i in range(4)] for v in range(2)]
            sk = E(nc.sbuf_tensor("fsSk", [128, 2, 2048], F32))
            sk_t = T()
            for o in range(2):
                S.dma("sp", sk[:, o, :], skipb[o, :].partition_broadcast(128), sk_t, writes=[sk_t])
            lp = [[E(nc.sbuf_tensor(f"fsL{i}{v}", [128, KA, PW * 128], BF16)) for v in range(2)] for i in range(4)]
            lp_t = [[T() for v in range(2)] for i in range(4)]
            ps = [[E(nc.psum_tensor(f"fsP{i}{v}", [128, 512], F32)) for v in range(2)] for i in range(4)]
            ps_t = [[T() for v in range(2)] for i in range(4)]
            ue = [E(nc.sbuf_tensor(f"fsUe{v}", [128, 512], F32)) for v in range(2)]
            be = [E(nc.sbuf_tensor(f"fsBe{v}", [128, 512], F32)) for v in range(2)]
            ue_t = [T() for _ in range(2)]
            be_t = [T() for _ in range(2)]
            fo = [E(nc.sbuf_tensor(f"fsFo{v}", [128, 4, 512], BF16)) for v in range(2)]
            fo_t = [T() for _ in range(2)]
            npan = NKT // PW
            seq = [(o, cq) for o in range(2) for cq in range(4)]
            def loadR(si):
                o, cq = seq[si]
                v = si % 2
                for i in range(4):
                    src = hsD if i < 2 else hdD
                    for j in range(KA):
                        S.dma("sp", R[v][i][:, j, :], self.eo_rows(src[o], j, i % 2)[:, cq * 512:(cq + 1) * 512], R_t[v][i][j], writes=[R_t[v][i][j]])
            pcount = [0]
            def loadL(pi):
                v = pcount[0] % 2
                for i in range(4):
                    S.dma("sp", lp[i][v][:], pk(Fw[i], 0, KA, pi * PW * 128, PW * 128), lp_t[i][v], writes=[lp_t[i][v]])
                pcount[0] += 1
                return v
            loadR(0)
            nxt = loadL(0)
            it = 0
            for si, (o, cq) in enumerate(seq):
                if si + 1 < len(seq):
                    loadR(si + 1)
                rv = si % 2
                for pi in range(npan):
                    b = nxt
                    if pi + 1 < npan:
                        nxt = loadL(pi + 1)
                    elif si + 1 < len(seq):
                        nxt = loadL(0)
                    for mi in range(PW):
                        mt = pi * PW + mi
                        pb = it % 2
                        for i in range(4):
                            for kt in range(KA):
                                S.op("pe", lambda e, kt=kt, i=i, mi=mi, b=b, pb=pb, rv=rv: e.matmul(
                                    ps[i][pb][:], lhsT=lp[i][b][:, kt, mi * 128:(mi + 1) * 128], rhs=R[rv][i][:, kt, :],
                                    start=(kt == 0), stop=(kt == KA - 1)),
                                    reads=[lp_t[i][b], R_t[rv][i][kt]], writes=[ps_t[i][pb]], sig=(kt == KA - 1))
                        S.op("dve", lambda e, pb=pb, o=o, cq=cq: e.tensor_tensor(out=ue[pb][:], in0=ps[0][pb][:], in1=sk[:, o, cq * 512:(cq + 1) * 512], op=ALU.add),
                             reads=[ps_t[0][pb], sk_t], writes=[ue_t[pb]])
                        S.op("act", lambda e, pb=pb: e.activation(out=be[pb][:], in_=ps[2][pb][:], func=AF.Copy), reads=[ps_t[2][pb]], writes=[be_t[pb]])
                        for (oi, src_, src_t, pi_, op) in ((0, ue, ue_t, 1, ALU.add), (2, ue, ue_t, 1, ALU.subtract), (1, be, be_t, 3, ALU.add), (3, be, be_t, 3, ALU.subtract)):
                            S.op("dve", lambda e, oi=oi, src_=src_, pi_=pi_, op=op, pb=pb: e.tensor_tensor(out=fo[pb][:, oi, :], in0=src_[pb][:], in1=ps[pi_][pb][:], op=op),
                                 reads=[src_t[pb], ps_t[pi_][pb]], writes=[fo_t[pb]])
                        S.dma("sp", F2D[o, mt * 128:(mt + 1) * 128, :, cq * 512:(cq + 1) * 512], fo[pb][:], fo_t[pb], reads=[fo_t[pb]],
                              writes=[self.tok(("F2", o, mt, cq))])
                        it += 1
            S.barrier()

    def phase_conv_fwd(self, o, ztm, ztok, Fw, F2D, Y2D):
        nc, S, L, NT = self.nc, self.S, self.L, self.NT
        NK, NKT, HA, KA = self.dft_dims()
        PW = 3 if NKT % 3 == 0 else (5 if NKT % 5 == 0 else 1)
        with ExitStack() as es:
            E = es.enter_context
            R = [E(nc.sbuf_tensor(f"cfR{o}{q}", [128, KA, 2048], BF16)) for q in range(2)]
            R_t = [[T() for _ in range(KA)] for q in range(2)]
            for j in range(KA):
                for q in range(2):
                    S.dma("sp", R[q][:, j, :], self.eo_rows(ztm, j, q), R_t[q][j], reads=ztok(j), writes=[R_t[q][j]])
            lp = [[E(nc.sbuf_tensor(f"cfL{o}{i}{b}", [128, KA, PW * 128], BF16)) for b in range(2)] for i in range(4)]
            lp_t = [[T() for b in range(2)] for i in range(4)]
            ps = [[E(nc.psum_tensor(f"cfP{o}{i}{b}", [128, 512], F32)) for b in range(2)] for i in range(4)]
            ps_t = [[T() for b in range(2)] for i in range(4)]
            ft = [E(nc.sbuf_tensor(f"cfF{o}{b}", [128, 4, 512], BF16)) for b in range(2)]
            ft_t = [T() for _ in range(2)]
            names = ["ao", "bo", "pr", "qr", "pi", "qi", "m1", "m2", "m3", "m4", "m5", "m6", "m7", "m8", "d1", "d2", "e1", "e2"]
            tbs = [{n: E(nc.sbuf_tensor(f"cf_{n}{o}_{v}", [128, 512], F32)) for n in names} for v in range(2)]
            tts = [{n: T() for n in names} for v in range(2)]
            yo = [E(nc.sbuf_tensor(f"cfY{o}{b}", [128, 4, 512], BF16)) for b in range(2)]
            yo_t = [[T() for _ in range(4)] for b in range(2)]
            npan = NKT // PW
            def load(pi):
                for i in range(4):
                    S.dma("sp", lp[i][pi % 2][:], pk(Fw[i], 0, KA, pi * PW * 128, PW * 128), lp_t[i][pi % 2], writes=[lp_t[i][pi % 2]])
            load(0)
            pending = []
            def tt(op, out, a, b_, eng):
                S.op(eng, lambda e, tb=tb: e.tensor_tensor(out=tb[out][:], in0=tb[a][:], in1=tb[b_][:], op=op), reads=[tt_[a], tt_[b_]], writes=[tt_[out]])
            it = 0
            nit = NKT * 4
            def loadF(i):
                mt_, cq_ = i // 4, i % 4
                S.dma("sp", ft[i % 2][:], F2D[o, mt_ * 128:(mt_ + 1) * 128, :, cq_ * 512:(cq_ + 1) * 512], ft_t[i % 2], writes=[ft_t[i % 2]])
            loadF(0)
            for pi in range(npan):
                if pi + 1 < npan:
                    load(pi + 1)
                b = pi % 2
                for mi in range(PW):
                    mt = pi * PW + mi
                    for cq in range(4):
                        pb = it % 2
                        tb, tt_ = tbs[pb], tts[pb]
                        if it + 1 < nit:
                            loadF(it + 1)
                        for (pi_, mats) in ((0, ((0, 0), (1, 1))), (1, ((2, 0), (3, 1))), (2, ((0, 0),)), (3, ((2, 0),))):
                            nmm = len(mats) * KA
                            c_ = 0
                            for (i, q) in mats:
                                for kt in range(KA):
                                    S.op("pe", lambda e, kt=kt, i=i, q=q, mi=mi, b=b, pb=pb, cq=cq, pi_=pi_, c_=c_, nmm=nmm: e.matmul(
                                        ps[pi_][pb][:], lhsT=lp[i][b][:, kt, mi * 128:(mi + 1) * 128], rhs=R[q][:, kt, cq * 512:(cq + 1) * 512],
                                        start=(c_ == 0), stop=(c_ == nmm - 1)),
                                        reads=[lp_t[i][b], R_t[q][kt]], writes=[ps_t[pi_][pb]], sig=(c_ == nmm - 1))
                                    c_ += 1
                        S.op("act", lambda e, pb=pb, tb=tb: e.activation(out=tb["ao"][:], in_=ps[2][pb][:], func=AF.Copy, scale=2.0), reads=[ps_t[2][pb]], writes=[tt_["ao"]])
                        S.op("act", lambda e, pb=pb, tb=tb: e.activation(out=tb["bo"][:], in_=ps[3][pb][:], func=AF.Copy, scale=2.0), reads=[ps_t[3][pb]], writes=[tt_["bo"]])
                        S.op("dve", lambda e, pb=pb, tb=tb: e.tensor_tensor(out=tb["qr"][:], in0=tb["ao"][:], in1=ps[0][pb][:], op=ALU.subtract),
                             reads=[ps_t[0][pb], tt_["ao"]], writes=[tt_["qr"]])
                        S.op("dve", lambda e, pb=pb, tb=tb: e.tensor_tensor(out=tb["qi"][:], in0=tb["bo"][:], in1=ps[1][pb][:], op=ALU.subtract),
                             reads=[ps_t[1][pb], tt_["bo"]], writes=[tt_["qi"]])
                        for (out, pi_, fi) in (("m1", 0, 0), ("m2", 1, 1), ("m3", 0, 1), ("m4", 1, 0)):
                            S.op("dve", lambda e, out=out, pi_=pi_, fi=fi, pb=pb, tb=tb: e.tensor_tensor(out=tb[out][:], in0=ps[pi_][pb][:], in1=ft[pb][:, fi, :], op=ALU.mult),
                                 reads=[ps_t[pi_][pb], ft_t[pb]], writes=[tt_[out]])
                        for (out, a, fi) in (("m5", "qr", 2), ("m6", "qi", 3), ("m7", "qr", 3), ("m8", "qi", 2)):
                            S.op("dve", lambda e, out=out, a=a, fi=fi, pb=pb, tb=tb: e.tensor_tensor(out=tb[out][:], in0=tb[a][:], in1=ft[pb][:, fi, :], op=ALU.mult),
                                 reads=[tt_[a], ft_t[pb]], writes=[tt_[out]])
                        tt(ALU.subtract, "d1", "m1", "m2", "pool")
                        tt(ALU.add, "e1", "m3", "m4", "pool")
                        tt(ALU.subtract, "d2", "m5", "m6", "pool")
                        tt(ALU.add, "e2", "m7", "m8", "pool")
                        if pending:
                            pending.pop()()
                        def finish(specs, pb=pb, tb=tb, tt_=tt_, mt=mt, cq=cq):
                            for (oi, a, b_, op, eng) in specs:
                                S.op(eng, lambda e, oi=oi, a=a, b_=b_, op=op: e.tensor_tensor(out=yo[pb][:, oi, :], in0=tb[a][:], in1=tb[b_][:], op=op),
                                     reads=[tt_[a], tt_[b_]], writes=[yo_t[pb][oi]])
                                S.dma("sp", Y2D[oi, mt * 128:(mt + 1) * 128, cq * 512:(cq + 1) * 512], yo[pb][:, oi, :], yo_t[pb][oi], reads=[yo_t[pb][oi]],
                                      writes=[self.tok(("Y2", o, oi, mt, cq))])
                        finish(((0, "d1", "d2", ALU.add, "pool"), (2, "d1", "d2", ALU.subtract, "pool")))
                        pending.append(lambda f=finish: f(((1, "e1", "e2", ALU.add, "dve"), (3, "e1", "e2", ALU.subtract, "dve"))))
                        it += 1
            while pending:
                pending.pop()()
            S.barrier()

    def phase_conv_inv(self, o, Y2D, Iv, xgT, xg_row0, xg_tok, znT, zn_tok):
        nc, S, L, NT, NB = self.nc, self.S, self.L, self.NT, self.NB
        NK, NKT, HA, KA = self.dft_dims()
        KT = 4 * NKT
        NAB = HA // 512
        with ExitStack() as es:
            E = es.enter_context
            lp = [E(nc.sbuf_tensor(f"ciL{o}{i}", [128, KT, 512], BF16)) for i in range(2)]
            lp_t = [[T() for _ in range(4)] for i in range(2)]
            rp = [E(nc.sbuf_tensor(f"ciR{o}{i}", [128, KT, 512], BF16)) for i in range(2)]
            rp_t = [[T() for _ in range(4)] for i in range(2)]
            ps = [[E(nc.psum_tensor(f"ciP{o}{i}{q}", [128, 512], F32)) for q in range(2)] for i in range(4)]
            ps_t = [[T() for q in range(2)] for i in range(4)]
            xg = [E(nc.sbuf_tensor(f"ciX{o}{i}", [128, 1024], BF16)) for i in range(4)]
            xg_t = [T() for _ in range(4)]
            ob = [E(nc.sbuf_tensor(f"ciO{o}{i}", [128, 1024], BF16)) for i in range(4)]
            ob_t = [T() for _ in range(4)]
            rseq = [(cp_, ab) for cp_ in range(2) for ab in range(NAB)]
            def loadR(i):
                ab = rseq[i][1]
                rb = i % 2
                for part in range(4):
                    S.dma("sp", rp[rb][:, part * NKT:(part + 1) * NKT, :], pk(Iv[part], 0, NKT, ab * 512, 512), rp_t[rb][part], writes=[rp_t[rb][part]])
            it = 0
            ri = 0
            loadR(0)
            for cp_ in range(2):
                for j in range(2):
                    for part in range(4):
                        S.dma("sp", lp[j][:, part * NKT:(part + 1) * NKT, :], pk(Y2D[part], 0, NKT, (cp_ * 2 + j) * 512, 512), lp_t[j][part], writes=[lp_t[j][part]])
                for ab in range(NAB):
                    if ri + 1 < len(rseq):
                        loadR(ri + 1)
                    rb = ri % 2
                    ri += 1
                    for j in range(2):
                        for mi in range(4):
                            ct = (cp_ * 2 + j) * 4 + mi
                            pb = it % 4
                            S.dma("sp", xg[pb][:], xgT[xg_row0 + ct * 128:xg_row0 + (ct + 1) * 128, ab * 1024:(ab + 1) * 1024], xg_t[pb], reads=xg_tok(ct), writes=[xg_t[pb]])
                            for q in range(2):
                                for kk in range(2 * NKT):
                                    kt = q * 2 * NKT + kk
                                    S.op("pe", lambda e, kt=kt, kk=kk, mi=mi, j=j, rb=rb, pb=pb, q=q: e.matmul(
                                        ps[pb][q][:], lhsT=lp[j][:, kt, mi * 128:(mi + 1) * 128], rhs=rp[rb][:, kt, :],
                                        start=(kk == 0), stop=(kk == 2 * NKT - 1)),
                                        reads=[lp_t[j][kt // NKT], rp_t[rb][kt // NKT]], writes=[ps_t[pb][q]], sig=(kk == 2 * NKT - 1))
                            obv = ob[pb][:].rearrange("p (a two) -> p a two", two=2)
                            xgv = xg[pb][:].rearrange("p (a two) -> p a two", two=2)
                            for q in range(2):
                                S.op("dve", lambda e, pb=pb, q=q, obv=obv, xgv=xgv: e.tensor_tensor(out=obv[:, :, q], in0=ps[pb][q][:], in1=xgv[:, :, q], op=ALU.mult),
                                     reads=[ps_t[pb][q], xg_t[pb]], writes=[ob_t[pb]])
                            S.dma("sp", znT[ct * 128:(ct + 1) * 128, ab * 1024:(ab + 1) * 1024], ob[pb][:], ob_t[pb], reads=[ob_t[pb]], writes=zn_tok(ct, ab))
                            it += 1
            S.barrier()

    def phase_d(self, yT, yhT, gT, w_go, w_ho, mT):
        nc, S, L, NT, NB = self.nc, self.S, self.L, self.NT, self.NB
        HT = L // 2
        ytoks = [self.tok(("yT", rb)) for rb in range(NB)]
        yhtoks = [self.tok(("yhT", ct, ab)) for ct in range(16) for ab in range(max(1, L // 1024))]
        with ExitStack() as es:
            E = es.enter_context
            Rg = E(nc.sbuf_tensor("dRg", [128, 16, L], BF16))
            Rh = E(nc.sbuf_tensor("dRh", [128, 16, L], BF16))
            Rg_t = [[T() for _ in range(4)] for th in range(2)]
            Rh_t = [[T() for _ in range(4)] for th in range(2)]
            for th in range(2):
                for j in range(4):
                    S.dma("sp", Rg[:, j * 4:(j + 1) * 4, th * HT:(th + 1) * HT], pk(yT, j * 512, 4, th * HT, HT), Rg_t[th][j], reads=ytoks, writes=[Rg_t[th][j]])
                for j in range(4):
                    S.dma("sp", Rh[:, j * 4:(j + 1) * 4, th * HT:(th + 1) * HT], pk(yhT, j * 512, 4, th * HT, HT), Rh_t[th][j], reads=yhtoks, writes=[Rh_t[th][j]])
            wg = [E(nc.sbuf_tensor(f"dWg{i}", [128, 16, 256], BF16)) for i in range(2)]
            wh = [E(nc.sbuf_tensor(f"dWh{i}", [128, 16, 256], BF16)) for i in range(2)]
            wg_t = [T() for _ in range(2)]
            wh_t = [T() for _ in range(2)]
            pg = [E(nc.psum_tensor(f"dPg{i}", [128, HT], F32)) for i in range(2)]
            ph = [E(nc.psum_tensor(f"dPh{i}", [128, HT], F32)) for i in range(2)]
            pg_t = [T() for _ in range(2)]
            ph_t = [T() for _ in range(2)]
            g0 = [E(nc.sbuf_tensor(f"dG0{i}", [128, HT], BF16)) for i in range(2)]
            g1 = [E(nc.sbuf_tensor(f"dG1{i}", [128, HT], BF16)) for i in range(2)]
            g0_t = [T() for _ in range(2)]
            g1_t = [T() for _ in range(2)]
            ta = [E(nc.sbuf_tensor(f"dTa{i}", [128, HT], F32)) for i in range(2)]
            tb = [E(nc.sbuf_tensor(f"dTb{i}", [128, HT], F32)) for i in range(2)]
            ta_t = [T() for _ in range(2)]
            tb_t = [T() for _ in range(2)]
            ob = [E(nc.sbuf_tensor(f"dO{i}", [128, HT], BF16)) for i in range(2)]
            ob_t = [T() for _ in range(2)]
            def load(pi):
                S.dma("pool", wg[pi % 2][:], pk(w_go, 0, 16, pi * 256, 256), wg_t[pi % 2], writes=[wg_t[pi % 2]])
                S.dma("pool", wh[pi % 2][:], pk(w_ho, 0, 16, pi * 256, 256), wh_t[pi % 2], writes=[wh_t[pi % 2]])
                for kt in range(16):
                    S.op("act", lambda e, kt=kt, pi=pi: e.activation(out=wg[pi % 2][:, kt, :], in_=wg[pi % 2][:, kt, :], func=AF.Copy,
                                                                 scale=self.cp[:, COL_G + kt:COL_G + kt + 1]),
                         reads=[wg_t[pi % 2], self.cp_t], writes=[wg_t[pi % 2]])
            load(0)
            it = 0
            for pi in range(8):
                if pi + 1 < 8:
                    load(pi + 1)
                b = pi % 2
                for mi in range(2):
                    mt = pi * 2 + mi
                    for th in range(2):
                        pb = it % 2
                        S.dma("sp", g0[pb][:], gT[mt * 128:(mt + 1) * 128, th * HT:(th + 1) * HT], g0_t[pb], reads=[self.tok(("gT", mt))], writes=[g0_t[pb]])
                        S.dma("sp", g1[pb][:], gT[2048 + mt * 128:2048 + (mt + 1) * 128, th * HT:(th + 1) * HT], g1_t[pb], reads=[self.tok(("gT", 16 + mt))], writes=[g1_t[pb]])
                        for (pp, pp_t, ww, ww_t, RR, RR_t) in ((pg, pg_t, wg, wg_t, Rg, Rg_t), (ph, ph_t, wh, wh_t, Rh, Rh_t)):
                            for nb in range(HT // 512):
                                for kt in range(16):
                                    S.op("pe", lambda e, kt=kt, nb=nb, mi=mi, b=b, pb=pb, pp=pp, ww=ww, RR=RR, th=th: e.matmul(
                                        pp[pb][:, nb * 512:(nb + 1) * 512], lhsT=ww[b][:, kt, mi * 128:(mi + 1) * 128],
                                        rhs=RR[:, kt, th * HT + nb * 512:th * HT + (nb + 1) * 512],
                                        start=(kt == 0), stop=(kt == 15)),
                                        reads=[ww_t[b], RR_t[th][kt // 4]], writes=[pp_t[pb]], sig=(kt == 15 and nb == HT // 512 - 1))
                        S.op("dve", lambda e, pb=pb: e.tensor_tensor(out=ta[pb][:], in0=pg[pb][:], in1=g0[pb][:], op=ALU.mult),
                             reads=[pg_t[pb], g0_t[pb]], writes=[ta_t[pb]])
                        S.op("dve", lambda e, pb=pb: e.tensor_tensor(out=tb[pb][:], in0=ph[pb][:], in1=g1[pb][:], op=ALU.mult),
                             reads=[ph_t[pb], g1_t[pb]], writes=[tb_t[pb]])
                        S.op("dve", lambda e, pb=pb: e.tensor_tensor(out=ob[pb][:], in0=ta[pb][:], in1=tb[pb][:], op=ALU.add),
                             reads=[ta_t[pb], tb_t[pb]], writes=[ob_t[pb]])
                        S.dma("pool", mT[mt * 128:(mt + 1) * 128, th * HT:(th + 1) * HT], ob[pb][:], ob_t[pb], reads=[ob_t[pb]], writes=[self.tok(("mT", mt, th))])
                        it += 1
            S.barrier()

    def ln_stage1(self, res, res_t, add_ap, add_toks, s32, s32_t, st, st_t):
        S = self.S
        S.op("dve", lambda e: e.scalar_tensor_tensor(out=s32[:], in0=res[:], scalar=ALPHA, in1=add_ap, op0=ALU.mult, op1=ALU.add),
             reads=[res_t] + add_toks, writes=[s32_t])
        for j in range(4):
            S.op("dve", lambda e, j=j: e.bn_stats(out=st[:, j * 6:(j + 1) * 6], in_=s32[:, j * 512:(j + 1) * 512]), reads=[s32_t], writes=[st_t])
        S.op("dve", lambda e: e.bn_aggr(out=st[:, 24:26], in_=st[:, 0:24]), reads=[st_t], writes=[st_t])
        S.op("dve", lambda e: e.tensor_scalar(out=st[:, 26:27], in0=st[:, 25:26], scalar1=LN_EPS, scalar2=None, op0=ALU.add), reads=[st_t], writes=[st_t])
        S.op("dve", lambda e: e.tensor_scalar(out=st[:, 28:29], in0=st[:, 24:25], scalar1=-1.0, scalar2=None, op0=ALU.mult), reads=[st_t], writes=[st_t])
        S.op("act", lambda e: e.activation(out=st[:, 26:27], in_=st[:, 26:27], func=AF.Ln), reads=[st_t], writes=[st_t])
        S.op("act", lambda e: e.activation(out=st[:, 26:27], in_=st[:, 26:27], func=AF.Exp, scale=-0.5), reads=[st_t], writes=[st_t])
        S.op("act", lambda e: e.activation(out=st[:, 27:28], in_=st[:, 28:29], func=AF.Identity, scale=st[:, 26:27]), reads=[st_t], writes=[st_t])
        S.op("act", lambda e: e.activation(out=s32[:], in_=s32[:], func=AF.Identity, scale=st[:, 26:27], bias=st[:, 27:28]), reads=[s32_t, st_t], writes=[s32_t])

    def ln_stage2(self, s32, s32_t, lg, lb, lgb_t):
        S = self.S
        S.op("dve", lambda e: e.tensor_tensor(out=s32[:], in0=s32[:], in1=lg[:], op=ALU.mult), reads=[s32_t, lgb_t], writes=[s32_t])
        S.op("dve", lambda e: e.tensor_tensor(out=s32[:], in0=s32[:], in1=lb[:], op=ALU.add), reads=[s32_t, lgb_t], writes=[s32_t])

    def phase_e(self, mT, w_out, x, lnp, h1D, h1b):
        nc, S, L, NT, NB = self.nc, self.S, self.L, self.NT, self.NB
        with ExitStack() as es:
            E = es.enter_context
            R = E(nc.sbuf_tensor("eR", [128, 16, 2048], BF16))
            R_t = [T() for _ in range(4)]
            for j in range(4):
                S.dma("pool", R[:, j * 4:(j + 1) * 4, :], pk(w_out, j * 512, 4, 0, 2048), R_t[j], writes=[R_t[j]])
            lg = E(nc.sbuf_tensor("eLg", [128, 2048], F32))
            lb = E(nc.sbuf_tensor("eLb", [128, 2048], F32))
            lgb_t = T()
            S.dma("sp", lg[:], lnp[0, :].partition_broadcast(128), lgb_t, writes=[lgb_t])
            S.dma("sp", lb[:], lnp[1, :].partition_broadcast(128), lgb_t, writes=[lgb_t])
            lp = [E(nc.sbuf_tensor(f"eL{i}", [128, 16, 512], BF16)) for i in range(2)]
            lp_t = [T() for _ in range(2)]
            ps = [E(nc.psum_tensor(f"eP{i}", [128, 2048], F32)) for i in range(2)]
            ps_t = [T() for _ in range(2)]
            xin = [E(nc.sbuf_tensor(f"eX{i}", [128, 2048], F32)) for i in range(2)]
            xin_t = [T() for _ in range(2)]
            s32 = [E(nc.sbuf_tensor(f"eS{i}", [128, 2048], F32)) for i in range(3)]
            s32_t = [T() for _ in range(3)]
            st = [E(nc.sbuf_tensor(f"eSt{i}", [128, 32], F32)) for i in range(3)]
            st_t = [T() for _ in range(3)]
            hb = [E(nc.sbuf_tensor(f"eHb{i}", [128, 2048], BF16)) for i in range(2)]
            hb_t = [T() for _ in range(2)]
            mtoks = lambda nb: [self.tok(("mT", mt, (nb * 512) // (L // 2))) for mt in range(16)]
            def load(nb):
                S.dma("sp", lp[nb % 2][:], pk(mT, 0, 16, nb * 512, 512), lp_t[nb % 2], reads=mtoks(nb), writes=[lp_t[nb % 2]])
            load(0)
            NS = 3
            def stage2(tt):
                sb_ = tt % NS
                self.ln_stage2(s32[sb_], s32_t[sb_], lg, lb, lgb_t)
                S.dma("pool", h1D[tt * 128:(tt + 1) * 128, :], s32[sb_][:], s32_t[sb_], reads=[s32_t[sb_]], writes=[self.tok(("h1D", tt))])
                S.op("act", lambda e, sb_=sb_: e.activation(out=hb[tt % 2][:], in_=s32[sb_][:], func=AF.Copy), reads=[s32_t[sb_]], writes=[hb_t[tt % 2]])
                S.dma("pool", h1b[tt * 128:(tt + 1) * 128, :], hb[tt % 2][:], hb_t[tt % 2], reads=[hb_t[tt % 2]], writes=[self.tok(("h1b", tt))])
            for nb in range(NB):
                if nb + 1 < NB:
                    load(nb + 1)
                for ti in range(4):
                    tt = nb * 4 + ti
                    pb = tt % 2
                    sb_ = tt % NS
                    S.dma("sp", xin[pb][:], x[tt * 128:(tt + 1) * 128, :], xin_t[pb], writes=[xin_t[pb]])
                    for db in range(4):
                        for kt in range(16):
                            S.op("pe", lambda e, kt=kt, db=db, ti=ti, nb=nb, pb=pb: e.matmul(
                                ps[pb][:, db * 512:(db + 1) * 512], lhsT=lp[nb % 2][:, kt, ti * 128:(ti + 1) * 128], rhs=R[:, kt, db * 512:(db + 1) * 512],
                                start=(kt == 0), stop=(kt == 15)),
                                reads=[lp_t[nb % 2], R_t[kt // 4]], writes=[ps_t[pb]], sig=(kt == 15 and db == 3))
                    self.ln_stage1(xin[pb], xin_t[pb], ps[pb][:], [ps_t[pb]], s32[sb_], s32_t[sb_], st[sb_], st_t[sb_])
                    if tt >= 1:
                        stage2(tt - 1)
            stage2(NT - 1)
            S.barrier()

    def phase_f(self, h1T, w_ff1, uT):
        nc, S, L, NT, NB = self.nc, self.S, self.L, self.NT, self.NB
        with ExitStack() as es:
            E = es.enter_context
            R = E(nc.sbuf_tensor("fR", [128, 16, L], BF16))
            R_t = [T() for _ in range(4)]
            for j in range(4):
                S.dma("sp", R[:, j * 4:(j + 1) * 4, :], pk(h1T, j * 512, 4, 0, L), R_t[j], reads=[self.tok(("h1T", rb)) for rb in range(NB)], writes=[R_t[j]])
            wp = [E(nc.sbuf_tensor(f"fW{i}", [128, 16, 256], BF16)) for i in range(3)]
            wp_t = [T() for _ in range(3)]
            ps = [E(nc.psum_tensor(f"fP{i}", [128, L], F32)) for i in range(2)]
            ps_t = [T() for _ in range(2)]
            rl = [E(nc.sbuf_tensor(f"fRl{i}", [128, L], F32)) for i in range(2)]
            rl_t = [T() for _ in range(2)]
            ob = [E(nc.sbuf_tensor(f"fO{i}", [128, L], BF16)) for i in range(2)]
            ob_t = [T() for _ in range(2)]
            npan = DFF // 256
            def load(pi):
                S.dma("pool", wp[pi % 3][:], pk(w_ff1, 0, 16, pi * 256, 256), wp_t[pi % 3], writes=[wp_t[pi % 3]])
            load(0)
            load(1)
            for pi in range(npan):
                if pi + 2 < npan:
                    load(pi + 2)
                b = pi % 3
                for mi in range(2):
                    mt = pi * 2 + mi
                    pb = mt % 2
                    for nb in range(NB):
                        for kt in range(16):
                            S.op("pe", lambda e, kt=kt, nb=nb, mi=mi, b=b, pb=pb: e.matmul(
                                ps[pb][:, nb * 512:(nb + 1) * 512], lhsT=wp[b][:, kt, mi * 128:(mi + 1) * 128], rhs=R[:, kt, nb * 512:(nb + 1) * 512],
                                start=(kt == 0), stop=(kt == 15)),
                                reads=[wp_t[b], R_t[kt // 4]], writes=[ps_t[pb]], sig=(kt == 15 and nb == NB - 1))
                    S.op("act", lambda e, pb=pb: e.activation(out=rl[pb][:], in_=ps[pb][:], func=AF.Relu), reads=[ps_t[pb]], writes=[rl_t[pb]])
                    S.op("dve", lambda e, pb=pb: e.tensor_tensor(out=ob[pb][:], in0=rl[pb][:], in1=rl[pb][:], op=ALU.mult), reads=[rl_t[pb]], writes=[ob_t[pb]])
                    S.dma("sp", uT[mt * 128:(mt + 1) * 128, :], ob[pb][:], ob_t[pb], reads=[ob_t[pb]], writes=[self.tok(("uT", mt))])
            S.barrier()

    def phase_g(self, uT, w_ff2, ffT):
        nc, S, L, NT, NB = self.nc, self.S, self.L, self.NT, self.NB
        HT = L // 2
        NBH = HT // 512
        with ExitStack() as es:
            E = es.enter_context
            lp = [E(nc.sbuf_tensor(f"gL{i}", [128, 16, 512], BF16)) for i in range(4)]
            lp_t = [T() for _ in range(4)]
            rp = [E(nc.sbuf_tensor(f"gR{i}", [128, 16, 512], BF16)) for i in range(3)]
            rp_t = [T() for _ in range(3)]
            ps = [E(nc.psum_tensor(f"gP{i}", [128, HT], F32)) for i in range(4)]
            ps_t = [T() for _ in range(4)]
            ob = [E(nc.sbuf_tensor(f"gO{i}", [128, HT], F32)) for i in range(4)]
            ob_t = [T() for _ in range(4)]
            utoks = lambda kc: [self.tok(("uT", kc * 16 + j)) for j in range(16)]
            def loadL(mg, kc):
                S.dma("pool", lp[kc][:], pk(w_ff2, kc * 2048, 16, mg * 512, 512), lp_t[kc], writes=[lp_t[kc]])
            rseq = [(mg, th, kc, nb) for mg in range(4) for th in range(2) for kc in range(4) for nb in range(NBH)]
            def loadR(i):
                mg, th, kc, nb = rseq[i]
                S.dma("sp", rp[i % 3][:], pk(uT, kc * 2048, 16, th * HT + nb * 512, 512), rp_t[i % 3], reads=utoks(kc), writes=[rp_t[i % 3]])
            for kc in range(4):
                loadL(0, kc)
            loadR(0)
            loadR(1)
            ri = 0
            for mg in range(4):
                for th in range(2):
                    for kc in range(4):
                        for nb in range(NBH):
                            if ri + 2 < len(rseq):
                                loadR(ri + 2)
                            rb = ri % 3
                            for mi in range(4):
                                for kt in range(16):
                                    lastk = (kc == 3 and kt == 15)
                                    S.op("pe", lambda e, kt=kt, nb=nb, mi=mi, kc=kc, rb=rb, lastk=lastk: e.matmul(
                                        ps[mi][:, nb * 512:(nb + 1) * 512], lhsT=lp[kc][:, kt, mi * 128:(mi + 1) * 128], rhs=rp[rb][:, kt, :],
                                        start=(kc == 0 and kt == 0), stop=lastk),
                                        reads=[lp_t[kc], rp_t[rb]], writes=[ps_t[mi]], sig=(kt == 15))
                            ri += 1
                        if th == 1 and mg + 1 < 4:
                            loadL(mg + 1, kc)
                    for mi in range(4):
                        mt = mg * 4 + mi
                        if mi % 2 == 0:
                            S.op("act", lambda e, mi=mi: e.activation(out=ob[mi][:], in_=ps[mi][:], func=AF.Copy), reads=[ps_t[mi]], writes=[ob_t[mi]])
                        else:
                            S.op("dve", lambda e, mi=mi: e.tensor_copy(out=ob[mi][:], in_=ps[mi][:]), reads=[ps_t[mi]], writes=[ob_t[mi]])
                        S.dma("sp", ffT[mt * 128:(mt + 1) * 128, th * HT:(th + 1) * HT], ob[mi][:], ob_t[mi], reads=[ob_t[mi]], writes=[self.tok(("ffT", mt, th))])
            S.barrier()

    def phase_h(self, ffT, h1D, lnp2, out, idf, idf_t):
        nc, S, L, NT, NB = self.nc, self.S, self.L, self.NT, self.NB
        with ExitStack() as es:
            E = es.enter_context
            lg = E(nc.sbuf_tensor("hLg", [128, 2048], F32))
            lb = E(nc.sbuf_tensor("hLb", [128, 2048], F32))
            lgb_t = T()
            S.dma("sp", lg[:], lnp2[0, :].partition_broadcast(128), lgb_t, writes=[lgb_t])
            S.dma("sp", lb[:], lnp2[1, :].partition_broadcast(128), lgb_t, writes=[lgb_t])
            hin = [E(nc.sbuf_tensor(f"hH{i}", [128, 2048], F32)) for i in range(2)]
            hin_t = [T() for _ in range(2)]
            fin = [E(nc.sbuf_tensor(f"hF{i}", [128, 16, 512], F32)) for i in range(2)]
            fin_t = [T() for _ in range(2)]
            ps = [E(nc.psum_tensor(f"hP{i}", [128, 2048], F32)) for i in range(2)]
            ps_t = [T() for _ in range(2)]
            NS = 3
            s32 = [E(nc.sbuf_tensor(f"hS{i}", [128, 2048], F32)) for i in range(NS)]
            s32_t = [T() for _ in range(NS)]
            st = [E(nc.sbuf_tensor(f"hSt{i}", [128, 32], F32)) for i in range(NS)]
            st_t = [T() for _ in range(NS)]
            def stage2(tt):
                sb_ = tt % NS
                self.ln_stage2(s32[sb_], s32_t[sb_], lg, lb, lgb_t)
                S.dma("pool", out[tt * 128:(tt + 1) * 128, :], s32[sb_][:], s32_t[sb_], reads=[s32_t[sb_]], writes=[self.tok(("out", tt))])
            for tt in range(NT):
                pb = tt % 2
                sb_ = tt % NS
                th = (tt * 128) // (L // 2)
                if tt == 0:
                    S.dma("sp", hin[0][:], h1D[0:128, :], hin_t[0], reads=[self.tok(("h1D", 0))], writes=[hin_t[0]])
                if tt + 1 < NT:
                    S.dma("sp", hin[1 - pb][:], h1D[(tt + 1) * 128:(tt + 2) * 128, :], hin_t[1 - pb], reads=[self.tok(("h1D", tt + 1))], writes=[hin_t[1 - pb]])
                fb = (tt // 4) % 2
                ti = tt % 4
                if ti == 0:
                    S.dma("sp", fin[fb][:], ffT[:, tt * 128:tt * 128 + 512].rearrange("(k p) t -> p k t", p=128), fin_t[fb],
                          reads=[self.tok(("ffT", mt, th)) for mt in range(16)], writes=[fin_t[fb]])
                for dt in range(16):
                    S.op("pe", lambda e, dt=dt, pb=pb, fb=fb, ti=ti: e.transpose(ps[pb][:, dt * 128:(dt + 1) * 128], fin[fb][:, dt, ti * 128:(ti + 1) * 128], idf[:]),
                         reads=[fin_t[fb], idf_t], writes=[ps_t[pb]], sig=(dt == 15))
                self.ln_stage1(hin[pb], hin_t[pb], ps[pb][:], [ps_t[pb]], s32[sb_], s32_t[sb_], st[sb_], st_t[sb_])
                if tt >= 1:
                    stage2(tt - 1)
            stage2(NT - 1)
            S.barrier()


COL_CONV = 0
COL_BA = 192
COL_G = 208
NCOL = 224


def host_layout(inputs, L):
    w_in = np.asarray(inputs["w_in"][0], dtype=np.float32)
    afab = np.zeros((D, 128), np.float32)
    afab[:, 0:16] = w_in[:, 6144:6160]
    afab[:, 32:48] = w_in[:, 6160:6176]
    w_fm = np.ascontiguousarray(np.concatenate([w_in[:, 0:2048], w_in[:, 6176:16416], afab], axis=1))
    w_tm = np.ascontiguousarray(w_in[:, 2048:6144])
    colp = np.zeros((128, NCOL), np.float32)
    cw = np.asarray(inputs["hy_conv_w"][0], np.float32)
    cb = np.asarray(inputs["hy_conv_b"][0], np.float32)
    for ct in range(48):
        for j in range(3):
            colp[:, COL_CONV + ct * 4 + j] = cw[j, ct * 128:(ct + 1) * 128]
        colp[:, COL_CONV + ct * 4 + 3] = cb[ct * 128:(ct + 1) * 128]
    baf = np.asarray(inputs["gla_ba_f"][0], np.float32)
    bab = np.asarray(inputs["gla_ba_b"][0], np.float32)
    for dt in range(8):
        colp[:, COL_BA + dt] = baf[dt * 128:(dt + 1) * 128]
        colp[:, COL_BA + 8 + dt] = bab[dt * 128:(dt + 1) * 128]
    gvec = np.asarray(inputs["gla_norm_g"], np.float32).reshape(2048)
    for et in range(16):
        colp[:, COL_G + et] = gvec[et * 128:(et + 1) * 128]
    wa2p = np.zeros((2, 64, 1024), np.float32)
    wa2p[0, 0:16] = np.asarray(inputs["gla_wa2_f"][0], np.float32)
    wa2p[1, 32:48] = np.asarray(inputs["gla_wa2_b"][0], np.float32)
    gng = np.asarray(inputs["gla_norm_g"], np.float32).reshape(1, 2048)
    jj = np.arange(128)[:, None]
    ii = np.arange(128)[None, :]
    mf = np.where(ii >= jj, 1.0, 0.0).astype(np.float32)
    mb = np.where(jj > ii, 1.0, 0.0).astype(np.float32)
    masks = np.stack([np.tile(mf, (1, 4)), np.tile(mb, (1, 4))], axis=1)
    shared = dict(w_fm=w_fm, w_tm=w_tm, ident=np.eye(128, dtype=np.float32), colp=colp,
                  wa2p=wa2p, gng=gng, masks=np.ascontiguousarray(masks))
    NT = L // 128
    mlpw = np.zeros((64, 3, 64), np.float32)
    mlpw[0:EMB, 0, :] = np.asarray(inputs["hy_w1"][0], np.float32)
    mlpw[:, 1, :] = np.asarray(inputs["hy_w2"][0], np.float32)
    mlpw[:, 2, :] = np.asarray(inputs["hy_w3"][0], np.float32)
    mlpc = np.stack([np.asarray(inputs["hy_b1"][0], np.float32), np.asarray(inputs["hy_b2"][0], np.float32),
                     np.asarray(inputs["hy_b3"][0], np.float32), np.asarray(inputs["hy_freq"][0], np.float32)], axis=1)
    w4aug = np.concatenate([np.asarray(inputs["hy_w4"][0], np.float32), np.asarray(inputs["hy_b4"], np.float32).reshape(1, 8192)], axis=0)
    shared.update(mlpw=mlpw, mlpc=np.ascontiguousarray(mlpc), w4aug=np.ascontiguousarray(w4aug),
                  skipb=np.ascontiguousarray(np.asarray(inputs["hy_skip"][0], np.float32)))
    shared.update(const_tables(L))
    f32c = lambda a: np.ascontiguousarray(np.asarray(a, np.float32))
    shared.update(w_go=f32c(inputs["w_gla_o"][0]), w_ho=f32c(inputs["w_hy_o"][0]), w_out=f32c(inputs["w_out"][0]),
                  w_ff1=f32c(inputs["w_ff1"][0]), w_ff2=f32c(inputs["w_ff2"][0]),
                  lnp1=f32c(np.stack([inputs["ln1_g"][0], inputs["ln1_b"][0]])), lnp2=f32c(np.stack([inputs["ln2_g"][0], inputs["ln2_b"][0]])))
    return shared


_CONST = {}


def const_tables(L):
    if L in _CONST:
        return _CONST[L]
    NT = L // 128
    f32 = np.float32
    t = np.linspace(0.0, 1.0, L, dtype=f32)
    bands = (EMB - 1) // 2
    f = np.linspace(1e-4, bands - 1, bands, dtype=f32)
    wpos = (2.0 * math.pi * np.arange(L, dtype=f32) / L).astype(f32)
    ang = wpos[:, None] * f[None, :]
    emb = np.concatenate([t[:, None], np.cos(ang), -np.sin(ang)], axis=-1).astype(f32)
    embT = np.zeros((64, L), f32)
    embT[0:EMB] = emb.T
    min_decay = math.log(1e-2) / 1.5
    max_decay = math.log(1e-2) / 0.3
    deltas = np.abs(np.linspace(min_decay, max_decay, HYW, dtype=f32)).astype(f32).reshape(1, HYW)
    negt = np.ascontiguousarray((-t).reshape(NT, 128).T.astype(f32))
    NK = ((((2 * L - 1) + 3) // 4 + 1) + 127) // 128 * 128
    N = 4 * (NK - 1)
    HA = L // 2
    a = np.arange(HA, dtype=np.int64)[:, None]
    k = np.arange(NK, dtype=np.int64)[None, :]
    th = 2.0 * math.pi / N
    pe = ((2 * a * k) % N).astype(np.float64) * th
    po = (((2 * a + 1) * k) % N).astype(np.float64) * th
    Fw64 = np.stack([np.cos(pe), np.cos(po), -np.sin(pe), -np.sin(po)])
    sk = np.full((NK,), 2.0 / N)
    sk[0] = 1.0 / N
    sk[NK - 1] = 1.0 / N
    Iv64 = np.stack([Fw64[0].T, Fw64[2].T, Fw64[1].T, Fw64[3].T]) * sk[None, :, None]
    Fw = np.ascontiguousarray(Fw64.astype(ml_dtypes.bfloat16))
    Iv = np.ascontiguousarray(Iv64.astype(ml_dtypes.bfloat16))
    _CONST[L] = dict(embT=embT, deltas=deltas, negt=negt, Fw=Fw, Iv=Iv)
    return _CONST[L]


_CACHE = {}


def kernel(**inputs):
    L = inputs["x"].shape[1]
    B = inputs["x"].shape[0]
    shared = host_layout(inputs, L)
    if L not in _CACHE:
        p = Prog(L=L)
        p.build()
        _CACHE[L] = p
    p = _CACHE[L]
    in_maps = []
    for b in range(B):
        m = dict(shared)
        m["x"] = np.ascontiguousarray(np.asarray(inputs["x"][b], np.float32))
        in_maps.append(m)
    res = run_bass_kernel_spmd(p.nc, in_maps, core_ids=list(range(B)))
    return np.stack([r["out"] for r in res.results], axis=0).astype(np.float32)
```

```python
import math
from contextlib import ExitStack

import numpy as np
import ml_dtypes

import concourse.bass as bass
import concourse.mybir as mybir
from concourse.bass_utils import run_bass_kernel_spmd

F32 = mybir.dt.float32
BF16 = mybir.dt.bfloat16
AF = mybir.ActivationFunctionType
ALU = mybir.AluOpType
AX = mybir.AxisListType

D = 2048
NCORES = 8
GLA_H = 4
DK = 256
DV = 512
KEYW = 1024
RANK = 16
TAU = 16.0
HYW = 2048
HID = 64
EMB = 33
DFF = 8192
LN_EPS = 1e-5
ALPHA = 2.0 ** 0.25
CH = 128


class Tok:
    __slots__ = ("w", "r", "name", "dsem", "dcnt", "did")

    def __init__(self, name=""):
        self.w = {}
        self.r = {}
        self.name = name
        self.dsem = None
        self.dcnt = 0
        self.did = None


class Sched:
    def __init__(self, nc, es, n_dma_sems=96):
        self.nc = nc
        self.eng = dict(pe=nc.tensor, act=nc.scalar, dve=nc.vector, pool=nc.gpsimd, sp=nc.sync)
        self.semobj = {}
        self.cnt = {}
        for k in ("pe", "act", "dve", "pool"):
            self.semobj[k] = es.enter_context(nc.semaphore("sem_" + k))
            self.cnt[k] = 0
        self.seen = {k: {} for k in self.eng}
        self.free_dsems = []
        for i in range(n_dma_sems):
            sem = es.enter_context(nc.semaphore(f"dsem{i}"))
            self.semobj[f"d{i}"] = sem
            self.free_dsems.append((sem, f"d{i}", 0))
        self.active = []
        self.pe_pending = False
        self.n_inst = {k: 0 for k in self.eng}
        self.n_wait = 0

    def _waits(self, e, reads, writes):
        need = {}
        for t in reads:
            for k, c in t.w.items():
                if need.get(k, 0) < c:
                    need[k] = c
        for t in writes:
            for k, c in t.w.items():
                if k == e:
                    continue
                if need.get(k, 0) < c:
                    need[k] = c
            for k, c in t.r.items():
                if k == e:
                    continue
                if need.get(k, 0) < c:
                    need[k] = c
        seen = self.seen[e]
        for k, c in need.items():
            if e == "pe" and k == "pe":
                continue
            if seen.get(k, 0) >= c:
                continue
            self.eng[e].wait_ge(self.semobj[k], c)
            self.n_wait += 1
            seen[k] = c

    def op(self, e, fn, reads=(), writes=(), sig=True):
        self._waits(e, reads, writes)
        ins = fn(self.eng[e])
        self.n_inst[e] += 1
        if sig:
            self.cnt[e] += 1
            ins.then_inc(self.semobj[e], 1)
            c = self.cnt[e]
            if e == "pe":
                self.pe_pending = False
        else:
            assert e == "pe"
            c = self.cnt[e] + 1
            self.pe_pending = True
        for t in reads:
            if t.r.get(e, 0) < c:
                t.r[e] = c
        for t in writes:
            t.w = {e: c}
            t.r = {}
        return ins

    def dma(self, q, out, in_, sb, reads=(), writes=()):
        self._waits(q, reads, writes)
        if sb.dsem is None:
            sb.dsem, sb.did, sb.dcnt = self.free_dsems.pop()
            self.active.append(sb)
        ins = self.eng[q].dma_start(out=out, in_=in_)
        self.n_inst[q] += 1
        sb.dcnt += 16
        ins.then_inc(sb.dsem, 16)
        k, c = sb.did, sb.dcnt
        for t in reads:
            if t.r.get(k, 0) < c:
                t.r[k] = c
        for t in writes:
            t.w = {k: c}
            t.r = {}
        return ins

    def barrier(self):
        assert not self.pe_pending, "PE has unsignaled instructions at barrier"
        targets = {k: self.cnt[k] for k in ("pe", "act", "dve", "pool")}
        for t in self.active:
            targets[t.did] = t.dcnt
        for e in self.eng:
            seen = self.seen[e]
            for k, c in targets.items():
                if c > seen.get(k, 0):
                    self.eng[e].wait_ge(self.semobj[k], c)
                    self.n_wait += 1
                    seen[k] = c
        for t in self.active:
            self.free_dsems.append((t.dsem, t.did, t.dcnt))
            t.dsem = None
        self.active = []


def T(name=""):
    return Tok(name)


def pk(dram, r0, kt, c0, w):
    return dram[r0:r0 + kt * 128, c0:c0 + w].rearrange("(k p) w -> p k w", p=128)


class Prog:
    def __init__(self, L=2048, debug=False, upto=99):
        self.L = L
        self.NT = L // 128
        self.NB = L // 512
        self.debug = debug
        self.upto = upto
        self.nc = bass.Bass("TRN2", target_bir_lowering=False)
        self.ins = {}
        self.scr = {}
        self.toks = {}

    def inp(self, name, shape, dt=F32):
        t = self.nc.dram_tensor(name, list(shape), dt, kind="ExternalInput").ap()
        self.ins[name] = t
        return t

    def scratch(self, name, shape, dt):
        kind = "ExternalOutput" if self.debug else "Internal"
        t = self.nc.dram_tensor(name, list(shape), dt, kind=kind).ap()
        self.scr[name] = t
        return t

    def tok(self, key):
        if key not in self.toks:
            self.toks[key] = Tok(str(key))
        return self.toks[key]

    def build(self):
        nc, L, NT, NB = self.nc, self.L, self.NT, self.NB
        x = self.inp("x", [L, D])
        w_fm = self.inp("w_fm", [D, 97 * 128])
        w_tm = self.inp("w_tm", [D, 4096])
        ident = self.inp("ident", [128, 128])
        colp = self.inp("colp", [128, NCOL])
        out = self.nc.dram_tensor("out", [L, D], F32, kind="ExternalOutput").ap()
        self.out = out
        qkT = self.scratch("qkT", [2048, L], BF16)
        ucT = self.scratch("ucT", [6144, L], BF16)
        gT = self.scratch("gT", [4096, L], BF16)
        afbT = self.scratch("afbT", [128, L], F32)
        vtm = self.scratch("vtm", [L, 2048], BF16)
        srtm = self.scratch("srtm", [L, 2048], BF16)
        wa2p = self.inp("wa2p", [2, 64, 1024])
        gng = self.inp("gng", [1, 2048])
        masks = self.inp("masks", [128, 2, 512])
        ofD = self.scratch("ofD", [L, 2048], F32)
        ytm = self.scratch("ytm", [L, 2048], BF16)
        yT = self.scratch("yT", [2048, L], BF16)
        embT = self.inp("embT", [64, L])
        mlpw = self.inp("mlpw", [64, 3, 64])
        mlpc = self.inp("mlpc", [64, 4])
        w4aug = self.inp("w4aug", [65, 8192])
        negt = self.inp("negt", [128, NT])
        deltas = self.inp("deltas", [1, 2048])
        skipb = self.inp("skipb", [2, 2048])
        NK_, NKT_, HA_, KA_ = self.dft_dims()
        Fw = self.inp("Fw", [4, HA_, NK_], BF16)
        Iv = self.inp("Iv", [4, NK_, HA_], BF16)
        hsD = self.scratch("hsD", [2, L, 2048], BF16)
        hdD = self.scratch("hdD", [2, L, 2048], BF16)
        F2D = self.scratch("F2D", [2, NK_, 4, 2048], BF16)
        Y2D = self.scratch("Y2D", [4, NK_, 2048], BF16)
        vhtm = self.scratch("vhtm", [L, 2048], BF16)
        z1T = self.scratch("z1T", [2048, L], BF16)
        z1tm = self.scratch("z1tm", [L, 2048], BF16)
        yhT = self.scratch("yhT", [2048, L], BF16)
        w_go = self.inp("w_go", [2048, 2048])
        w_ho = self.inp("w_ho", [2048, 2048])
        w_out = self.inp("w_out", [2048, 2048])
        w_ff1 = self.inp("w_ff1", [2048, DFF])
        w_ff2 = self.inp("w_ff2", [DFF, 2048])
        lnp1 = self.inp("lnp1", [2, 2048])
        lnp2 = self.inp("lnp2", [2, 2048])
        mT = self.scratch("mT", [2048, L], BF16)
        h1D = self.scratch("h1D", [L, 2048], F32)
        h1b = self.scratch("h1b", [L, 2048], BF16)
        h1T = self.scratch("h1T", [2048, L], BF16)
        uT = self.scratch("uT", [DFF, L], BF16)
        ffT = self.scratch("ffT", [2048, L], F32)
        ffD = self.scratch("ffD", [L, 2048], F32)

        with ExitStack() as es:
            S = Sched(nc, es)
            self.S = S
            E = es.enter_context
            cp = E(nc.sbuf_tensor("colp_sb", [128, NCOL], F32))
            cp_t = T("colp")
            S.dma("sp", cp[:], colp[:, :], cp_t, writes=[cp_t])
            idb = E(nc.sbuf_tensor("ident_sb", [128, 128], BF16))
            idb_t = T("ident")
            S.dma("pool", idb[:], ident[:, :], idb_t, writes=[idb_t])
            self.cp, self.cp_t, self.idb, self.idb_t = cp, cp_t, idb, idb_t
            idf = E(nc.sbuf_tensor("identf_sb", [128, 128], F32))
            idf_t = T("identf")
            S.dma("sp", idf[:], ident[:, :], idf_t, writes=[idf_t])
            self.idf, self.idf_t = idf, idf_t

            self.phase_a(x, w_fm, w_tm, qkT, ucT, gT, afbT, vtm, srtm)
            if self.upto >= 2:
                self.phase_b(qkT, vtm, afbT, srtm, ofD, ytm, wa2p, gng, masks)
                self.transpose_dram(ytm, yT, L, 2048, lambda rt: [self.tok(("ytm", rt))],
                                    lambda rb: [self.tok(("yT", rb))], "trY")
            if self.upto >= 3:
                self.phase_c_filters(embT, mlpw, mlpc, w4aug, negt, deltas, hsD, hdD)
                self.phase_filter_spectra(Fw, hsD, hdD, F2D, skipb)
                self.transpose_dram(ucT[0:2048, :], vhtm, 2048, L, lambda rt: [self.tok(("ucT", rt))],
                                    lambda rb: [self.tok(("vhtm", rb))], "trV")
                self.phase_conv_fwd(0, vhtm, lambda j: [self.tok(("vhtm", rb)) for rb in range(4)], Fw, F2D, Y2D)
                self.phase_conv_inv(0, Y2D, Iv, ucT, 2048, lambda ct: [self.tok(("ucT", 16 + ct))], z1T,
                                    lambda ct, ab: [self.tok(("z1T", ct, ab))])
                self.transpose_dram(z1T, z1tm, 2048, L, lambda rt: [self.tok(("z1T", rt, ab)) for ab in range(max(1, L // 1024))],
                                    lambda rb: [self.tok(("z1tm", rb))], "trZ")
                self.phase_conv_fwd(1, z1tm, lambda j: [self.tok(("z1tm", rb)) for rb in range(4)], Fw, F2D, Y2D)
                self.phase_conv_inv(1, Y2D, Iv, ucT, 4096, lambda ct: [self.tok(("ucT", 32 + ct))], yhT,
                                    lambda ct, ab: [self.tok(("yhT", ct, ab))])
            if self.upto >= 4:
                self.phase_d(yT, yhT, gT, w_go, w_ho, mT)
                self.phase_e(mT, w_out, x, lnp1, h1D, h1b)
                self.transpose_dram(h1b, h1T, L, 2048, lambda rt: [self.tok(("h1b", rt))], lambda rb: [self.tok(("h1T", rb))], "trH")
                self.phase_f(h1T, w_ff1, uT)
                self.phase_g(uT, w_ff2, ffT)
                self.phase_h(ffT, h1D, lnp2, out, idf, idf_t)

            S.barrier()
        return nc

    def phase_a(self, x, w_fm, w_tm, qkT, ucT, gT, afbT, vtm, srtm):
        nc, S, L, NT, NB = self.nc, self.S, self.L, self.NT, self.NB
        cp, cp_t = self.cp, self.cp_t
        with ExitStack() as es:
            E = es.enter_context
            xT = E(nc.sbuf_tensor("xT", [128, 16, L], BF16))
            xT_t = [T(f"xT{i}") for i in range(NT)]
            with ExitStack() as es0:
                E0 = es0.enter_context
                xb = [E0(nc.sbuf_tensor(f"xb{i}", [128, D], BF16)) for i in range(2)]
                xb_t = [T(f"xb{i}") for i in range(2)]
                pt = [E0(nc.psum_tensor(f"pt{i}", [128, D], BF16)) for i in range(2)]
                pt_t = [T(f"pt{i}") for i in range(2)]
                for tt in range(NT):
                    b = tt % 2
                    S.dma("pool", xb[b][:], x[tt * 128:(tt + 1) * 128, :], xb_t[b], writes=[xb_t[b]])
                    for dt in range(16):
                        S.op("pe", lambda e, dt=dt, b=b: e.transpose(pt[b][:, dt * 128:(dt + 1) * 128],
                                                                    xb[b][:, dt * 128:(dt + 1) * 128], self.idb[:]),
                             reads=[xb_t[b], self.idb_t], writes=[pt_t[b]], sig=(dt == 15))
                    eng = "dve" if tt % 2 == 0 else "act"
                    src = pt[b][:].rearrange("p (k t) -> p k t", t=128)
                    dst = xT[:, :, tt * 128:(tt + 1) * 128]
                    if eng == "dve":
                        S.op("dve", lambda e, s=src, d=dst: e.tensor_copy(out=d, in_=s), reads=[pt_t[b]], writes=[xT_t[tt]])
                    else:
                        S.op("act", lambda e, s=src, d=dst: e.activation(out=d, in_=s, func=AF.Copy), reads=[pt_t[b]], writes=[xT_t[tt]])
                S.barrier()
            with ExitStack() as es1:
                E1 = es1.enter_context
                NWB = 3
                wp = [E1(nc.sbuf_tensor(f"wp{i}", [128, 16, 256], BF16)) for i in range(NWB)]
                wp_t = [T(f"wp{i}") for i in range(NWB)]
                ps = [E1(nc.psum_tensor(f"psA{i}", [128, L], F32)) for i in range(2)]
                ps_t = [T(f"psA{i}") for i in range(2)]
                ob = [E1(nc.sbuf_tensor(f"obA{i}", [128, L], BF16)) for i in range(2)]
                ob_t = [T(f"obA{i}") for i in range(2)]
                of = [E1(nc.sbuf_tensor(f"ofA{i}", [128, L], F32)) for i in range(2)]
                of_t = [T(f"ofA{i}") for i in range(2)]
                npan = 49
                def load(pi):
                    w = 256 if pi < 48 else 128
                    b = pi % NWB
                    S.dma("pool", wp[b][:, :, 0:w], pk(w_fm, 0, 16, pi * 256, w), wp_t[b], writes=[wp_t[b]])
                load(0)
                load(1)
                mt = 0
                for pi in range(npan):
                    if pi + 2 < npan:
                        load(pi + 2)
                    b = pi % NWB
                    for mi in range(2 if pi < 48 else 1):
                        pb = mt % 2
                        for nb in range(NB):
                            for kt in range(16):
                                S.op("pe", lambda e, kt=kt, nb=nb, mi=mi, b=b, pb=pb: e.matmul(
                                    ps[pb][:, nb * 512:(nb + 1) * 512], lhsT=wp[b][:, kt, mi * 128:(mi + 1) * 128],
                                    rhs=xT[:, kt, nb * 512:(nb + 1) * 512], start=(kt == 0), stop=(kt == 15)),
                                    reads=[wp_t[b]] + xT_t, writes=[ps_t[pb]], sig=(kt == 15 and nb == NB - 1))
                        if mt < 16:
                            dst_tok = self.tok(("qkT", mt))
                            if mt % 2 == 0:
                                S.op("dve", lambda e, pb=pb: e.tensor_copy(out=ob[pb][:], in_=ps[pb][:]), reads=[ps_t[pb]], writes=[ob_t[pb]])
                            else:
                                S.op("act", lambda e, pb=pb: e.activation(out=ob[pb][:], in_=ps[pb][:], func=AF.Copy), reads=[ps_t[pb]], writes=[ob_t[pb]])
                            S.dma("sp", qkT[mt * 128:(mt + 1) * 128, :], ob[pb][:], ob_t[pb], reads=[ob_t[pb]], writes=[dst_tok])
                        elif mt < 64:
                            ct = mt - 16
                            c0 = COL_CONV + ct * 4
                            S.op("act", lambda e, pb=pb, c0=c0: e.activation(out=of[pb][:], in_=ps[pb][:], func=AF.Identity,
                                                                              scale=cp[:, c0 + 1:c0 + 2], bias=cp[:, c0 + 3:c0 + 4]),
                                 reads=[ps_t[pb], cp_t], writes=[of_t[pb]])
                            S.op("dve", lambda e, pb=pb, c0=c0: e.scalar_tensor_tensor(out=of[pb][:, 1:L], in0=ps[pb][:, 0:L - 1], scalar=cp[:, c0:c0 + 1],
                                                                                       in1=of[pb][:, 1:L], op0=ALU.mult, op1=ALU.add),
                                 reads=[ps_t[pb], of_t[pb], cp_t], writes=[of_t[pb]])
                            S.op("dve", lambda e, pb=pb, c0=c0: e.scalar_tensor_tensor(out=ob[pb][:, 0:L - 1], in0=ps[pb][:, 1:L], scalar=cp[:, c0 + 2:c0 + 3],
                                                                                       in1=of[pb][:, 0:L - 1], op0=ALU.mult, op1=ALU.add),
                                 reads=[ps_t[pb], of_t[pb], cp_t], writes=[ob_t[pb]])
                            S.op("dve", lambda e, pb=pb: e.tensor_copy(out=ob[pb][:, L - 1:L], in_=of[pb][:, L - 1:L]),
                                 reads=[of_t[pb], ob_t[pb]], writes=[ob_t[pb]])
                            dst_tok = self.tok(("ucT", ct))
                            S.dma("sp", ucT[ct * 128:(ct + 1) * 128, :], ob[pb][:], ob_t[pb], reads=[ob_t[pb]], writes=[dst_tok])
                        elif mt < 96:
                            gt = mt - 64
                            S.op("act", lambda e, pb=pb: e.activation(out=ob[pb][:], in_=ps[pb][:], func=AF.Sigmoid), reads=[ps_t[pb]], writes=[ob_t[pb]])
                            dst_tok = self.tok(("gT", gt))
                            S.dma("sp", gT[gt * 128:(gt + 1) * 128, :], ob[pb][:], ob_t[pb], reads=[ob_t[pb]], writes=[dst_tok])
                        else:
                            S.op("dve", lambda e, pb=pb: e.tensor_copy(out=of[pb][:], in_=ps[pb][:]), reads=[ps_t[pb]], writes=[of_t[pb]])
                            dst_tok = self.tok(("afbT", 0))
                            S.dma("sp", afbT[:, :], of[pb][:], of_t[pb], reads=[of_t[pb]], writes=[dst_tok])
                        mt += 1
                S.barrier()
            with ExitStack() as es2:
                E2 = es2.enter_context
                wr = [E2(nc.sbuf_tensor(f"wr{i}", [128, 16, 512], BF16)) for i in range(2)]
                wr_t = [T(f"wr{i}") for i in range(2)]
                ps = [E2(nc.psum_tensor(f"psB{i}", [128, 512], F32)) for i in range(4)]
                ps_t = [T(f"psB{i}") for i in range(4)]
                ob = [E2(nc.sbuf_tensor(f"obB{i}", [128, 512], BF16)) for i in range(4)]
                ob_t = [T(f"obB{i}") for i in range(4)]
                def loadr(cb):
                    S.dma("pool", wr[cb % 2][:], pk(w_tm, 0, 16, cb * 512, 512), wr_t[cb % 2], writes=[wr_t[cb % 2]])
                loadr(0)
                it = 0
                for cb in range(8):
                    if cb + 1 < 8:
                        loadr(cb + 1)
                    b = cb % 2
                    for tt in range(NT):
                        pb = it % 4
                        for kt in range(16):
                            S.op("pe", lambda e, kt=kt, tt=tt, b=b, pb=pb: e.matmul(
                                ps[pb][:], lhsT=xT[:, kt, tt * 128:(tt + 1) * 128], rhs=wr[b][:, kt, :], start=(kt == 0), stop=(kt == 15)),
                                reads=[wr_t[b], xT_t[tt]], writes=[ps_t[pb]], sig=(kt == 15))
                        if cb < 4:
                            S.op("dve", lambda e, pb=pb: e.tensor_copy(out=ob[pb][:], in_=ps[pb][:]), reads=[ps_t[pb]], writes=[ob_t[pb]])
                            dst, dtok = vtm, self.tok(("vtm", tt, cb))
                        else:
                            S.op("act", lambda e, pb=pb: e.activation(out=ob[pb][:], in_=ps[pb][:], func=AF.Silu), reads=[ps_t[pb]], writes=[ob_t[pb]])
                            dst, dtok = srtm, self.tok(("srtm", tt, cb - 4))
                        c0 = (cb % 4) * 512
                        S.dma("sp", dst[tt * 128:(tt + 1) * 128, c0:c0 + 512], ob[pb][:], ob_t[pb], reads=[ob_t[pb]], writes=[dtok])
                        it += 1
                S.barrier()


    def transpose_dram(self, src, dst, R, C, src_tok, dst_tok, name, dt=BF16, ident=None, ident_t=None):
        nc, S = self.nc, self.S
        CT = C // 128
        idm = self.idb if ident is None else ident
        idm_t = self.idb_t if ident_t is None else ident_t
        with ExitStack() as es:
            E = es.enter_context
            ib = [E(nc.sbuf_tensor(f"{name}_ib{i}", [128, C], dt)) for i in range(2)]
            ib_t = [T() for _ in range(2)]
            pt = [E(nc.psum_tensor(f"{name}_pt{i}", [128, C], dt)) for i in range(2)]
            pt_t = [T() for _ in range(2)]
            st = [E(nc.sbuf_tensor(f"{name}_st{i}", [128, CT, 512], dt)) for i in range(2)]
            st_t = [T() for _ in range(2)]
            S.dma("sp", ib[0][:], src[0:128, :], ib_t[0], reads=src_tok(0), writes=[ib_t[0]])
            for rt in range(R // 128):
                b = rt % 2
                rb = rt // 4
                sb = rb % 2
                if rt + 1 < R // 128:
                    S.dma("sp", ib[1 - b][:], src[(rt + 1) * 128:(rt + 2) * 128, :], ib_t[1 - b], reads=src_tok(rt + 1), writes=[ib_t[1 - b]])
                for ct in range(CT):
                    S.op("pe", lambda e, ct=ct, b=b: e.transpose(pt[b][:, ct * 128:(ct + 1) * 128], ib[b][:, ct * 128:(ct + 1) * 128], idm[:]),
                         reads=[ib_t[b], idm_t], writes=[pt_t[b]], sig=(ct == CT - 1))
                srcv = pt[b][:].rearrange("p (k t) -> p k t", t=128)
                dstv = st[sb][:, :, (rt % 4) * 128:(rt % 4 + 1) * 128]
                if rt % 2 == 0:
                    S.op("dve", lambda e, s_=srcv, d_=dstv: e.tensor_copy(out=d_, in_=s_), reads=[pt_t[b]], writes=[st_t[sb]])
                else:
                    S.op("act", lambda e, s_=srcv, d_=dstv: e.activation(out=d_, in_=s_, func=AF.Copy), reads=[pt_t[b]], writes=[st_t[sb]])
                if rt % 4 == 3:
                    S.dma("sp", dst[:, rb * 512:(rb + 1) * 512].rearrange("(k p) r -> p k r", p=128), st[sb][:], st_t[sb],
                          reads=[st_t[sb]], writes=dst_tok(rb))
            S.barrier()

    def phase_b(self, qkT, vtm, afbT, srtm, ofD, ytm, wa2p, gng, masks):
        nc, S, L, NT, NB = self.nc, self.S, self.L, self.NT, self.NB
        cp, cp_t = self.cp, self.cp_t
        qk_toks = [self.tok(("qkT", i)) for i in range(16)]
        with ExitStack() as es:
            E = es.enter_context
            qx = E(nc.sbuf_tensor("qx", [128, 8, L], BF16))
            kx = E(nc.sbuf_tensor("kx", [128, 8, L], BF16))
            qx_t = [T() for _ in range(8)]
            kx_t = [T() for _ in range(8)]
            dec = E(nc.sbuf_tensor("dec", [128, 8, NT], F32))
            dec_t = [T() for _ in range(8)]
            mk = E(nc.sbuf_tensor("mk", [128, 2, 512], F32))
            mk_t = T()
            S.dma("sp", mk[:], masks[:, :, :], mk_t, writes=[mk_t])
            smask = E(nc.sbuf_tensor("smask", [128, L], F32))
            smask_t = T()
            S.op("dve", lambda e: e.memset(smask[:], 1.0), writes=[smask_t])
            S.op("dve", lambda e: e.memset(smask[:].rearrange("p (n c) -> p n c", c=CH)[:, :, 0:1], 0.0), writes=[smask_t])
            negba = E(nc.sbuf_tensor("negba", [128, 16], F32))
            negba_t = T()
            S.op("dve", lambda e: e.tensor_scalar(out=negba[:], in0=cp[:, COL_BA:COL_BA + 16], scalar1=-1.0, scalar2=None, op0=ALU.mult),
                 reads=[cp_t], writes=[negba_t])
            S32 = E(nc.sbuf_tensor("S32", [128, 8, 512], F32))
            Sbf = E(nc.sbuf_tensor("Sbf", [128, 8, 512], BF16))
            S32_t = [T() for _ in range(8)]
            Sbf_t = [T() for _ in range(8)]

            for dr in (0, 1):
                with ExitStack() as esp:
                    Ep = esp.enter_context
                    zps = [Ep(nc.psum_tensor(f"zps{dr}{i}", [128, L], F32)) for i in range(2)]
                    zps_t = [T() for _ in range(2)]
                    w2 = Ep(nc.sbuf_tensor(f"w2_{dr}", [64, 1024], F32))
                    w2_t = T()
                    S.dma("sp", w2[:], wa2p[dr, :, :], w2_t, writes=[w2_t])
                    af = Ep(nc.sbuf_tensor(f"af_{dr}", [64, L], F32))
                    af_t = T()
                    S.dma("sp", af[:], afbT[0:64, :], af_t, reads=[self.tok(("afbT", 0))], writes=[af_t])
                    nl = [Ep(nc.sbuf_tensor(f"nl{dr}{i}", [128, L], F32)) for i in range(2)]
                    cs = [Ep(nc.sbuf_tensor(f"cs{dr}{i}", [128, L], F32)) for i in range(2)]
                    e1 = [Ep(nc.sbuf_tensor(f"e1{dr}{i}", [128, L], F32)) for i in range(2)]
                    e2 = [Ep(nc.sbuf_tensor(f"e2{dr}{i}", [128, L], F32)) for i in range(2)]
                    qr = [Ep(nc.sbuf_tensor(f"qr{dr}{i}", [128, L], BF16)) for i in range(2)]
                    kr = [Ep(nc.sbuf_tensor(f"kr{dr}{i}", [128, L], BF16)) for i in range(2)]
                    nl_t = [T() for _ in range(2)]
                    cs_t = [T() for _ in range(2)]
                    e1_t = [T() for _ in range(2)]
                    e2_t = [T() for _ in range(2)]
                    qr_t = [T() for _ in range(2)]
                    kr_t = [T() for _ in range(2)]
                    def prep1(dt):
                        b = dt % 2
                        S.dma("sp", qr[b][:], qkT[dt * 128:(dt + 1) * 128, :], qr_t[b], reads=[qk_toks[dt]], writes=[qr_t[b]])
                        S.dma("sp", kr[b][:], qkT[1024 + dt * 128:1024 + (dt + 1) * 128, :], kr_t[b], reads=[qk_toks[8 + dt]], writes=[kr_t[b]])
                        for nb in range(NB):
                            S.op("pe", lambda e, nb=nb, dt=dt, b=b: e.matmul(zps[b][:, nb * 512:(nb + 1) * 512], lhsT=w2[:, dt * 128:(dt + 1) * 128],
                                                                           rhs=af[:, nb * 512:(nb + 1) * 512], start=True, stop=True),
                                 reads=[w2_t, af_t], writes=[zps_t[b]], sig=(nb == NB - 1))
                        bc = dr * 8 + dt
                        S.op("act", lambda e, b=b, bc=bc: e.activation(out=e1[b][:], in_=zps[b][:], func=AF.Exp, scale=-1.0, bias=negba[:, bc:bc + 1]),
                             reads=[zps_t[b], negba_t], writes=[e1_t[b]])
                        S.op("act", lambda e, b=b: e.activation(out=nl[b][:], in_=e1[b][:], func=AF.Ln, bias=1.0, scale=1.0),
                             reads=[e1_t[b]], writes=[nl_t[b]])
                        S.op("dve", lambda e, b=b: e.tensor_tensor_scan(out=cs[b][:], data0=smask[:], data1=nl[b][:], initial=0.0, op0=ALU.mult, op1=ALU.add),
                             reads=[smask_t, nl_t[b]], writes=[cs_t[b]])

                    def prep2(dt):
                        b = dt % 2
                        bc = dr * 8 + dt
                        csl = cs[b][:].rearrange("p (n c) -> p n c", c=CH)[:, :, CH - 1:CH]
                        if dr == 0:
                            S.op("act", lambda e, b=b: e.activation(out=e1[b][:], in_=cs[b][:], func=AF.Exp, scale=-1.0 / TAU),
                                 reads=[cs_t[b]], writes=[e1_t[b]])
                            S.op("act", lambda e, b=b: e.activation(out=e2[b][:], in_=cs[b][:], func=AF.Exp, scale=1.0 / TAU),
                                 reads=[cs_t[b]], writes=[e2_t[b]])
                        else:
                            S.op("dve", lambda e, b=b: e.tensor_tensor(out=nl[b][:], in0=cs[b][:], in1=nl[b][:], op=ALU.subtract),
                                 reads=[cs_t[b], nl_t[b]], writes=[nl_t[b]])
                            S.op("act", lambda e, b=b: e.activation(out=e1[b][:], in_=nl[b][:], func=AF.Exp, scale=1.0 / TAU),
                                 reads=[nl_t[b]], writes=[e1_t[b]])
                            S.op("act", lambda e, b=b: e.activation(out=e2[b][:], in_=nl[b][:], func=AF.Exp, scale=-1.0 / TAU),
                                 reads=[nl_t[b]], writes=[e2_t[b]])
                        S.op("act", lambda e, dt=dt, csl=csl: e.activation(out=dec[:, dt, :].rearrange("p (n o) -> p n o", o=1), in_=csl, func=AF.Exp, scale=-1.0 / TAU),
                             reads=[cs_t[b]], writes=[dec_t[dt]])
                        S.op("dve", lambda e, b=b, dt=dt: e.scalar_tensor_tensor(out=qx[:, dt, :], in0=qr[b][:], scalar=DK ** -0.5, in1=e1[b][:],
                                                                                  op0=ALU.mult, op1=ALU.mult),
                             reads=[qr_t[b], e1_t[b]], writes=[qx_t[dt]])
                        S.op("pool", lambda e, b=b, dt=dt: e.tensor_tensor(out=kx[:, dt, :], in0=kr[b][:], in1=e2[b][:], op=ALU.mult),
                             reads=[kr_t[b], e2_t[b]], writes=[kx_t[dt]])
                    prep1(0)
                    for dt in range(8):
                        if dt + 1 < 8:
                            prep1(dt + 1)
                        prep2(dt)
                    S.barrier()
                with ExitStack() as esc:
                    Ec = esc.enter_context
                    vb = [Ec(nc.sbuf_tensor(f"vb{dr}{i}", [128, 2048], BF16)) for i in range(2)]
                    vb_t = [T() for _ in range(2)]
                    pkT = Ec(nc.psum_tensor(f"pkT{dr}", [128, 1024], BF16))
                    pkT_t = T()
                    kxT = [Ec(nc.sbuf_tensor(f"kxT{dr}{i}", [128, 1024], BF16)) for i in range(2)]
                    kxT_t = [T() for _ in range(2)]
                    psS = Ec(nc.psum_tensor(f"psS{dr}", [128, 512], F32))
                    psS_t = T()
                    sT = [Ec(nc.sbuf_tensor(f"sT{dr}{i}", [128, 512], BF16)) for i in range(2)]
                    sT_t = [T() for _ in range(2)]
                    psO = Ec(nc.psum_tensor(f"psO{dr}", [128, 2048], F32))
                    psO_t = [T() for _ in range(4)]
                    psKV = [Ec(nc.psum_tensor(f"psKV{dr}{i}", [128, 512], F32)) for i in range(2)]
                    psKV_t = [T() for _ in range(2)]
                    o32 = [Ec(nc.sbuf_tensor(f"o32{dr}{i}", [128, 2048], F32)) for i in range(2)]
                    o32_t = [T() for _ in range(2)]
                    if dr == 1:
                        ofin = [Ec(nc.sbuf_tensor(f"ofin{i}", [128, 2048], F32)) for i in range(2)]
                        ofin_t = [T() for _ in range(2)]
                        srin = [Ec(nc.sbuf_tensor(f"srin{i}", [128, 2048], BF16)) for i in range(2)]
                        srin_t = [T() for _ in range(2)]
                        ybf = [Ec(nc.sbuf_tensor(f"ybf{i}", [128, 2048], BF16)) for i in range(2)]
                        ybf_t = [T() for _ in range(2)]
                        junk = Ec(nc.sbuf_tensor("junkB", [128, 512], F32))
                        junk_t = T()
                        ssq = [Ec(nc.sbuf_tensor(f"ssq{i}", [128, 4], F32)) for i in range(2)]
                        ssq_t = [T() for _ in range(2)]
                    order = list(range(NT)) if dr == 0 else list(range(NT - 1, -1, -1))

                    def stage_a(step, n):
                        b = step % 2
                        c0 = n * CH
                        S.dma("sp", vb[b][:], vtm[c0:c0 + 128, :], vb_t[b], reads=[self.tok(("vtm", n, j)) for j in range(4)], writes=[vb_t[b]])
                        if dr == 1:
                            S.dma("sp", ofin[b][:], ofD[c0:c0 + 128, :], ofin_t[b], reads=[self.tok(("ofD", n))], writes=[ofin_t[b]])
                            S.dma("sp", srin[b][:], srtm[c0:c0 + 128, :], srin_t[b], reads=[self.tok(("srtm", n, j)) for j in range(4)], writes=[srin_t[b]])
                        for dt in range(8):
                            S.op("pe", lambda e, dt=dt, c0=c0: e.transpose(pkT[:, dt * 128:(dt + 1) * 128], kx[:, dt, c0:c0 + 128], self.idb[:]),
                                 reads=[kx_t[dt], self.idb_t], writes=[pkT_t], sig=(dt == 7))
                        S.op("dve", lambda e, b=b: e.tensor_copy(out=kxT[b][:], in_=pkT[:]), reads=[pkT_t], writes=[kxT_t[b]])
                        for h in range(4):
                            for dd in range(2):
                                dt = 2 * h + dd
                                S.op("pe", lambda e, h=h, dt=dt, dd=dd, c0=c0: e.matmul(psS[:, h * 128:(h + 1) * 128], lhsT=kx[:, dt, c0:c0 + 128],
                                                                                     rhs=qx[:, dt, c0:c0 + 128], start=(dd == 0), stop=(dd == 1)),
                                     reads=[kx_t[dt], qx_t[dt]], writes=[psS_t], sig=(h == 3 and dd == 1))
                        S.op("dve", lambda e, b=b: e.tensor_tensor(out=sT[b][:], in0=psS[:], in1=mk[:, dr, :], op=ALU.mult),
                             reads=[psS_t, mk_t], writes=[sT_t[b]])

                    def stage_b(step, n):
                        b = step % 2
                        c0 = n * CH
                        first = (step == 0)
                        last = (step == NT - 1)
                        bw = (dr == 1)
                        for h in range(4):
                            S.op("pe", lambda e, h=h, b=b: e.matmul(psO[:, h * 512:(h + 1) * 512], lhsT=sT[b][:, h * 128:(h + 1) * 128],
                                                                    rhs=vb[b][:, h * 512:(h + 1) * 512], start=True, stop=(first and not bw)),
                                 reads=[sT_t[b], vb_t[b]], writes=[psO_t[h]], sig=(first and not bw))
                            if not first:
                                for dd in range(2):
                                    dt = 2 * h + dd
                                    S.op("pe", lambda e, h=h, dt=dt, dd=dd, c0=c0: e.matmul(psO[:, h * 512:(h + 1) * 512], lhsT=qx[:, dt, c0:c0 + 128],
                                                                                         rhs=Sbf[:, dt, :], start=False, stop=(dd == 1 and not bw)),
                                         reads=[qx_t[dt], Sbf_t[dt]], writes=[psO_t[h]], sig=(dd == 1 and not bw))
                            if bw:
                                S.op("pe", lambda e, h=h, b=b: e.matmul(psO[:, h * 512:(h + 1) * 512], lhsT=self.idf[:], rhs=ofin[b][:, h * 512:(h + 1) * 512],
                                                                        start=False, stop=True),
                                     reads=[self.idf_t, ofin_t[b]], writes=[psO_t[h]], sig=True)
                        if dr == 0:
                            S.op("act", lambda e, b=b: e.activation(out=o32[b][:], in_=psO[:], func=AF.Copy), reads=psO_t, writes=[o32_t[b]])
                            S.dma("sp", ofD[c0:c0 + 128, :], o32[b][:], o32_t[b], reads=[o32_t[b]], writes=[self.tok(("ofD", n))])
                        else:
                            for h in range(4):
                                S.op("act", lambda e, b=b, h=h: e.activation(out=junk[:], in_=psO[:, h * 512:(h + 1) * 512], func=AF.Square,
                                                                            accum_out=ssq[b][:, h:h + 1]),
                                     reads=[psO_t[h]], writes=[junk_t, ssq_t[b]])
                            S.op("dve", lambda e, b=b: e.tensor_scalar(out=ssq[b][:], in0=ssq[b][:], scalar1=1.0 / DV, scalar2=LN_EPS, op0=ALU.mult, op1=ALU.add),
                                 reads=[ssq_t[b]], writes=[ssq_t[b]])
                            S.op("act", lambda e, b=b: e.activation(out=ssq[b][:], in_=ssq[b][:], func=AF.Ln), reads=[ssq_t[b]], writes=[ssq_t[b]])
                            S.op("act", lambda e, b=b: e.activation(out=ssq[b][:], in_=ssq[b][:], func=AF.Exp, scale=-0.5), reads=[ssq_t[b]], writes=[ssq_t[b]])
                            for h in range(4):
                                S.op("dve", lambda e, b=b, h=h: e.scalar_tensor_tensor(out=ybf[b][:, h * 512:(h + 1) * 512], in0=psO[:, h * 512:(h + 1) * 512],
                                                                                      scalar=ssq[b][:, h:h + 1], in1=srin[b][:, h * 512:(h + 1) * 512],
                                                                                      op0=ALU.mult, op1=ALU.mult),
                                     reads=[psO_t[h], ssq_t[b], srin_t[b]], writes=[ybf_t[b]])
                            S.dma("sp", ytm[c0:c0 + 128, :], ybf[b][:], ybf_t[b], reads=[ybf_t[b]], writes=[self.tok(("ytm", n))])
                        if not last:
                            m = n if dr == 0 else n - 1
                            mprev = mprev_box[0]
                            for h in range(4):
                                for dd in range(2):
                                    dt = 2 * h + dd
                                    kb = dt % 2
                                    S.op("pe", lambda e, dt=dt, h=h, b=b, kb=kb: e.matmul(psKV[kb][:], lhsT=kxT[b][:, dt * 128:(dt + 1) * 128],
                                                                                       rhs=vb[b][:, h * 512:(h + 1) * 512], start=True, stop=True),
                                         reads=[kxT_t[b], vb_t[b]], writes=[psKV_t[kb]])
                                    if first:
                                        S.op("dve", lambda e, dt=dt, kb=kb: e.tensor_copy(out=S32[:, dt, :], in_=psKV[kb][:]),
                                             reads=[psKV_t[kb]], writes=[S32_t[dt]])
                                    else:
                                        S.op("dve", lambda e, dt=dt, kb=kb, mprev=mprev: e.scalar_tensor_tensor(out=S32[:, dt, :], in0=S32[:, dt, :], scalar=dec[:, dt, mprev:mprev + 1],
                                                                                                       in1=psKV[kb][:], op0=ALU.mult, op1=ALU.add),
                                             reads=[S32_t[dt], psKV_t[kb], dec_t[dt]], writes=[S32_t[dt]])
                                    S.op("act", lambda e, dt=dt, m=m: e.activation(out=Sbf[:, dt, :], in_=S32[:, dt, :], func=AF.Identity, scale=dec[:, dt, m:m + 1]),
                                         reads=[S32_t[dt], dec_t[dt]], writes=[Sbf_t[dt]])
                            mprev_box[0] = m

                    mprev_box = [None]
                    stage_a(0, order[0])
                    for step, n in enumerate(order):
                        if step + 1 < NT:
                            stage_a(step + 1, order[step + 1])
                        stage_b(step, n)
                    S.barrier()


    def phase_c_filters(self, embT, mlpw, mlpc, w4aug, negt, deltas, hsD, hdD):
        nc, S, L, NT, NB = self.nc, self.S, self.L, self.NT, self.NB
        PI = math.pi
        with ExitStack() as es:
            E = es.enter_context
            hid = E(nc.sbuf_tensor("hid3", [65, L], BF16))
            hid_t = T()
            w4 = E(nc.sbuf_tensor("w4sb", [65, 8192], BF16))
            w4_t = T()
            S.dma("pool", w4[:], w4aug[:, :], w4_t, writes=[w4_t])
            wsd = E(nc.sbuf_tensor("w4sd", [65, 2, 2, 2048], BF16))
            wsd_t = T()
            for o in range(2):
                S.op("dve", lambda e, o=o: e.tensor_tensor(out=wsd[:, o, 0, :], in0=w4[:, o * 4096:o * 4096 + 2048], in1=w4[:, o * 4096 + 2048:o * 4096 + 4096], op=ALU.add),
                     reads=[w4_t], writes=[wsd_t])
                S.op("dve", lambda e, o=o: e.tensor_tensor(out=wsd[:, o, 1, :], in0=w4[:, o * 4096:o * 4096 + 2048], in1=w4[:, o * 4096 + 2048:o * 4096 + 4096], op=ALU.subtract),
                     reads=[w4_t], writes=[wsd_t])
            with ExitStack() as es0:
                E0 = es0.enter_context
                em = E0(nc.sbuf_tensor("embsb", [64, L], F32))
                em_t = T()
                S.dma("sp", em[:], embT[:, :], em_t, writes=[em_t])
                mw = E0(nc.sbuf_tensor("mlpw_sb", [64, 3, 64], F32))
                mw_t = T()
                S.dma("sp", mw[:], mlpw[:, :, :], mw_t, writes=[mw_t])
                mc = E0(nc.sbuf_tensor("mlpc_sb", [64, 8], F32))
                mc_t = T()
                S.dma("sp", mc[:, 0:4], mlpc[:, :], mc_t, writes=[mc_t])
                for l in range(3):
                    S.op("dve", lambda e, l=l: e.tensor_tensor(out=mc[:, 4 + l:5 + l], in0=mc[:, l:l + 1], in1=mc[:, 3:4], op=ALU.mult),
                         reads=[mc_t], writes=[mc_t])
                hp = E0(nc.psum_tensor("hps", [64, L], F32))
                hp_t = T()
                ha = [E0(nc.sbuf_tensor(f"ha{i}", [64, L], F32)) for i in range(2)]
                ha_t = [T() for _ in range(2)]
                t1 = E0(nc.sbuf_tensor("hwrap1", [64, L], F32))
                t1_t = T()
                t2 = E0(nc.sbuf_tensor("hwrap2", [64, L], F32))
                t2_t = T()
                cur, cur_t = em, em_t
                for l in range(3):
                    for nb in range(NB):
                        S.op("pe", lambda e, l=l, nb=nb, cur=cur: e.matmul(hp[:, nb * 512:(nb + 1) * 512], lhsT=mw[:, l, :], rhs=cur[:, nb * 512:(nb + 1) * 512],
                                                                       start=True, stop=True),
                             reads=[mw_t, cur_t], writes=[hp_t], sig=(nb == NB - 1))
                    a, a_t = ha[l % 2], ha_t[l % 2]
                    S.op("act", lambda e, l=l, a=a: e.activation(out=a[:], in_=hp[:], func=AF.Identity, scale=mc[:, 3:4], bias=mc[:, 4 + l:5 + l]),
                         reads=[hp_t, mc_t], writes=[a_t])
                    S.op("dve", lambda e, a=a: e.tensor_scalar(out=t1[:], in0=a[:], scalar1=PI, scalar2=-2.0 * PI, op0=ALU.is_gt, op1=ALU.mult),
                         reads=[a_t], writes=[t1_t])
                    S.op("dve", lambda e, a=a: e.tensor_scalar(out=t2[:], in0=a[:], scalar1=-PI, scalar2=2.0 * PI, op0=ALU.is_lt, op1=ALU.mult),
                         reads=[a_t], writes=[t2_t])
                    S.op("dve", lambda e, a=a: e.tensor_tensor(out=a[:], in0=a[:], in1=t1[:], op=ALU.add), reads=[a_t, t1_t], writes=[a_t])
                    S.op("dve", lambda e, a=a: e.tensor_tensor(out=a[:], in0=a[:], in1=t2[:], op=ALU.add), reads=[a_t, t2_t], writes=[a_t])
                    if l < 2:
                        S.op("act", lambda e, a=a: e.activation(out=a[:], in_=a[:], func=AF.Sin), reads=[a_t], writes=[a_t])
                        cur, cur_t = a, a_t
                    else:
                        S.op("act", lambda e, a=a: e.activation(out=hid[0:64, :], in_=a[:], func=AF.Sin), reads=[a_t], writes=[hid_t])
                        S.op("dve", lambda e: e.memset(hid[64:65, :], 1.0), writes=[hid_t])
                S.barrier()
            with ExitStack() as es1:
                E1 = es1.enter_context
                dl = E1(nc.sbuf_tensor("dlb", [128, 2048], F32))
                dl_t = T()
                S.dma("sp", dl[:], deltas[0, :].partition_broadcast(128), dl_t, writes=[dl_t])
                ng = E1(nc.sbuf_tensor("negt_sb", [128, NT], F32))
                ng_t = T()
                S.dma("sp", ng[:], negt[:, :], ng_t, writes=[ng_t])
                dtile = [E1(nc.sbuf_tensor(f"dtile{i}", [128, 512], F32)) for i in range(2)]
                dtile_t = [T() for _ in range(2)]
                p0 = [E1(nc.psum_tensor(f"fp0{i}", [128, 512], F32)) for i in range(2)]
                p1 = [E1(nc.psum_tensor(f"fp1{i}", [128, 512], F32)) for i in range(2)]
                p0_t = [T() for _ in range(2)]
                p1_t = [T() for _ in range(2)]
                h0 = [E1(nc.sbuf_tensor(f"fh0{i}", [128, 512], F32)) for i in range(2)]
                h1 = [E1(nc.sbuf_tensor(f"fh1{i}", [128, 512], F32)) for i in range(2)]
                h0_t = [T() for _ in range(2)]
                h1_t = [T() for _ in range(2)]
                hs = [E1(nc.sbuf_tensor(f"fhs{i}", [128, 512], BF16)) for i in range(2)]
                hd = [E1(nc.sbuf_tensor(f"fhd{i}", [128, 512], BF16)) for i in range(2)]
                hs_t = [T() for _ in range(2)]
                hd_t = [T() for _ in range(2)]
                it = 0
                for tt in range(NT):
                    for cb in range(4):
                        db = (tt * 4 + cb) % 2
                        S.op("act", lambda e, db=db, cb=cb, tt=tt: e.activation(out=dtile[db][:], in_=dl[:, cb * 512:(cb + 1) * 512], func=AF.Exp, scale=ng[:, tt:tt + 1]),
                             reads=[dl_t, ng_t], writes=[dtile_t[db]])
                        for o in range(2):
                            b = it % 2
                            c0 = o * 4096 + cb * 512
                            if tt == 0:
                                S.op("pe", lambda e, b=b, c0=c0, tt=tt: e.matmul(p0[b][:], lhsT=hid[:, tt * 128:(tt + 1) * 128], rhs=w4[:, c0:c0 + 512], start=True, stop=True),
                                     reads=[hid_t, w4_t], writes=[p0_t[b]])
                                S.op("pe", lambda e, b=b, c0=c0, tt=tt: e.matmul(p1[b][:], lhsT=hid[:, tt * 128:(tt + 1) * 128], rhs=w4[:, c0 + 2048:c0 + 2560], start=True, stop=True),
                                     reads=[hid_t, w4_t], writes=[p1_t[b]])
                                S.op("dve", lambda e, b=b, db=db: e.tensor_tensor(out=h0[b][:], in0=p0[b][:], in1=dtile[db][:], op=ALU.mult),
                                     reads=[p0_t[b], dtile_t[db]], writes=[h0_t[b]])
                                S.op("dve", lambda e, b=b, db=db: e.tensor_tensor(out=h1[b][:], in0=p1[b][:], in1=dtile[db][:], op=ALU.mult),
                                     reads=[p1_t[b], dtile_t[db]], writes=[h1_t[b]])
                                S.op("dve", lambda e, b=b: e.memset(h1[b][0:1, :], 0.0), reads=[h1_t[b]], writes=[h1_t[b]])
                                S.op("dve", lambda e, b=b: e.tensor_tensor(out=hs[b][:], in0=h0[b][:], in1=h1[b][:], op=ALU.add),
                                     reads=[h0_t[b], h1_t[b]], writes=[hs_t[b]])
                                S.op("dve", lambda e, b=b: e.tensor_tensor(out=hd[b][:], in0=h0[b][:], in1=h1[b][:], op=ALU.subtract),
                                     reads=[h0_t[b], h1_t[b]], writes=[hd_t[b]])
                            else:
                                S.op("pe", lambda e, b=b, o=o, cb=cb, tt=tt: e.matmul(p0[b][:], lhsT=hid[:, tt * 128:(tt + 1) * 128], rhs=wsd[:, o, 0, cb * 512:(cb + 1) * 512], start=True, stop=True),
                                     reads=[hid_t, wsd_t], writes=[p0_t[b]])
                                S.op("pe", lambda e, b=b, o=o, cb=cb, tt=tt: e.matmul(p1[b][:], lhsT=hid[:, tt * 128:(tt + 1) * 128], rhs=wsd[:, o, 1, cb * 512:(cb + 1) * 512], start=True, stop=True),
                                     reads=[hid_t, wsd_t], writes=[p1_t[b]])
                                S.op("dve", lambda e, b=b, db=db: e.tensor_tensor(out=hs[b][:], in0=p0[b][:], in1=dtile[db][:], op=ALU.mult),
                                     reads=[p0_t[b], dtile_t[db]], writes=[hs_t[b]])
                                S.op("dve", lambda e, b=b, db=db: e.tensor_tensor(out=hd[b][:], in0=p1[b][:], in1=dtile[db][:], op=ALU.mult),
                                     reads=[p1_t[b], dtile_t[db]], writes=[hd_t[b]])
                            S.dma("sp", hsD[o, tt * 128:(tt + 1) * 128, cb * 512:(cb + 1) * 512], hs[b][:], hs_t[b], reads=[hs_t[b]], writes=[self.tok(("hsD", o, tt, cb))])
                            S.dma("sp", hdD[o, tt * 128:(tt + 1) * 128, cb * 512:(cb + 1) * 512], hd[b][:], hd_t[b], reads=[hd_t[b]], writes=[self.tok(("hdD", o, tt, cb))])
                            it += 1
                S.barrier()

    def dft_dims(self):
        L = self.L
        NK = ((((2 * L - 1) + 3) // 4 + 1) + 127) // 128 * 128
        return NK, NK // 128, L // 2, L // 256

    def eo_rows(self, src2d, j, par):
        return src2d[j * 256:(j + 1) * 256, :].rearrange("(p two) c -> two p c", two=2)[par]

    def phase_filter_spectra(self, Fw, hsD, hdD, F2D, skipb):
        nc, S, L, NT = self.nc, self.S, self.L, self.NT
        NK, NKT, HA, KA = self.dft_dims()
        PW = 3 if NKT % 3 == 0 else (5 if NKT % 5 == 0 else 1)
        with ExitStack() as es:
            E = es.enter_context
            R = [[E(nc.sbuf_tensor(f"fsR{v}{i}", [128, KA, 512], BF16)) for i in range(4)] for v in range(2)]
            R_t = [[[T() for _ in range(KA)] for i in range(4)] for v in range(2)]
            sk = E(nc.sbuf_tensor("fsSk", [128, 2, 2048], F32))
            sk_t = T()
            for o in range(2):
                S.dma("sp", sk[:, o, :], skipb[o, :].partition_broadcast(128), sk_t, writes=[sk_t])
            lp = [[E(nc.sbuf_tensor(f"fsL{i}{v}", [128, KA, PW * 128], BF16)) for v in range(2)] for i in range(4)]
            lp_t = [[T() for v in range(2)] for i in range(4)]
            ps = [[E(nc.psum_tensor(f"fsP{i}{v}", [128, 512], F32)) for v in range(2)] for i in range(4)]
            ps_t = [[T() for v in range(2)] for i in range(4)]
            ue = [E(nc.sbuf_tensor(f"fsUe{v}", [128, 512], F32)) for v in range(2)]
            be = [E(nc.sbuf_tensor(f"fsBe{v}", [128, 512], F32)) for v in range(2)]
            ue_t = [T() for _ in range(2)]
            be_t = [T() for _ in range(2)]
            fo = [E(nc.sbuf_tensor(f"fsFo{v}", [128, 4, 512], BF16)) for v in range(2)]
            fo_t = [T() for _ in range(2)]
            npan = NKT // PW
            seq = [(o, cq) for o in range(2) for cq in range(4)]
            def loadR(si):
                o, cq = seq[si]
                v = si % 2
                for i in range(4):
                    src = hsD if i < 2 else hdD
                    for j in range(KA):
                        S.dma("sp", R[v][i][:, j, :], self.eo_rows(src[o], j, i % 2)[:, cq * 512:(cq + 1) * 512], R_t[v][i][j], writes=[R_t[v][i][j]])
            pcount = [0]
            def loadL(pi):
                v = pcount[0] % 2
                for i in range(4):
                    S.dma("sp", lp[i][v][:], pk(Fw[i], 0, KA, pi * PW * 128, PW * 128), lp_t[i][v], writes=[lp_t[i][v]])
                pcount[0] += 1
                return v
            loadR(0)
            nxt = loadL(0)
            it = 0
            for si, (o, cq) in enumerate(seq):
                if si + 1 < len(seq):
                    loadR(si + 1)
                rv = si % 2
                for pi in range(npan):
                    b = nxt
                    if pi + 1 < npan:
                        nxt = loadL(pi + 1)
                    elif si + 1 < len(seq):
                        nxt = loadL(0)
                    for mi in range(PW):
                        mt = pi * PW + mi
                        pb = it % 2
                        for i in range(4):
                            for kt in range(KA):
                                S.op("pe", lambda e, kt=kt, i=i, mi=mi, b=b, pb=pb, rv=rv: e.matmul(
                                    ps[i][pb][:], lhsT=lp[i][b][:, kt, mi * 128:(mi + 1) * 128], rhs=R[rv][i][:, kt, :],
                                    start=(kt == 0), stop=(kt == KA - 1)),
                                    reads=[lp_t[i][b], R_t[rv][i][kt]], writes=[ps_t[i][pb]], sig=(kt == KA - 1))
                        S.op("dve", lambda e, pb=pb, o=o, cq=cq: e.tensor_tensor(out=ue[pb][:], in0=ps[0][pb][:], in1=sk[:, o, cq * 512:(cq + 1) * 512], op=ALU.add),
                             reads=[ps_t[0][pb], sk_t], writes=[ue_t[pb]])
                        S.op("act", lambda e, pb=pb: e.activation(out=be[pb][:], in_=ps[2][pb][:], func=AF.Copy), reads=[ps_t[2][pb]], writes=[be_t[pb]])
                        for (oi, src_, src_t, pi_, op) in ((0, ue, ue_t, 1, ALU.add), (2, ue, ue_t, 1, ALU.subtract), (1, be, be_t, 3, ALU.add), (3, be, be_t, 3, ALU.subtract)):
                            S.op("dve", lambda e, oi=oi, src_=src_, pi_=pi_, op=op, pb=pb: e.tensor_tensor(out=fo[pb][:, oi, :], in0=src_[pb][:], in1=ps[pi_][pb][:], op=op),
                                 reads=[src_t[pb], ps_t[pi_][pb]], writes=[fo_t[pb]])
                        S.dma("sp", F2D[o, mt * 128:(mt + 1) * 128, :, cq * 512:(cq + 1) * 512], fo[pb][:], fo_t[pb], reads=[fo_t[pb]],
                              writes=[self.tok(("F2", o, mt, cq))])
                        it += 1
            S.barrier()

    def phase_conv_fwd(self, o, ztm, ztok, Fw, F2D, Y2D):
        nc, S, L, NT = self.nc, self.S, self.L, self.NT
        NK, NKT, HA, KA = self.dft_dims()
        PW = 3 if NKT % 3 == 0 else (5 if NKT % 5 == 0 else 1)
        with ExitStack() as es:
            E = es.enter_context
            R = [E(nc.sbuf_tensor(f"cfR{o}{q}", [128, KA, 2048], BF16)) for q in range(2)]
            R_t = [[T() for _ in range(KA)] for q in range(2)]
            for j in range(KA):
                for q in range(2):
                    S.dma("sp", R[q][:, j, :], self.eo_rows(ztm, j, q), R_t[q][j], reads=ztok(j), writes=[R_t[q][j]])
            lp = [[E(nc.sbuf_tensor(f"cfL{o}{i}{b}", [128, KA, PW * 128], BF16)) for b in range(2)] for i in range(4)]
            lp_t = [[T() for b in range(2)] for i in range(4)]
            ps = [[E(nc.psum_tensor(f"cfP{o}{i}{b}", [128, 512], F32)) for b in range(2)] for i in range(4)]
            ps_t = [[T() for b in range(2)] for i in range(4)]
            ft = [E(nc.sbuf_tensor(f"cfF{o}{b}", [128, 4, 512], BF16)) for b in range(2)]
            ft_t = [T() for _ in range(2)]
            names = ["ao", "bo", "pr", "qr", "pi", "qi", "m1", "m2", "m3", "m4", "m5", "m6", "m7", "m8", "d1", "d2", "e1", "e2"]
            tbs = [{n: E(nc.sbuf_tensor(f"cf_{n}{o}_{v}", [128, 512], F32)) for n in names} for v in range(2)]
            tts = [{n: T() for n in names} for v in range(2)]
            yo = [E(nc.sbuf_tensor(f"cfY{o}{b}", [128, 4, 512], BF16)) for b in range(2)]
            yo_t = [[T() for _ in range(4)] for b in range(2)]
            npan = NKT // PW
            def load(pi):
                for i in range(4):
                    S.dma("sp", lp[i][pi % 2][:], pk(Fw[i], 0, KA, pi * PW * 128, PW * 128), lp_t[i][pi % 2], writes=[lp_t[i][pi % 2]])
            load(0)
            pending = []
            def tt(op, out, a, b_, eng):
                S.op(eng, lambda e, tb=tb: e.tensor_tensor(out=tb[out][:], in0=tb[a][:], in1=tb[b_][:], op=op), reads=[tt_[a], tt_[b_]], writes=[tt_[out]])
            it = 0
            nit = NKT * 4
            def loadF(i):
                mt_, cq_ = i // 4, i % 4
                S.dma("sp", ft[i % 2][:], F2D[o, mt_ * 128:(mt_ + 1) * 128, :, cq_ * 512:(cq_ + 1) * 512], ft_t[i % 2], writes=[ft_t[i % 2]])
            loadF(0)
            for pi in range(npan):
                if pi + 1 < npan:
                    load(pi + 1)
                b = pi % 2
                for mi in range(PW):
                    mt = pi * PW + mi
                    for cq in range(4):
                        pb = it % 2
                        tb, tt_ = tbs[pb], tts[pb]
                        if it + 1 < nit:
                            loadF(it + 1)
                        for (pi_, mats) in ((0, ((0, 0), (1, 1))), (1, ((2, 0), (3, 1))), (2, ((0, 0),)), (3, ((2, 0),))):
                            nmm = len(mats) * KA
                            c_ = 0
                            for (i, q) in mats:
                                for kt in range(KA):
                                    S.op("pe", lambda e, kt=kt, i=i, q=q, mi=mi, b=b, pb=pb, cq=cq, pi_=pi_, c_=c_, nmm=nmm: e.matmul(
                                        ps[pi_][pb][:], lhsT=lp[i][b][:, kt, mi * 128:(mi + 1) * 128], rhs=R[q][:, kt, cq * 512:(cq + 1) * 512],
                                        start=(c_ == 0), stop=(c_ == nmm - 1)),
                                        reads=[lp_t[i][b], R_t[q][kt]], writes=[ps_t[pi_][pb]], sig=(c_ == nmm - 1))
                                    c_ += 1
                        S.op("act", lambda e, pb=pb, tb=tb: e.activation(out=tb["ao"][:], in_=ps[2][pb][:], func=AF.Copy, scale=2.0), reads=[ps_t[2][pb]], writes=[tt_["ao"]])
                        S.op("act", lambda e, pb=pb, tb=tb: e.activation(out=tb["bo"][:], in_=ps[3][pb][:], func=AF.Copy, scale=2.0), reads=[ps_t[3][pb]], writes=[tt_["bo"]])
                        S.op("dve", lambda e, pb=pb, tb=tb: e.tensor_tensor(out=tb["qr"][:], in0=tb["ao"][:], in1=ps[0][pb][:], op=ALU.subtract),
                             reads=[ps_t[0][pb], tt_["ao"]], writes=[tt_["qr"]])
                        S.op("dve", lambda e, pb=pb, tb=tb: e.tensor_tensor(out=tb["qi"][:], in0=tb["bo"][:], in1=ps[1][pb][:], op=ALU.subtract),
                             reads=[ps_t[1][pb], tt_["bo"]], writes=[tt_["qi"]])
                        for (out, pi_, fi) in (("m1", 0, 0), ("m2", 1, 1), ("m3", 0, 1), ("m4", 1, 0)):
                            S.op("dve", lambda e, out=out, pi_=pi_, fi=fi, pb=pb, tb=tb: e.tensor_tensor(out=tb[out][:], in0=ps[pi_][pb][:], in1=ft[pb][:, fi, :], op=ALU.mult),
                                 reads=[ps_t[pi_][pb], ft_t[pb]], writes=[tt_[out]])
                        for (out, a, fi) in (("m5", "qr", 2), ("m6", "qi", 3), ("m7", "qr", 3), ("m8", "qi", 2)):
                            S.op("dve", lambda e, out=out, a=a, fi=fi, pb=pb, tb=tb: e.tensor_tensor(out=tb[out][:], in0=tb[a][:], in1=ft[pb][:, fi, :], op=ALU.mult),
                                 reads=[tt_[a], ft_t[pb]], writes=[tt_[out]])
                        tt(ALU.subtract, "d1", "m1", "m2", "pool")
                        tt(ALU.add, "e1", "m3", "m4", "pool")
                        tt(ALU.subtract, "d2", "m5", "m6", "pool")
                        tt(ALU.add, "e2", "m7", "m8", "pool")
                        if pending:
                            pending.pop()()
                        def finish(specs, pb=pb, tb=tb, tt_=tt_, mt=mt, cq=cq):
                            for (oi, a, b_, op, eng) in specs:
                                S.op(eng, lambda e, oi=oi, a=a, b_=b_, op=op: e.tensor_tensor(out=yo[pb][:, oi, :], in0=tb[a][:], in1=tb[b_][:], op=op),
                                     reads=[tt_[a], tt_[b_]], writes=[yo_t[pb][oi]])
                                S.dma("sp", Y2D[oi, mt * 128:(mt + 1) * 128, cq * 512:(cq + 1) * 512], yo[pb][:, oi, :], yo_t[pb][oi], reads=[yo_t[pb][oi]],
                                      writes=[self.tok(("Y2", o, oi, mt, cq))])
                        finish(((0, "d1", "d2", ALU.add, "pool"), (2, "d1", "d2", ALU.subtract, "pool")))
                        pending.append(lambda f=finish: f(((1, "e1", "e2", ALU.add, "dve"), (3, "e1", "e2", ALU.subtract, "dve"))))
                        it += 1
            while pending:
                pending.pop()()
            S.barrier()

    def phase_conv_inv(self, o, Y2D, Iv, xgT, xg_row0, xg_tok, znT, zn_tok):
        nc, S, L, NT, NB = self.nc, self.S, self.L, self.NT, self.NB
        NK, NKT, HA, KA = self.dft_dims()
        KT = 4 * NKT
        NAB = HA // 512
        with ExitStack() as es:
            E = es.enter_context
            lp = [E(nc.sbuf_tensor(f"ciL{o}{i}", [128, KT, 512], BF16)) for i in range(2)]
            lp_t = [[T() for _ in range(4)] for i in range(2)]
            rp = [E(nc.sbuf_tensor(f"ciR{o}{i}", [128, KT, 512], BF16)) for i in range(2)]
            rp_t = [[T() for _ in range(4)] for i in range(2)]
            ps = [[E(nc.psum_tensor(f"ciP{o}{i}{q}", [128, 512], F32)) for q in range(2)] for i in range(4)]
            ps_t = [[T() for q in range(2)] for i in range(4)]
            xg = [E(nc.sbuf_tensor(f"ciX{o}{i}", [128, 1024], BF16)) for i in range(4)]
            xg_t = [T() for _ in range(4)]
            ob = [E(nc.sbuf_tensor(f"ciO{o}{i}", [128, 1024], BF16)) for i in range(4)]
            ob_t = [T() for _ in range(4)]
            rseq = [(cp_, ab) for cp_ in range(2) for ab in range(NAB)]
            def loadR(i):
                ab = rseq[i][1]
                rb = i % 2
                for part in range(4):
                    S.dma("sp", rp[rb][:, part * NKT:(part + 1) * NKT, :], pk(Iv[part], 0, NKT, ab * 512, 512), rp_t[rb][part], writes=[rp_t[rb][part]])
            it = 0
            ri = 0
            loadR(0)
            for cp_ in range(2):
                for j in range(2):
                    for part in range(4):
                        S.dma("sp", lp[j][:, part * NKT:(part + 1) * NKT, :], pk(Y2D[part], 0, NKT, (cp_ * 2 + j) * 512, 512), lp_t[j][part], writes=[lp_t[j][part]])
                for ab in range(NAB):
                    if ri + 1 < len(rseq):
                        loadR(ri + 1)
                    rb = ri % 2
                    ri += 1
                    for j in range(2):
                        for mi in range(4):
                            ct = (cp_ * 2 + j) * 4 + mi
                            pb = it % 4
                            S.dma("sp", xg[pb][:], xgT[xg_row0 + ct * 128:xg_row0 + (ct + 1) * 128, ab * 1024:(ab + 1) * 1024], xg_t[pb], reads=xg_tok(ct), writes=[xg_t[pb]])
                            for q in range(2):
                                for kk in range(2 * NKT):
                                    kt = q * 2 * NKT + kk
                                    S.op("pe", lambda e, kt=kt, kk=kk, mi=mi, j=j, rb=rb, pb=pb, q=q: e.matmul(
                                        ps[pb][q][:], lhsT=lp[j][:, kt, mi * 128:(mi + 1) * 128], rhs=rp[rb][:, kt, :],
                                        start=(kk == 0), stop=(kk == 2 * NKT - 1)),
                                        reads=[lp_t[j][kt // NKT], rp_t[rb][kt // NKT]], writes=[ps_t[pb][q]], sig=(kk == 2 * NKT - 1))
                            obv = ob[pb][:].rearrange("p (a two) -> p a two", two=2)
                            xgv = xg[pb][:].rearrange("p (a two) -> p a two", two=2)
                            for q in range(2):
                                S.op("dve", lambda e, pb=pb, q=q, obv=obv, xgv=xgv: e.tensor_tensor(out=obv[:, :, q], in0=ps[pb][q][:], in1=xgv[:, :, q], op=ALU.mult),
                                     reads=[ps_t[pb][q], xg_t[pb]], writes=[ob_t[pb]])
                            S.dma("sp", znT[ct * 128:(ct + 1) * 128, ab * 1024:(ab + 1) * 1024], ob[pb][:], ob_t[pb], reads=[ob_t[pb]], writes=zn_tok(ct, ab))
                            it += 1
            S.barrier()

    def phase_d(self, yT, yhT, gT, w_go, w_ho, mT):
        nc, S, L, NT, NB = self.nc, self.S, self.L, self.NT, self.NB
        HT = L // 2
        ytoks = [self.tok(("yT", rb)) for rb in range(NB)]
        yhtoks = [self.tok(("yhT", ct, ab)) for ct in range(16) for ab in range(max(1, L // 1024))]
        with ExitStack() as es:
            E = es.enter_context
            Rg = E(nc.sbuf_tensor("dRg", [128, 16, L], BF16))
            Rh = E(nc.sbuf_tensor("dRh", [128, 16, L], BF16))
            Rg_t = [[T() for _ in range(4)] for th in range(2)]
            Rh_t = [[T() for _ in range(4)] for th in range(2)]
            for th in range(2):
                for j in range(4):
                    S.dma("sp", Rg[:, j * 4:(j + 1) * 4, th * HT:(th + 1) * HT], pk(yT, j * 512, 4, th * HT, HT), Rg_t[th][j], reads=ytoks, writes=[Rg_t[th][j]])
                for j in range(4):
                    S.dma("sp", Rh[:, j * 4:(j + 1) * 4, th * HT:(th + 1) * HT], pk(yhT, j * 512, 4, th * HT, HT), Rh_t[th][j], reads=yhtoks, writes=[Rh_t[th][j]])
            wg = [E(nc.sbuf_tensor(f"dWg{i}", [128, 16, 256], BF16)) for i in range(2)]
            wh = [E(nc.sbuf_tensor(f"dWh{i}", [128, 16, 256], BF16)) for i in range(2)]
            wg_t = [T() for _ in range(2)]
            wh_t = [T() for _ in range(2)]
            pg = [E(nc.psum_tensor(f"dPg{i}", [128, HT], F32)) for i in range(2)]
            ph = [E(nc.psum_tensor(f"dPh{i}", [128, HT], F32)) for i in range(2)]
            pg_t = [T() for _ in range(2)]
            ph_t = [T() for _ in range(2)]
            g0 = [E(nc.sbuf_tensor(f"dG0{i}", [128, HT], BF16)) for i in range(2)]
            g1 = [E(nc.sbuf_tensor(f"dG1{i}", [128, HT], BF16)) for i in range(2)]
            g0_t = [T() for _ in range(2)]
            g1_t = [T() for _ in range(2)]
            ta = [E(nc.sbuf_tensor(f"dTa{i}", [128, HT], F32)) for i in range(2)]
            tb = [E(nc.sbuf_tensor(f"dTb{i}", [128, HT], F32)) for i in range(2)]
            ta_t = [T() for _ in range(2)]
            tb_t = [T() for _ in range(2)]
            ob = [E(nc.sbuf_tensor(f"dO{i}", [128, HT], BF16)) for i in range(2)]
            ob_t = [T() for _ in range(2)]
            def load(pi):
                S.dma("pool", wg[pi % 2][:], pk(w_go, 0, 16, pi * 256, 256), wg_t[pi % 2], writes=[wg_t[pi % 2]])
                S.dma("pool", wh[pi % 2][:], pk(w_ho, 0, 16, pi * 256, 256), wh_t[pi % 2], writes=[wh_t[pi % 2]])
                for kt in range(16):
                    S.op("act", lambda e, kt=kt, pi=pi: e.activation(out=wg[pi % 2][:, kt, :], in_=wg[pi % 2][:, kt, :], func=AF.Copy,
                                                                 scale=self.cp[:, COL_G + kt:COL_G + kt + 1]),
                         reads=[wg_t[pi % 2], self.cp_t], writes=[wg_t[pi % 2]])
            load(0)
            it = 0
            for pi in range(8):
                if pi + 1 < 8:
                    load(pi + 1)
                b = pi % 2
                for mi in range(2):
                    mt = pi * 2 + mi
                    for th in range(2):
                        pb = it % 2
                        S.dma("sp", g0[pb][:], gT[mt * 128:(mt + 1) * 128, th * HT:(th + 1) * HT], g0_t[pb], reads=[self.tok(("gT", mt))], writes=[g0_t[pb]])
                        S.dma("sp", g1[pb][:], gT[2048 + mt * 128:2048 + (mt + 1) * 128, th * HT:(th + 1) * HT], g1_t[pb], reads=[self.tok(("gT", 16 + mt))], writes=[g1_t[pb]])
                        for (pp, pp_t, ww, ww_t, RR, RR_t) in ((pg, pg_t, wg, wg_t, Rg, Rg_t), (ph, ph_t, wh, wh_t, Rh, Rh_t)):
                            for nb in range(HT // 512):
                                for kt in range(16):
                                    S.op("pe", lambda e, kt=kt, nb=nb, mi=mi, b=b, pb=pb, pp=pp, ww=ww, RR=RR, th=th: e.matmul(
                                        pp[pb][:, nb * 512:(nb + 1) * 512], lhsT=ww[b][:, kt, mi * 128:(mi + 1) * 128],
                                        rhs=RR[:, kt, th * HT + nb * 512:th * HT + (nb + 1) * 512],
                                        start=(kt == 0), stop=(kt == 15)),
                                        reads=[ww_t[b], RR_t[th][kt // 4]], writes=[pp_t[pb]], sig=(kt == 15 and nb == HT // 512 - 1))
                        S.op("dve", lambda e, pb=pb: e.tensor_tensor(out=ta[pb][:], in0=pg[pb][:], in1=g0[pb][:], op=ALU.mult),
                             reads=[pg_t[pb], g0_t[pb]], writes=[ta_t[pb]])
                        S.op("dve", lambda e, pb=pb: e.tensor_tensor(out=tb[pb][:], in0=ph[pb][:], in1=g1[pb][:], op=ALU.mult),
                             reads=[ph_t[pb], g1_t[pb]], writes=[tb_t[pb]])
                        S.op("dve", lambda e, pb=pb: e.tensor_tensor(out=ob[pb][:], in0=ta[pb][:], in1=tb[pb][:], op=ALU.add),
                             reads=[ta_t[pb], tb_t[pb]], writes=[ob_t[pb]])
                        S.dma("sp", mT[mt * 128:(mt + 1) * 128, th * HT:(th + 1) * HT], ob[pb][:], ob_t[pb], reads=[ob_t[pb]], writes=[self.tok(("mT", mt, th))])
                        it += 1
            S.barrier()

    def ln_stage1(self, res, res_t, add_ap, add_toks, s32, s32_t, st, st_t):
        S = self.S
        S.op("dve", lambda e: e.scalar_tensor_tensor(out=s32[:], in0=res[:], scalar=ALPHA, in1=add_ap, op0=ALU.mult, op1=ALU.add),
             reads=[res_t] + add_toks, writes=[s32_t])
        for j in range(4):
            S.op("dve", lambda e, j=j: e.bn_stats(out=st[:, j * 6:(j + 1) * 6], in_=s32[:, j * 512:(j + 1) * 512]), reads=[s32_t], writes=[st_t])
        S.op("dve", lambda e: e.bn_aggr(out=st[:, 24:26], in_=st[:, 0:24]), reads=[st_t], writes=[st_t])
        S.op("dve", lambda e: e.tensor_scalar(out=st[:, 26:27], in0=st[:, 25:26], scalar1=LN_EPS, scalar2=None, op0=ALU.add), reads=[st_t], writes=[st_t])
        S.op("dve", lambda e: e.tensor_scalar(out=st[:, 28:29], in0=st[:, 24:25], scalar1=-1.0, scalar2=None, op0=ALU.mult), reads=[st_t], writes=[st_t])
        S.op("act", lambda e: e.activation(out=st[:, 26:27], in_=st[:, 26:27], func=AF.Ln), reads=[st_t], writes=[st_t])
        S.op("act", lambda e: e.activation(out=st[:, 26:27], in_=st[:, 26:27], func=AF.Exp, scale=-0.5), reads=[st_t], writes=[st_t])
        S.op("act", lambda e: e.activation(out=st[:, 27:28], in_=st[:, 28:29], func=AF.Identity, scale=st[:, 26:27]), reads=[st_t], writes=[st_t])
        S.op("act", lambda e: e.activation(out=s32[:], in_=s32[:], func=AF.Identity, scale=st[:, 26:27], bias=st[:, 27:28]), reads=[s32_t, st_t], writes=[s32_t])

    def ln_stage2(self, s32, s32_t, lg, lb, lgb_t):
        S = self.S
        S.op("dve", lambda e: e.tensor_tensor(out=s32[:], in0=s32[:], in1=lg[:], op=ALU.mult), reads=[s32_t, lgb_t], writes=[s32_t])
        S.op("dve", lambda e: e.tensor_tensor(out=s32[:], in0=s32[:], in1=lb[:], op=ALU.add), reads=[s32_t, lgb_t], writes=[s32_t])

    def phase_e(self, mT, w_out, x, lnp, h1D, h1b):
        nc, S, L, NT, NB = self.nc, self.S, self.L, self.NT, self.NB
        with ExitStack() as es:
            E = es.enter_context
            R = E(nc.sbuf_tensor("eR", [128, 16, 2048], BF16))
            R_t = [T() for _ in range(4)]
            for j in range(4):
                S.dma("pool", R[:, j * 4:(j + 1) * 4, :], pk(w_out, j * 512, 4, 0, 2048), R_t[j], writes=[R_t[j]])
            lg = E(nc.sbuf_tensor("eLg", [128, 2048], F32))
            lb = E(nc.sbuf_tensor("eLb", [128, 2048], F32))
            lgb_t = T()
            S.dma("sp", lg[:], lnp[0, :].partition_broadcast(128), lgb_t, writes=[lgb_t])
            S.dma("sp", lb[:], lnp[1, :].partition_broadcast(128), lgb_t, writes=[lgb_t])
            lp = [E(nc.sbuf_tensor(f"eL{i}", [128, 16, 512], BF16)) for i in range(2)]
            lp_t = [T() for _ in range(2)]
            ps = [E(nc.psum_tensor(f"eP{i}", [128, 2048], F32)) for i in range(2)]
            ps_t = [T() for _ in range(2)]
            xin = [E(nc.sbuf_tensor(f"eX{i}", [128, 2048], F32)) for i in range(2)]
            xin_t = [T() for _ in range(2)]
            s32 = [E(nc.sbuf_tensor(f"eS{i}", [128, 2048], F32)) for i in range(3)]
            s32_t = [T() for _ in range(3)]
            st = [E(nc.sbuf_tensor(f"eSt{i}", [128, 32], F32)) for i in range(3)]
            st_t = [T() for _ in range(3)]
            hb = [E(nc.sbuf_tensor(f"eHb{i}", [128, 2048], BF16)) for i in range(2)]
            hb_t = [T() for _ in range(2)]
            mtoks = lambda nb: [self.tok(("mT", mt, (nb * 512) // (L // 2))) for mt in range(16)]
            def load(nb):
                S.dma("sp", lp[nb % 2][:], pk(mT, 0, 16, nb * 512, 512), lp_t[nb % 2], reads=mtoks(nb), writes=[lp_t[nb % 2]])
            load(0)
            NS = 3
            def stage2(tt):
                sb_ = tt % NS
                self.ln_stage2(s32[sb_], s32_t[sb_], lg, lb, lgb_t)
                S.dma("sp", h1D[tt * 128:(tt + 1) * 128, :], s32[sb_][:], s32_t[sb_], reads=[s32_t[sb_]], writes=[self.tok(("h1D", tt))])
                S.op("act", lambda e, sb_=sb_: e.activation(out=hb[tt % 2][:], in_=s32[sb_][:], func=AF.Copy), reads=[s32_t[sb_]], writes=[hb_t[tt % 2]])
                S.dma("sp", h1b[tt * 128:(tt + 1) * 128, :], hb[tt % 2][:], hb_t[tt % 2], reads=[hb_t[tt % 2]], writes=[self.tok(("h1b", tt))])
            for nb in range(NB):
                if nb + 1 < NB:
                    load(nb + 1)
                for ti in range(4):
                    tt = nb * 4 + ti
                    pb = tt % 2
                    sb_ = tt % NS
                    if tt == 0:
                        S.dma("sp", xin[0][:], x[0:128, :], xin_t[0], writes=[xin_t[0]])
                    if tt + 1 < NT:
                        S.dma("sp", xin[1 - pb][:], x[(tt + 1) * 128:(tt + 2) * 128, :], xin_t[1 - pb], writes=[xin_t[1 - pb]])
                    for db in range(4):
                        for kt in range(16):
                            S.op("pe", lambda e, kt=kt, db=db, ti=ti, nb=nb, pb=pb: e.matmul(
                                ps[pb][:, db * 512:(db + 1) * 512], lhsT=lp[nb % 2][:, kt, ti * 128:(ti + 1) * 128], rhs=R[:, kt, db * 512:(db + 1) * 512],
                                start=(kt == 0), stop=(kt == 15)),
                                reads=[lp_t[nb % 2], R_t[kt // 4]], writes=[ps_t[pb]], sig=(kt == 15 and db == 3))
                    self.ln_stage1(xin[pb], xin_t[pb], ps[pb][:], [ps_t[pb]], s32[sb_], s32_t[sb_], st[sb_], st_t[sb_])
                    if tt >= 1:
                        stage2(tt - 1)
            stage2(NT - 1)
            S.barrier()

    def phase_f(self, h1T, w_ff1, uT):
        nc, S, L, NT, NB = self.nc, self.S, self.L, self.NT, self.NB
        with ExitStack() as es:
            E = es.enter_context
            R = E(nc.sbuf_tensor("fR", [128, 16, L], BF16))
            R_t = [T() for _ in range(4)]
            for j in range(4):
                S.dma("sp", R[:, j * 4:(j + 1) * 4, :], pk(h1T, j * 512, 4, 0, L), R_t[j], reads=[self.tok(("h1T", rb)) for rb in range(NB)], writes=[R_t[j]])
            wp = [E(nc.sbuf_tensor(f"fW{i}", [128, 16, 256], BF16)) for i in range(3)]
            wp_t = [T() for _ in range(3)]
            ps = [E(nc.psum_tensor(f"fP{i}", [128, L], F32)) for i in range(2)]
            ps_t = [T() for _ in range(2)]
            rl = [E(nc.sbuf_tensor(f"fRl{i}", [128, L], F32)) for i in range(2)]
            rl_t = [T() for _ in range(2)]
            ob = [E(nc.sbuf_tensor(f"fO{i}", [128, L], BF16)) for i in range(2)]
            ob_t = [T() for _ in range(2)]
            npan = DFF // 256
            def load(pi):
                S.dma("pool", wp[pi % 3][:], pk(w_ff1, 0, 16, pi * 256, 256), wp_t[pi % 3], writes=[wp_t[pi % 3]])
            load(0)
            load(1)
            for pi in range(npan):
                if pi + 2 < npan:
                    load(pi + 2)
                b = pi % 3
                for mi in range(2):
                    mt = pi * 2 + mi
                    pb = mt % 2
                    for nb in range(NB):
                        for kt in range(16):
                            S.op("pe", lambda e, kt=kt, nb=nb, mi=mi, b=b, pb=pb: e.matmul(
                                ps[pb][:, nb * 512:(nb + 1) * 512], lhsT=wp[b][:, kt, mi * 128:(mi + 1) * 128], rhs=R[:, kt, nb * 512:(nb + 1) * 512],
                                start=(kt == 0), stop=(kt == 15)),
                                reads=[wp_t[b], R_t[kt // 4]], writes=[ps_t[pb]], sig=(kt == 15 and nb == NB - 1))
                    S.op("act", lambda e, pb=pb: e.activation(out=rl[pb][:], in_=ps[pb][:], func=AF.Relu), reads=[ps_t[pb]], writes=[rl_t[pb]])
                    S.op("dve", lambda e, pb=pb: e.tensor_tensor(out=ob[pb][:], in0=rl[pb][:], in1=rl[pb][:], op=ALU.mult), reads=[rl_t[pb]], writes=[ob_t[pb]])
                    S.dma("sp", uT[mt * 128:(mt + 1) * 128, :], ob[pb][:], ob_t[pb], reads=[ob_t[pb]], writes=[self.tok(("uT", mt))])
            S.barrier()

    def phase_g(self, uT, w_ff2, ffT):
        nc, S, L, NT, NB = self.nc, self.S, self.L, self.NT, self.NB
        HT = L // 2
        NBH = HT // 512
        with ExitStack() as es:
            E = es.enter_context
            lp = [E(nc.sbuf_tensor(f"gL{i}", [128, 16, 512], BF16)) for i in range(4)]
            lp_t = [T() for _ in range(4)]
            rp = [E(nc.sbuf_tensor(f"gR{i}", [128, 16, 512], BF16)) for i in range(3)]
            rp_t = [T() for _ in range(3)]
            ps = [E(nc.psum_tensor(f"gP{i}", [128, HT], F32)) for i in range(4)]
            ps_t = [T() for _ in range(4)]
            ob = [E(nc.sbuf_tensor(f"gO{i}", [128, HT], F32)) for i in range(4)]
            ob_t = [T() for _ in range(4)]
            utoks = lambda kc: [self.tok(("uT", kc * 16 + j)) for j in range(16)]
            def loadL(mg, kc):
                S.dma("pool", lp[kc][:], pk(w_ff2, kc * 2048, 16, mg * 512, 512), lp_t[kc], writes=[lp_t[kc]])
            rseq = [(mg, th, kc, nb) for mg in range(4) for th in range(2) for kc in range(4) for nb in range(NBH)]
            def loadR(i):
                mg, th, kc, nb = rseq[i]
                S.dma("sp", rp[i % 3][:], pk(uT, kc * 2048, 16, th * HT + nb * 512, 512), rp_t[i % 3], reads=utoks(kc), writes=[rp_t[i % 3]])
            for kc in range(4):
                loadL(0, kc)
            loadR(0)
            loadR(1)
            ri = 0
            for mg in range(4):
                for th in range(2):
                    for kc in range(4):
                        for nb in range(NBH):
                            if ri + 2 < len(rseq):
                                loadR(ri + 2)
                            rb = ri % 3
                            for mi in range(4):
                                for kt in range(16):
                                    lastk = (kc == 3 and kt == 15)
                                    S.op("pe", lambda e, kt=kt, nb=nb, mi=mi, kc=kc, rb=rb, lastk=lastk: e.matmul(
                                        ps[mi][:, nb * 512:(nb + 1) * 512], lhsT=lp[kc][:, kt, mi * 128:(mi + 1) * 128], rhs=rp[rb][:, kt, :],
                                        start=(kc == 0 and kt == 0), stop=lastk),
                                        reads=[lp_t[kc], rp_t[rb]], writes=[ps_t[mi]], sig=(kt == 15))
                            ri += 1
                        if th == 1 and mg + 1 < 4:
                            loadL(mg + 1, kc)
                    for mi in range(4):
                        mt = mg * 4 + mi
                        if mi % 2 == 0:
                            S.op("act", lambda e, mi=mi: e.activation(out=ob[mi][:], in_=ps[mi][:], func=AF.Copy), reads=[ps_t[mi]], writes=[ob_t[mi]])
                        else:
                            S.op("dve", lambda e, mi=mi: e.tensor_copy(out=ob[mi][:], in_=ps[mi][:]), reads=[ps_t[mi]], writes=[ob_t[mi]])
                        S.dma("sp", ffT[mt * 128:(mt + 1) * 128, th * HT:(th + 1) * HT], ob[mi][:], ob_t[mi], reads=[ob_t[mi]], writes=[self.tok(("ffT", mt, th))])
            S.barrier()

    def phase_h(self, ffT, h1D, lnp2, out, idf, idf_t):
        nc, S, L, NT, NB = self.nc, self.S, self.L, self.NT, self.NB
        with ExitStack() as es:
            E = es.enter_context
            lg = E(nc.sbuf_tensor("hLg", [128, 2048], F32))
            lb = E(nc.sbuf_tensor("hLb", [128, 2048], F32))
            lgb_t = T()
            S.dma("sp", lg[:], lnp2[0, :].partition_broadcast(128), lgb_t, writes=[lgb_t])
            S.dma("sp", lb[:], lnp2[1, :].partition_broadcast(128), lgb_t, writes=[lgb_t])
            hin = [E(nc.sbuf_tensor(f"hH{i}", [128, 2048], F32)) for i in range(2)]
            hin_t = [T() for _ in range(2)]
            fin = [E(nc.sbuf_tensor(f"hF{i}", [128, 16, 512], F32)) for i in range(2)]
            fin_t = [T() for _ in range(2)]
            ps = [E(nc.psum_tensor(f"hP{i}", [128, 2048], F32)) for i in range(2)]
            ps_t = [T() for _ in range(2)]
            NS = 3
            s32 = [E(nc.sbuf_tensor(f"hS{i}", [128, 2048], F32)) for i in range(NS)]
            s32_t = [T() for _ in range(NS)]
            st = [E(nc.sbuf_tensor(f"hSt{i}", [128, 32], F32)) for i in range(NS)]
            st_t = [T() for _ in range(NS)]
            def stage2(tt):
                sb_ = tt % NS
                self.ln_stage2(s32[sb_], s32_t[sb_], lg, lb, lgb_t)
                S.dma("sp", out[tt * 128:(tt + 1) * 128, :], s32[sb_][:], s32_t[sb_], reads=[s32_t[sb_]], writes=[self.tok(("out", tt))])
            for tt in range(NT):
                pb = tt % 2
                sb_ = tt % NS
                th = (tt * 128) // (L // 2)
                if tt == 0:
                    S.dma("sp", hin[0][:], h1D[0:128, :], hin_t[0], reads=[self.tok(("h1D", 0))], writes=[hin_t[0]])
                if tt + 1 < NT:
                    S.dma("sp", hin[1 - pb][:], h1D[(tt + 1) * 128:(tt + 2) * 128, :], hin_t[1 - pb], reads=[self.tok(("h1D", tt + 1))], writes=[hin_t[1 - pb]])
                fb = (tt // 4) % 2
                ti = tt % 4
                if tt == 0:
                    S.dma("sp", fin[0][:], ffT[:, 0:512].rearrange("(k p) t -> p k t", p=128), fin_t[0],
                          reads=[self.tok(("ffT", mt, 0)) for mt in range(16)], writes=[fin_t[0]])
                if ti == 0 and tt + 4 < NT:
                    th2 = ((tt + 4) * 128) // (L // 2)
                    S.dma("sp", fin[1 - fb][:], ffT[:, (tt + 4) * 128:(tt + 4) * 128 + 512].rearrange("(k p) t -> p k t", p=128), fin_t[1 - fb],
                          reads=[self.tok(("ffT", mt, th2)) for mt in range(16)], writes=[fin_t[1 - fb]])
                for dt in range(16):
                    S.op("pe", lambda e, dt=dt, pb=pb, fb=fb, ti=ti: e.transpose(ps[pb][:, dt * 128:(dt + 1) * 128], fin[fb][:, dt, ti * 128:(ti + 1) * 128], idf[:]),
                         reads=[fin_t[fb], idf_t], writes=[ps_t[pb]], sig=(dt == 15))
                self.ln_stage1(hin[pb], hin_t[pb], ps[pb][:], [ps_t[pb]], s32[sb_], s32_t[sb_], st[sb_], st_t[sb_])
                if tt >= 1:
                    stage2(tt - 1)
            stage2(NT - 1)
            S.barrier()


COL_CONV = 0
COL_BA = 192
COL_G = 208
NCOL = 224


def host_layout(inputs, L):
    w_in = np.asarray(inputs["w_in"][0], dtype=np.float32)
    afab = np.zeros((D, 128), np.float32)
    afab[:, 0:16] = w_in[:, 6144:6160]
    afab[:, 32:48] = w_in[:, 6160:6176]
    w_fm = np.ascontiguousarray(np.concatenate([w_in[:, 0:2048], w_in[:, 6176:16416], afab], axis=1))
    w_tm = np.ascontiguousarray(w_in[:, 2048:6144])
    colp = np.zeros((128, NCOL), np.float32)
    cw = np.asarray(inputs["hy_conv_w"][0], np.float32)
    cb = np.asarray(inputs["hy_conv_b"][0], np.float32)
    for ct in range(48):
        for j in range(3):
            colp[:, COL_CONV + ct * 4 + j] = cw[j, ct * 128:(ct + 1) * 128]
        colp[:, COL_CONV + ct * 4 + 3] = cb[ct * 128:(ct + 1) * 128]
    baf = np.asarray(inputs["gla_ba_f"][0], np.float32)
    bab = np.asarray(inputs["gla_ba_b"][0], np.float32)
    for dt in range(8):
        colp[:, COL_BA + dt] = baf[dt * 128:(dt + 1) * 128]
        colp[:, COL_BA + 8 + dt] = bab[dt * 128:(dt + 1) * 128]
    gvec = np.asarray(inputs["gla_norm_g"], np.float32).reshape(2048)
    for et in range(16):
        colp[:, COL_G + et] = gvec[et * 128:(et + 1) * 128]
    wa2p = np.zeros((2, 64, 1024), np.float32)
    wa2p[0, 0:16] = np.asarray(inputs["gla_wa2_f"][0], np.float32)
    wa2p[1, 32:48] = np.asarray(inputs["gla_wa2_b"][0], np.float32)
    gng = np.asarray(inputs["gla_norm_g"], np.float32).reshape(1, 2048)
    jj = np.arange(128)[:, None]
    ii = np.arange(128)[None, :]
    mf = np.where(ii >= jj, 1.0, 0.0).astype(np.float32)
    mb = np.where(jj > ii, 1.0, 0.0).astype(np.float32)
    masks = np.stack([np.tile(mf, (1, 4)), np.tile(mb, (1, 4))], axis=1)
    shared = dict(w_fm=w_fm, w_tm=w_tm, ident=np.eye(128, dtype=np.float32), colp=colp,
                  wa2p=wa2p, gng=gng, masks=np.ascontiguousarray(masks))
    NT = L // 128
    mlpw = np.zeros((64, 3, 64), np.float32)
    mlpw[0:EMB, 0, :] = np.asarray(inputs["hy_w1"][0], np.float32)
    mlpw[:, 1, :] = np.asarray(inputs["hy_w2"][0], np.float32)
    mlpw[:, 2, :] = np.asarray(inputs["hy_w3"][0], np.float32)
    mlpc = np.stack([np.asarray(inputs["hy_b1"][0], np.float32), np.asarray(inputs["hy_b2"][0], np.float32),
                     np.asarray(inputs["hy_b3"][0], np.float32), np.asarray(inputs["hy_freq"][0], np.float32)], axis=1)
    w4aug = np.concatenate([np.asarray(inputs["hy_w4"][0], np.float32), np.asarray(inputs["hy_b4"], np.float32).reshape(1, 8192)], axis=0)
    shared.update(mlpw=mlpw, mlpc=np.ascontiguousarray(mlpc), w4aug=np.ascontiguousarray(w4aug),
                  skipb=np.ascontiguousarray(np.asarray(inputs["hy_skip"][0], np.float32)))
    shared.update(const_tables(L))
    f32c = lambda a: np.ascontiguousarray(np.asarray(a, np.float32))
    shared.update(w_go=f32c(inputs["w_gla_o"][0]), w_ho=f32c(inputs["w_hy_o"][0]), w_out=f32c(inputs["w_out"][0]),
                  w_ff1=f32c(inputs["w_ff1"][0]), w_ff2=f32c(inputs["w_ff2"][0]),
                  lnp1=f32c(np.stack([inputs["ln1_g"][0], inputs["ln1_b"][0]])), lnp2=f32c(np.stack([inputs["ln2_g"][0], inputs["ln2_b"][0]])))
    return shared


_CONST = {}


def const_tables(L):
    if L in _CONST:
        return _CONST[L]
    NT = L // 128
    f32 = np.float32
    t = np.linspace(0.0, 1.0, L, dtype=f32)
    bands = (EMB - 1) // 2
    f = np.linspace(1e-4, bands - 1, bands, dtype=f32)
    wpos = (2.0 * math.pi * np.arange(L, dtype=f32) / L).astype(f32)
    ang = wpos[:, None] * f[None, :]
    emb = np.concatenate([t[:, None], np.cos(ang), -np.sin(ang)], axis=-1).astype(f32)
    embT = np.zeros((64, L), f32)
    embT[0:EMB] = emb.T
    min_decay = math.log(1e-2) / 1.5
    max_decay = math.log(1e-2) / 0.3
    deltas = np.abs(np.linspace(min_decay, max_decay, HYW, dtype=f32)).astype(f32).reshape(1, HYW)
    negt = np.ascontiguousarray((-t).reshape(NT, 128).T.astype(f32))
    NK = ((((2 * L - 1) + 3) // 4 + 1) + 127) // 128 * 128
    N = 4 * (NK - 1)
    HA = L // 2
    a = np.arange(HA, dtype=np.int64)[:, None]
    k = np.arange(NK, dtype=np.int64)[None, :]
    th = 2.0 * math.pi / N
    pe = ((2 * a * k) % N).astype(np.float64) * th
    po = (((2 * a + 1) * k) % N).astype(np.float64) * th
    Fw64 = np.stack([np.cos(pe), np.cos(po), -np.sin(pe), -np.sin(po)])
    sk = np.full((NK,), 2.0 / N)
    sk[0] = 1.0 / N
    sk[NK - 1] = 1.0 / N
    Iv64 = np.stack([Fw64[0].T, Fw64[2].T, Fw64[1].T, Fw64[3].T]) * sk[None, :, None]
    Fw = np.ascontiguousarray(Fw64.astype(ml_dtypes.bfloat16))
    Iv = np.ascontiguousarray(Iv64.astype(ml_dtypes.bfloat16))
    _CONST[L] = dict(embT=embT, deltas=deltas, negt=negt, Fw=Fw, Iv=Iv)
    return _CONST[L]


_CACHE = {}


def kernel(**inputs):
    L = inputs["x"].shape[1]
    B = inputs["x"].shape[0]
    shared = host_layout(inputs, L)
    if L not in _CACHE:
        p = Prog(L=L)
        p.build()
        _CACHE[L] = p
    p = _CACHE[L]
    in_maps = []
    for b in range(B):
        m = dict(shared)
        m["x"] = np.ascontiguousarray(np.asarray(inputs["x"][b], np.float32))
        in_maps.append(m)
    res = run_bass_kernel_spmd(p.nc, in_maps, core_ids=list(range(B)))
    return np.stack([r["out"] for r in res.results], axis=0).astype(np.float32)
```
